# Optimizing a Trainium2 kernel written in Bass

```python
import jax
import jax.numpy as jnp
from jax import lax
import numpy as np

D_MODEL = 1024
BATCH = 4
SEQ = 8192
DEPTH = 4

CTX_LEN = 256
GRID_W = 64
HEAD_DIM = 64
NA_HEADS = 8
NA_WIDTH = NA_HEADS * HEAD_DIM
NA_KH = 8
NA_KW = 16
LRU_WIDTH = D_MODEL // 2
LRU_BLOCKS = 8
LRU_BLOCK_DIM = LRU_WIDTH // LRU_BLOCKS
LRU_C = 8.0
CONV_W = 4
SWA_Q_HEADS = 8
SWA_KV_HEADS = 2
SWA_Q_WIDTH = SWA_Q_HEADS * HEAD_DIM
SWA_KV_WIDTH = SWA_KV_HEADS * HEAD_DIM
SWA_WINDOW = 128
SWA_BLOCK = 128
ROPE_BASE = 10000.0
D_FF = 4 * D_MODEL
NORM_EPS = 1e-6
MASK_VALUE = -1e30
IN_SIZES = (NA_WIDTH, NA_WIDTH, NA_WIDTH, LRU_WIDTH, LRU_WIDTH, SWA_Q_WIDTH, SWA_KV_WIDTH, SWA_KV_WIDTH, D_MODEL, D_MODEL, D_MODEL)
IN_SPLITS = tuple(sum(IN_SIZES[:i + 1]) for i in range(len(IN_SIZES) - 1))
IN_WIDTH = sum(IN_SIZES)
BRANCH_SIZES = (NA_WIDTH, LRU_WIDTH, SWA_Q_WIDTH)
BRANCH_SPLITS = (NA_WIDTH, NA_WIDTH + LRU_WIDTH)
MIX_WIDTH = sum(BRANCH_SIZES)

kernel_name = 'hybrid_natten_rglru_swa_prefix_dit'


def rms_norm(x, g):
    xf = x.astype(jnp.float32)
    y = xf * lax.rsqrt(jnp.mean(xf * xf, axis=-1, keepdims=True) + NORM_EPS)
    return (y * g.astype(jnp.float32)).astype(x.dtype)


def modulate(h, shift, scale):
    return h * (1 + scale) + shift


def heads(t, n):
    return t.reshape(*t.shape[:-1], n, HEAD_DIM)


def axial_rope(n_tokens):
    t = jnp.arange(n_tokens, dtype=jnp.int32)
    row = (t // GRID_W).astype(jnp.float32)
    col = (t % GRID_W).astype(jnp.float32)
    n_freq = HEAD_DIM // 4
    inv_freq = ROPE_BASE ** (-jnp.arange(n_freq, dtype=jnp.float32) / n_freq)
    ang = jnp.concatenate([row[:, None] * inv_freq, col[:, None] * inv_freq], axis=-1)
    return jnp.cos(ang), jnp.sin(ang)


def apply_rope(t, cos, sin):
    tf = t.astype(jnp.float32)
    half = HEAD_DIM // 2
    t1, t2 = tf[..., :half], tf[..., half:]
    cs, sn = cos[None, :, None, :], sin[None, :, None, :]
    return jnp.concatenate([t1 * cs - t2 * sn, t2 * cs + t1 * sn], axis=-1).astype(t.dtype)


def context_attention(q, k, v, sink):
    B, T, H, d = q.shape
    G = k.shape[2]
    R = H // G
    qg = q.reshape(B, T, G, R, d)
    s = jnp.einsum('btgrd,bsgd->bgrts', qg, k, preferred_element_type=jnp.float32) * (d ** -0.5)
    if sink is not None:
        sk = jnp.broadcast_to(sink.astype(jnp.float32).reshape(1, G, R, 1, 1), (B, G, R, T, 1))
        s = jnp.concatenate([s, sk], axis=-1)
    p = jax.nn.softmax(s, axis=-1)
    if sink is not None:
        p = p[..., :-1]
    o = jnp.einsum('bgrts,bsgd->btgrd', p.astype(v.dtype), v)
    return o.reshape(B, T, H, d)


def neighbourhood_attention(q, k, v, kc, vc, rpb):
    B, S, H, d = q.shape
    rows = S // GRID_W
    kh = min(NA_KH, rows)
    kw = NA_KW
    scale = d ** -0.5
    qg = q.reshape(B, rows, GRID_W, H, d)
    kg = k.reshape(B, rows, GRID_W, H, d)
    vg = v.reshape(B, rows, GRID_W, H, d)
    col = np.arange(GRID_W)
    col_start = np.clip(col - kw // 2, 0, GRID_W - kw)
    col_idx = col_start[:, None] + np.arange(kw)[None, :]
    dc = col_idx - col[:, None] + (NA_KW - 1)
    rpb_c = rpb[:, :, dc]

    def one_row(r):
        rs = jnp.clip(r - kh // 2, 0, rows - kh)
        kn = lax.dynamic_slice_in_dim(kg, rs, kh, axis=1)[:, :, col_idx]
        vn = lax.dynamic_slice_in_dim(vg, rs, kh, axis=1)[:, :, col_idx]
        qr = lax.dynamic_index_in_dim(qg, r, axis=1, keepdims=False)
        s_loc = jnp.einsum('bwhd,biwjhd->bhwij', qr, kn, preferred_element_type=jnp.float32) * scale
        dr = rs + jnp.arange(kh) - r + (NA_KH - 1)
        bias = jnp.take(rpb_c, dr, axis=1).transpose(0, 2, 1, 3)
        s_loc = s_loc + bias.astype(jnp.float32)[None]
        s_ctx = jnp.einsum('bwhd,bchd->bhwc', qr, kc, preferred_element_type=jnp.float32) * scale
        logits = jnp.concatenate([s_loc.reshape(B, H, GRID_W, kh * kw), s_ctx], axis=-1)
        p = jax.nn.softmax(logits, axis=-1)
        p_loc = p[..., :kh * kw].reshape(B, H, GRID_W, kh, kw).astype(v.dtype)
        p_ctx = p[..., kh * kw:].astype(v.dtype)
        return (jnp.einsum('bhwij,biwjhd->bwhd', p_loc, vn)
                + jnp.einsum('bhwc,bchd->bwhd', p_ctx, vc))

    out = lax.map(one_row, jnp.arange(rows))
    return out.transpose(1, 0, 2, 3, 4).reshape(B, S, H, d)


def sliding_window_attention(q, k, v, kc, vc, sink):
    B, S, H, d = q.shape
    G = k.shape[2]
    R = H // G
    blk = SWA_BLOCK
    nb = S // blk
    qb = q.reshape(B, nb, blk, G, R, d)

    def band(t):
        tp = jnp.pad(t, ((0, 0), (blk, blk), (0, 0), (0, 0))).reshape(B, nb + 2, blk, G, d)
        return jnp.concatenate([tp[:, :-2], tp[:, 1:-1], tp[:, 2:]], axis=2)

    kb, vb = band(k), band(v)
    qpos = np.arange(blk)[:, None]
    kpos = np.arange(3 * blk)[None, :] - blk
    kabs = np.arange(nb)[:, None] * blk + kpos
    valid = ((np.abs(kpos - qpos) <= SWA_WINDOW)[None]
             & ((kabs >= 0) & (kabs < S))[:, None, :])
    scale = d ** -0.5
    s_loc = jnp.einsum('bnqgrd,bnkgd->bngrqk', qb, kb, preferred_element_type=jnp.float32) * scale
    s_loc = jnp.where(valid[None, :, None, None], s_loc, MASK_VALUE)
    s_ctx = jnp.einsum('bnqgrd,bcgd->bngrqc', qb, kc, preferred_element_type=jnp.float32) * scale
    sk = jnp.broadcast_to(sink.astype(jnp.float32).reshape(1, 1, G, R, 1, 1), (B, nb, G, R, blk, 1))
    p = jax.nn.softmax(jnp.concatenate([s_loc, s_ctx, sk], axis=-1), axis=-1)
    n_loc = 3 * blk
    p_loc = p[..., :n_loc].astype(v.dtype)
    p_ctx = p[..., n_loc:-1].astype(v.dtype)
    o = (jnp.einsum('bngrqk,bnkgd->bnqgrd', p_loc, vb)
         + jnp.einsum('bngrqc,bcgd->bnqgrd', p_ctx, vc))
    return o.reshape(B, S, H, d)


def centred_dwconv(x, w, b):
    T = x.shape[1]
    left = CONV_W // 2
    right = CONV_W - 1 - left
    xp = jnp.pad(x, ((0, 0), (left, right), (0, 0)))
    y = xp[:, 0:T] * w[0]
    for i in range(1, CONV_W):
        y = y + xp[:, i:i + T] * w[i]
    return y + b


def block_diag_linear(x, w, b):
    xb = x.reshape(*x.shape[:-1], LRU_BLOCKS, LRU_BLOCK_DIM)
    y = jnp.einsum('...nc,ncd->...nd', xb, w)
    return y.reshape(x.shape) + b


def rglru_coeffs(u, w_a, b_a, w_x, b_x, lam):
    r = jax.nn.sigmoid(block_diag_linear(u, w_a, b_a).astype(jnp.float32))
    i = jax.nn.sigmoid(block_diag_linear(u, w_x, b_x).astype(jnp.float32))
    log_a = -LRU_C * r * jax.nn.softplus(-lam.astype(jnp.float32))
    a = jnp.exp(log_a)
    mult = jnp.sqrt(-jnp.expm1(2.0 * log_a))
    return a, mult * i * u.astype(jnp.float32)


def linear_scan(a, b, h0, reverse):
    if reverse:
        a, b = jnp.flip(a, axis=1), jnp.flip(b, axis=1)
    b = b.at[:, 0].add(a[:, 0] * h0)

    def combine(l, r):
        return l[0] * r[0], r[0] * l[1] + r[1]

    _, h = lax.associative_scan(combine, (a, b), axis=1)
    return jnp.flip(h, axis=1) if reverse else h


def rg_lru_bidirectional(u_lat, u_ctx, w_a, b_a, w_x, b_x, lam):
    B, _, C = u_lat.shape
    y_lat = jnp.zeros(u_lat.shape, jnp.float32)
    y_ctx = jnp.zeros(u_ctx.shape, jnp.float32)
    for d in range(2):
        rev = d == 1
        a_c, b_c = rglru_coeffs(u_ctx, w_a[d], b_a[d], w_x[d], b_x[d], lam[d])
        h_c = linear_scan(a_c, b_c, jnp.zeros((B, C), jnp.float32), rev)
        h_c_final = h_c[:, 0] if rev else h_c[:, -1]
        a_l, b_l = rglru_coeffs(u_lat, w_a[d], b_a[d], w_x[d], b_x[d], lam[d])
        h_l = linear_scan(a_l, b_l, h_c_final, rev)
        y_lat = y_lat + h_l
        y_ctx = y_ctx + h_c
    return y_lat, y_ctx


def merge_branches(ys, gates, w_branch, w_out):
    w_parts = jnp.split(w_branch, BRANCH_SPLITS, axis=0)
    m = jax.nn.sigmoid(gates[0]) * (ys[0] @ w_parts[0])
    for y, g, w in zip(ys[1:], gates[1:], w_parts[1:]):
        m = m + jax.nn.sigmoid(g) * (y @ w)
    return m @ w_out


def sq_relu_mlp(h, w1, w2):
    return jnp.square(jax.nn.relu(h @ w1)) @ w2


def setup_inputs(seed: int = 0) -> dict:
    key = jax.random.key(seed)
    ks = jax.random.split(key, 24)
    f32 = jnp.float32
    L = DEPTH

    def nrm(k, shape, s):
        return s * jax.random.normal(k, shape, f32)

    u = jax.random.uniform(ks[16], (L, 2, LRU_WIDTH), f32, 0.9, 0.999)
    a0 = u ** (1.0 / LRU_C)
    lru_lambda = jnp.log(a0) - jnp.log1p(-a0)
    return {
        'x': nrm(ks[0], (BATCH, SEQ, D_MODEL), 1.0),
        'c': nrm(ks[1], (BATCH, D_MODEL), 1.0),
        'ctx': nrm(ks[2], (BATCH, CTX_LEN, D_MODEL), 1.0),
        'c_ctx': nrm(ks[3], (D_MODEL,), 1.0),
        'w_mod': nrm(ks[4], (L, D_MODEL, 6 * D_MODEL), 0.5 * D_MODEL ** -0.5),
        'b_mod': nrm(ks[5], (L, 6 * D_MODEL), 0.02),
        'norm1_g': 1.0 + nrm(ks[6], (L, D_MODEL), 0.05),
        'norm2_g': 1.0 + nrm(ks[7], (L, D_MODEL), 0.05),
        'w_in': nrm(ks[8], (L, D_MODEL, IN_WIDTH), D_MODEL ** -0.5),
        'na_rpb': nrm(ks[9], (L, NA_HEADS, 2 * NA_KH - 1, 2 * NA_KW - 1), 0.1),
        'conv_w': nrm(ks[10], (L, CONV_W, LRU_WIDTH), CONV_W ** -0.5),
        'conv_b': nrm(ks[11], (L, LRU_WIDTH), 0.01),
        'lru_wa': nrm(ks[12], (L, 2, LRU_BLOCKS, LRU_BLOCK_DIM, LRU_BLOCK_DIM), LRU_BLOCK_DIM ** -0.5),
        'lru_ba': nrm(ks[13], (L, 2, LRU_WIDTH), 0.01),
        'lru_wx': nrm(ks[14], (L, 2, LRU_BLOCKS, LRU_BLOCK_DIM, LRU_BLOCK_DIM), LRU_BLOCK_DIM ** -0.5),
        'lru_bx': nrm(ks[15], (L, 2, LRU_WIDTH), 0.01),
        'lru_lambda': lru_lambda,
        'swa_sink': nrm(ks[17], (L, SWA_Q_HEADS), 0.5),
        'w_branch': nrm(ks[18], (L, MIX_WIDTH, D_MODEL), NA_WIDTH ** -0.5),
        'w_out': nrm(ks[19], (L, D_MODEL, D_MODEL), D_MODEL ** -0.5),
        'w_ff1': nrm(ks[20], (L, D_MODEL, D_FF), D_MODEL ** -0.5),
        'w_ff2': nrm(ks[21], (L, D_FF, D_MODEL), D_FF ** -0.5),
        'final_g': 1.0 + nrm(ks[22], (D_MODEL,), 0.05),
    }


def reference(x, c, ctx, c_ctx, w_mod, b_mod, norm1_g, norm2_g, w_in, na_rpb, conv_w, conv_b,
              lru_wa, lru_ba, lru_wx, lru_bx, lru_lambda, swa_sink, w_branch, w_out, w_ff1, w_ff2, final_g):
    dt = x.dtype
    B, S, _ = x.shape
    rope_cos, rope_sin = axial_rope(S)
    cond_lat = jax.nn.silu(c)[:, None, :]
    cond_ctx = jax.nn.silu(c_ctx)[None, None, :]
    xc = ctx
    for l in range(DEPTH):
        update_ctx = l < DEPTH - 1
        mod_l = jnp.split(cond_lat @ w_mod[l] + b_mod[l], 6, axis=-1)
        mod_c = jnp.split(cond_ctx @ w_mod[l] + b_mod[l], 6, axis=-1)

        h = modulate(rms_norm(x, norm1_g[l]), mod_l[0], mod_l[1])
        hc = modulate(rms_norm(xc, norm1_g[l]), mod_c[0], mod_c[1])
        qa, ka, va, xb, gb, qs, ks, vs, ga, gr, gs = jnp.split(h @ w_in[l], IN_SPLITS, axis=-1)
        qac, kac, vac, xbc, gbc, qsc, ksc, vsc, gac, grc, gsc = jnp.split(hc @ w_in[l], IN_SPLITS, axis=-1)

        kac_h, vac_h = heads(kac, NA_HEADS), heads(vac, NA_HEADS)
        y_a = neighbourhood_attention(heads(qa, NA_HEADS), heads(ka, NA_HEADS), heads(va, NA_HEADS),
                                      kac_h, vac_h, na_rpb[l]).reshape(B, S, NA_WIDTH)

        u_lat = centred_dwconv(xb, conv_w[l], conv_b[l])
        u_ctx = centred_dwconv(xbc, conv_w[l], conv_b[l])
        r_lat, r_ctx = rg_lru_bidirectional(u_lat, u_ctx, lru_wa[l], lru_ba[l], lru_wx[l], lru_bx[l], lru_lambda[l])
        y_b = r_lat.astype(dt) * jax.nn.gelu(gb)

        ksc_h, vsc_h = heads(ksc, SWA_KV_HEADS), heads(vsc, SWA_KV_HEADS)
        qs_r = apply_rope(heads(qs, SWA_Q_HEADS), rope_cos, rope_sin)
        ks_r = apply_rope(heads(ks, SWA_KV_HEADS), rope_cos, rope_sin)
        y_c = sliding_window_attention(qs_r, ks_r, heads(vs, SWA_KV_HEADS), ksc_h, vsc_h,
                                       swa_sink[l]).reshape(B, S, SWA_Q_WIDTH)

        x = x + mod_l[2] * merge_branches((y_a, y_b, y_c), (ga, gr, gs), w_branch[l], w_out[l])
        if update_ctx:
            y_ac = context_attention(heads(qac, NA_HEADS), kac_h, vac_h, None).reshape(B, -1, NA_WIDTH)
            y_bc = r_ctx.astype(dt) * jax.nn.gelu(gbc)
            y_cc = context_attention(heads(qsc, SWA_Q_HEADS), ksc_h, vsc_h, swa_sink[l]).reshape(B, -1, SWA_Q_WIDTH)
            xc = xc + mod_c[2] * merge_branches((y_ac, y_bc, y_cc), (gac, grc, gsc), w_branch[l], w_out[l])

        h2 = modulate(rms_norm(x, norm2_g[l]), mod_l[3], mod_l[4])
        x = x + mod_l[5] * sq_relu_mlp(h2, w_ff1[l], w_ff2[l])
        if update_ctx:
            h2c = modulate(rms_norm(xc, norm2_g[l]), mod_c[3], mod_c[4])
            xc = xc + mod_c[5] * sq_relu_mlp(h2c, w_ff1[l], w_ff2[l])
    return rms_norm(x, final_g)
```

```python
import numpy as np
from contextlib import ExitStack
import concourse.bass as bass
import concourse.mybir as mybir
from concourse.bass_utils import run_bass_kernel_spmd
from concourse.ap import AP

F32 = mybir.dt.float32
BF16 = mybir.dt.bfloat16
I32 = mybir.dt.int32
AF = mybir.ActivationFunctionType
ALU = mybir.AluOpType

D = 1024
CTX = 256
GW = 64
HD = 64
DFF = 4096
INW = 6400
MODW = 6144
MASKV = -30000.0
NPV = 112
PV_BMOD, PV_N1G, PV_N2G, PV_CW, PV_CB, PV_BA, PV_BX, PV_LAM, PV_SINK = 0, 48, 56, 64, 80, 84, 92, 100, 108
CM_ID, CM_PERM, CM_MPREV, CM_MNEXT, CM_ESEL, CM_SEL0, CM_SEL1 = 0, 128, 256, 512, 768, 896, 1152
NCM = 1408
OFF = dict(qa=0, ka=512, va=1024, xb=1536, gb=2048, qs=2560, ks=3072, vs=3200, ga=3328, gr=4352, gs=5376)


class Buf:
    __slots__ = ("w", "r", "excl")

    def __init__(self, excl=False):
        self.w = None
        self.r = []
        self.excl = excl


class FW:
    ENG = ("pe", "act", "dve", "pool", "sp")

    def __init__(self, nc, es, ndma=12):
        self.nc = nc
        self.e = {"pe": nc.tensor, "act": nc.scalar, "dve": nc.vector, "pool": nc.gpsimd, "sp": nc.sync}
        self.sem = {k: es.enter_context(nc.semaphore("s_" + k)) for k in self.ENG}
        self.cnt = {k: 0 for k in self.ENG}
        self.seen = {k: {} for k in self.ENG}
        self.pend = {k: [] for k in self.ENG}
        self.dq = {}
        for q in ("sp", "act", "pool"):
            self.dq[q] = {"sems": [es.enter_context(nc.semaphore("d_%s%d" % (q, i))) for i in range(ndma)],
                          "uses": [0] * ndma, "next": 0}
        self.nins = 0
        self.nwait = 0

    def _wait(self, eng, ev):
        key, val = ev
        if self.seen[eng].get(key, 0) >= val:
            return
        self.seen[eng][key] = val
        sem = self.sem[key] if isinstance(key, str) else self.dq[key[1]]["sems"][key[2]]
        self.e[eng].wait_ge(sem, val)
        self.nwait += 1

    def _deps(self, eng, reads, writes):
        for b in reads:
            if b.w is not None and (b.w[0] != eng or eng != "pe"):
                self._wait(eng, b.w)
            if b.excl:
                for ev in b.r:
                    if ev[0] != eng:
                        self._wait(eng, ev)
        for b in writes:
            if b.w is not None and b.w[0] != eng:
                self._wait(eng, b.w)
            for ev in b.r:
                if ev[0] != eng:
                    self._wait(eng, ev)

    def _reg(self, ev, reads, writes):
        for b in reads:
            if len(b.r) > 12:
                last = {}
                for k, v in b.r:
                    if last.get(k, 0) < v:
                        last[k] = v
                b.r = list(last.items())
            b.r.append(ev)
        for b in writes:
            b.w = ev
            b.r = []

    def op(self, eng, fn, reads=(), writes=(), sig=True):
        self._deps(eng, reads, writes)
        self.nins += 1
        ins = fn(self.e[eng])
        if sig:
            self.cnt[eng] += 1
            ev = (eng, self.cnt[eng])
            ins.then_inc(self.sem[eng], 1)
            for (r, w) in self.pend[eng]:
                self._reg(ev, r, w)
            self.pend[eng] = []
            self._reg(ev, reads, writes)
            return ev
        self.pend[eng].append((reads, writes))
        return None

    def dma(self, q, out, in_, reads=(), writes=()):
        d = self.dq[q]
        i = d["next"]
        d["next"] = (i + 1) % len(d["sems"])
        key = ("d", q, i)
        if d["uses"][i] > 0:
            self._wait(q, (key, 16 * d["uses"][i]))
        self._deps(q, reads, writes)
        d["uses"][i] += 1
        ev = (key, 16 * d["uses"][i])
        self.e[q].dma_start(out=out, in_=in_).then_inc(d["sems"][i], 16)
        self.nins += 1
        self._reg(ev, reads, writes)
        return ev

    def barrier(self):
        evs = [(k, self.cnt[k]) for k in self.ENG if self.cnt[k] > 0]
        for q, d in self.dq.items():
            for i, u in enumerate(d["uses"]):
                if u > 0:
                    evs.append((("d", q, i), 16 * u))
        for eng in self.ENG:
            for ev in evs:
                if ev[0] != eng:
                    self._wait(eng, ev)


class Prog:
    def __init__(self, S, depth, dbg=()):
        self.S = S
        self.T = S + CTX
        self.R = S // GW
        self.L = depth
        self.dbg = set(dbg)
        self.SL = S // 2
        self.TL = self.SL + 128
        self.chunks = [(c * 512, 512, 0) for c in range(self.SL // 512)] + [(self.SL, 128, 1)]
        self.nc = bass.Bass("TRN2", target_bir_lowering=False)
        self.par = {"sp": self.nc.sync.partition_id() % 2, "pool": self.nc.gpsimd.partition_id() % 2,
                    "act": self.nc.scalar.partition_id() % 2}

    def tmap(self, t0, n):
        S, SL = self.S, self.SL
        if t0 < S:
            assert (t0 % SL) + n <= SL
            return [(t0 // SL, t0 % SL, n, 0)]
        res, o = [], 0
        a, b = t0 - S, t0 - S + n
        for h in range(2):
            lo, hi = max(a, h * 128), min(b, (h + 1) * 128)
            if lo < hi:
                res.append((h, SL + lo - h * 128, hi - lo, lo - a))
        return res

    def ld_nat(self, q, tile_fn, loc, rows, writes):
        S, SL, TL = self.S, self.SL, self.TL
        for h in range(2):
            self.fw.dma(q, tile_fn(h * SL, SL), loc[rows, h, 0:SL], writes=writes)
            self.fw.dma(q, tile_fn(S + h * 128, 128), loc[rows, h, SL:TL], writes=writes)

    def din(self, name, shape, dt=F32):
        return self.nc.dram_tensor(name, list(shape), dt, kind="ExternalInput").ap()

    def dscr(self, name, shape, dt, shared=False):
        kind = "ExternalOutput" if name in self.dbg else "Internal"
        if shared and kind == "Internal":
            return self.nc.dram_tensor(name, list(shape), dt, kind=kind, addr_space="Shared").ap()
        return self.nc.dram_tensor(name, list(shape), dt, kind=kind).ap()

    def _uniq(self, name):
        self._nid = getattr(self, "_nid", 0) + 1
        return "%s_%d" % (name, self._nid)

    def sb(self, es, name, shape, dt):
        t = es.enter_context(self.nc.sbuf_tensor(self._uniq(name), list(shape), dt))
        return t

    def dump(self, name, ap_, reads):
        if "dump" not in self.dbg:
            return
        shp = [a[1] for a in ap_.ap]
        d = self.nc.dram_tensor("dbg_" + name, shp, ap_.dtype, kind="ExternalOutput").ap()
        self.fw.dma("sp", d, ap_, reads=reads)

    def ps(self, es, name, shape, dt=F32):
        return es.enter_context(self.nc.psum_tensor(self._uniq(name), list(shape), dt))

    def build(self):
        nc, S, T, L = self.nc, self.S, self.T, self.L
        i = self.i = {}
        TL, SL = self.TL, self.SL
        i["xin"] = self.din("xin", [TL, D])
        i["ep"] = self.din("ep", [1, 16], I32)
        i["cvec"] = self.din("cvec", [128, 16])
        i["w_mod"] = self.din("w_mod", [L, D, MODW])
        i["w_in"] = self.din("w_in", [L, D, INW])
        i["w_branch"] = self.din("w_branch", [L, 1536, D])
        i["w_out"] = self.din("w_out", [L, D, D])
        i["w_ff1"] = self.din("w_ff1", [L, D, DFF])
        i["w_ff2"] = self.din("w_ff2", [L, DFF, D])
        i["pvec"] = self.din("pvec", [128, L * NPV])
        i["fgv"] = self.din("fgv", [128, 8])
        i["lruw"] = self.din("lruw", [L, 2, 2, 4, 128, 128])
        i["nabias"] = self.din("nabias", [L, 5, 4, 128, 1280])
        i["ropec"] = self.din("ropec", [128, SL])
        i["ropes"] = self.din("ropes", [128, SL])
        i["cmat"] = self.din("cmat", [128, NCM])
        i["fgrow"] = self.din("fgrow", [D])
        self.out = nc.dram_tensor("out", [SL, D], F32, kind="ExternalOutput").ap()
        s = self.s = {}
        m = self.m = {}
        sh = self.sh = {}
        s["XA"] = self.dscr("XA", [TL, D], F32)
        s["XB"] = self.dscr("XB", [TL, D], F32)
        FM_l = self.dscr("FM_l", [2, 1088, TL], BF16)
        FM_m = self.dscr("FM_m", [1088, 2, TL], BF16)
        sh["FM"] = self.dscr("FM_s", [2, 1088, TL], BF16, shared=True)
        r0 = 0
        for n, f in (("QA", 256), ("KA", 256), ("GB", 256), ("QS", 256), ("KS", 64)):
            s[n] = FM_l[:, r0:r0 + f, :]
            m[n] = FM_m[r0:r0 + f]
            r0 += f
        s["XBC"] = self.dscr("XBC", [2, 256, TL], F32)
        m["XBC"] = self.dscr("XBC_m", [256, 2, TL], F32)
        sh["XBC"] = self.dscr("XBC_s", [2, 256, TL], F32, shared=True)
        V_l = self.dscr("V_l", [TL, 2, 320], BF16)
        V_m = self.dscr("V_m", [2, TL, 320], BF16)
        sh["V"] = self.dscr("V_s", [2, TL, 320], BF16, shared=True)
        s["VA"], s["VS"] = V_l[:, :, 0:256], V_l[:, :, 256:320]
        m["VA"], m["VS"] = V_m[:, :, 0:256], V_m[:, :, 256:320]
        Y_m = self.dscr("Y_m", [768, 2, TL], BF16)
        Y_t = self.dscr("Y_t", [2, 768, TL], BF16)
        sh["Y"] = self.dscr("Y_s", [2, 768, 2, TL], BF16, shared=True)
        for k_, n in enumerate(("YA", "YB", "YC")):
            m[n] = Y_m[k_ * 256:(k_ + 1) * 256]
            s[n] = Y_t[:, k_ * 256:(k_ + 1) * 256, :]
        self.bund = dict(FM_l=FM_l, FM_m=FM_m, V_l=V_l, V_m=V_m, Y_m=Y_m, Y_t=Y_t)
        s["GT"] = self.dscr("GT", [3072, TL], BF16)
        s["AT"] = self.dscr("AT", [DFF, TL], BF16)
        s["MODROW"] = self.dscr("MODROW", [L, 96, 128], F32)
        s["FLAG"] = nc.dram_tensor("FLAG", [2, 16], I32, kind="Internal", addr_space="Shared").ap()
        with ExitStack() as es:
            self.fw = FW(nc, es)
            self.ft = self.sb(es, "ft", [1, 32], I32)
            self.fsem = es.enter_context(nc.semaphore("fsem"))
            self.psem = es.enter_context(nc.semaphore("psem"))
            nc.sync.dma_start(out=self.ft[0:1, 0:16], in_=i["ep"]).then_inc(self.fsem, 16)
            nc.sync.wait_ge(self.fsem, 16)
            self.fn = 16
            self.pv = self.sb(es, "pv", [128, L * NPV], F32)
            self.modp = self.sb(es, "modp", [128, L, 4, 2, 8], F32)
            self.cm = self.sb(es, "cm", [128, NCM], F32)
            self.cmb = self.sb(es, "cmb", [128, NCM], BF16)
            self.b_pv, self.b_modp, self.b_cm, self.b_cmb = Buf(), Buf(), Buf(), Buf()
            fw = self.fw
            fw.dma("sp", self.pv[:], i["pvec"], writes=[self.b_pv])
            fw.dma("sp", self.cm[:], i["cmat"], writes=[self.b_cm])
            fw.op("dve", lambda e: e.tensor_copy(out=self.cmb[:], in_=self.cm[:]), reads=[self.b_cm], writes=[self.b_cmb])
            self.phase0()
            fw.barrier()
            for l in range(L):
                self.phase1(l)
                self.exchange1(l)
                if "stop1" in self.dbg:
                    break
                self.phase_na(l)
                fw.barrier()
                g1, g2 = self.phase_lru(l), self.phase_swa(l)
                live = [g1, g2]
                while live:
                    for g_, reps in ((g1, 1), (g2, 2)):
                        for _ in range(reps):
                            if g_ in live:
                                try:
                                    next(g_)
                                except StopIteration:
                                    live.remove(g_)
                self.exchange2(l)
                if "stop2c" in self.dbg:
                    break
                self.phase3(l)
                fw.barrier()
                if "stop3" in self.dbg:
                    break
                self.phase4a(l)
                fw.barrier()
                self.phase4b(l)
                fw.barrier()
            fw.barrier()
        return nc

    def pair_barrier(self, slot):
        nc, fw = self.nc, self.fw
        fw.barrier()
        q = ("sp", "act")[slot % 2]
        e = fw.e[q]
        flag = self.s["FLAG"]
        fs = slot % 2
        e.dma_start(out=flag[bass.ts(self.par[q], 1), fs:fs + 1], in_=self.ft[0:1, slot:slot + 1]).then_inc(self.fsem, 16)
        self.fn += 16
        e.wait_ge(self.fsem, self.fn)
        ft, psem, partner = self.ft, self.psem, 1 - self.par[q]

        def body(e2):
            with e2.register("rp%d" % slot) as rp, e2.register("rd%d" % slot) as rd, e2.register("rw%d" % slot) as rw:
                e2.load(rw, ft[0:1, slot:slot + 1])

                def cond():
                    e2.sem_clear(psem)
                    e2.dma_start(out=ft[0:1, 16:17], in_=flag[bass.ts(partner, 1), fs:fs + 1]).then_inc(psem, 16)
                    e2.wait_ge(psem, 16)
                    e2.load(rp, ft[0:1, 16:17])
                    e2.reg_sub(rd, rp, rw)
                    return rd
                with e2.While(cond):
                    e2.nop()

        with nc.Block() as block:
            {"sp": block.sync, "act": block.scalar, "pool": block.gpsimd}[q](body)

    def exchange1(self, l):
        fw, s, m, sh, bd = self.fw, self.s, self.m, self.sh, self.bund
        fw.barrier()
        pp = self.par["pool"]
        me, ot = bass.ts(pp, 1), bass.ts(1 - pp, 1)
        fw.dma("pool", sh["FM"][me].rearrange("o f t -> (o f) t"), bd["FM_l"][ot].rearrange("o f t -> (o f) t"))
        fw.dma("pool", sh["XBC"][me].rearrange("o f t -> (o f) t"), s["XBC"][ot].rearrange("o f t -> (o f) t"))
        fw.dma("pool", sh["V"][me].rearrange("o t f -> (o t) f"), bd["V_l"][:, ot, :].rearrange("t o f -> t (o f)"))
        fw.dma("pool", bd["FM_m"][:, me, :].rearrange("f o t -> f (o t)"), bd["FM_l"][me].rearrange("o f t -> (o f) t"))
        fw.dma("pool", m["XBC"][:, me, :].rearrange("f o t -> f (o t)"), s["XBC"][me].rearrange("o f t -> (o f) t"))
        fw.dma("pool", bd["V_m"][me].rearrange("o t f -> (o t) f"), bd["V_l"][:, me, :].rearrange("t o f -> t (o f)"))
        self.pair_barrier(2 * l)
        ps_ = self.par["sp"]
        ot2 = bass.ts(1 - ps_, 1)
        fw.dma("sp", bd["FM_m"][:, ot2, :].rearrange("f o t -> f (o t)"), sh["FM"][ot2].rearrange("o f t -> (o f) t"))
        fw.dma("sp", m["XBC"][:, ot2, :].rearrange("f o t -> f (o t)"), sh["XBC"][ot2].rearrange("o f t -> (o f) t"))
        fw.dma("sp", bd["V_m"][ot2].rearrange("o t f -> (o t) f"), sh["V"][ot2].rearrange("o t f -> (o t) f"))
        fw.barrier()

    def exchange2(self, l):
        fw, s, m, sh, bd = self.fw, self.s, self.m, self.sh, self.bund
        fw.barrier()
        pa = self.par["act"]
        fw.dma("act", sh["Y"][bass.ts(pa, 1)].rearrange("o f a t -> (o f) a t"), bd["Y_m"])
        self.pair_barrier(2 * l + 1)
        for fh in range(2):
            fw.dma("act", bd["Y_t"][fh], sh["Y"][fh, :, bass.ts(pa, 1), :].rearrange("f o t -> f (o t)"))
        fw.barrier()

    def phase0(self):
        nc, fw, L = self.nc, self.fw, self.L
        with ExitStack() as es:
            cv = self.sb(es, "cv", [128, 16], F32)
            sc = self.sb(es, "sc", [128, 16], F32)
            wm = [self.sb(es, "wm%d" % k, [128, 8, 512], F32) for k in range(2)]
            mfm = self.sb(es, "mfm", [128, 2, 48], F32)
            mrow = self.sb(es, "mrow", [96, 128], F32)
            pmod = self.ps(es, "pmod", [128, 48, 2])
            ptr = self.ps(es, "ptr", [96, 128])
            b_cv, b_sc, b_mfm, b_mrow = Buf(), Buf(), Buf(), Buf()
            b_wm = [Buf(), Buf()]
            b_pmod, b_ptr = Buf(True), Buf(True)
            fw.dma("sp", cv[:], self.i["cvec"], writes=[b_cv])
            fw.op("act", lambda e: e.activation(out=sc[:], in_=cv[:], func=AF.Silu), reads=[b_cv], writes=[b_sc])
            for l in range(L):
                for g in range(12):
                    w_, bw = wm[g % 2], b_wm[g % 2]
                    fw.dma("sp", w_[:], self.i["w_mod"][l, :, g * 512:(g + 1) * 512].rearrange("(k p) m -> p k m", p=128),
                           writes=[bw])
                    for jj in range(4):
                        j = g * 4 + jj
                        for k in range(8):
                            fw.op("pe", lambda e, w_=w_, jj=jj, k=k, j=j: e.matmul(
                                pmod[:, j, :], lhsT=w_[:, k, jj * 128:(jj + 1) * 128], rhs=sc[:, 2 * k:2 * k + 2],
                                start=(k == 0), stop=(k == 7)),
                                reads=[bw, b_sc], writes=[b_pmod], sig=(k == 7))
                pvl = self.pv[:, l * NPV:(l + 1) * NPV]
                for w in range(2):
                    fw.op("dve", lambda e, w=w, pvl=pvl: e.tensor_tensor(
                        out=mfm[:, w, :], in0=pmod[:, :, w], in1=pvl[:, PV_BMOD:PV_BMOD + 48], op=ALU.add),
                        reads=[b_pmod, self.b_pv], writes=[b_mfm])
                mp = self.modp
                for w in range(2):
                    fw.op("dve", lambda e, w=w, pvl=pvl, l=l: e.scalar_tensor_tensor(
                        out=mp[:, l, 0, w, :], in0=mfm[:, w, 8:16], scalar=1.0, in1=pvl[:, PV_N1G:PV_N1G + 8],
                        op0=ALU.add, op1=ALU.mult), reads=[b_mfm, self.b_pv], writes=[self.b_modp])
                    fw.op("dve", lambda e, w=w, l=l: e.tensor_copy(out=mp[:, l, 1, w, :], in_=mfm[:, w, 0:8]),
                          reads=[b_mfm], writes=[self.b_modp])
                    fw.op("dve", lambda e, w=w, pvl=pvl, l=l: e.scalar_tensor_tensor(
                        out=mp[:, l, 2, w, :], in0=mfm[:, w, 32:40], scalar=1.0, in1=pvl[:, PV_N2G:PV_N2G + 8],
                        op0=ALU.add, op1=ALU.mult), reads=[b_mfm, self.b_pv], writes=[self.b_modp])
                    fw.op("dve", lambda e, w=w, l=l: e.tensor_copy(out=mp[:, l, 3, w, :], in_=mfm[:, w, 24:32]),
                          reads=[b_mfm], writes=[self.b_modp])
                fw.op("pe", lambda e: e.transpose(ptr[:], mfm[:].rearrange("p w j -> p (w j)"), self.cm[:, CM_ID:CM_ID + 128]),
                      reads=[b_mfm, self.b_cm], writes=[b_ptr])
                fw.op("act", lambda e: e.activation(out=mrow[:], in_=ptr[:], func=AF.Copy), reads=[b_ptr], writes=[b_mrow])
                fw.dma("pool", self.s["MODROW"][l], mrow[:], reads=[b_mrow])

    def _norm_pipe(self, es, l, kA, kS, src, chunks, tag, keep_x=False):
        fw = self.fw
        nx = 2 if keep_x else 1
        xt = [self.sb(es, tag + "xt%d" % k, [128, 4, D], F32) for k in range(nx)]; b_xt = [Buf() for _ in range(nx)]
        junk = self.sb(es, tag + "junk", [128, D], BF16); b_junk = Buf()
        ss = self.sb(es, tag + "ss", [128, 4], F32); b_ss = Buf()
        rstd = self.sb(es, tag + "rstd", [128, 4], F32); b_rstd = Buf()
        xn = self.sb(es, tag + "xn", [128, 4, D], BF16); b_xn = Buf()
        hT = [self.sb(es, tag + "hT%d" % k, [128, 8, 512], BF16) for k in range(2)]; b_hT = [Buf(), Buf()]
        ptr = [self.ps(es, tag + "tr%d" % k, [128, 1024], BF16) for k in range(2)]; b_ptr = [Buf(True), Buf(True)]
        eps_t = self.sb(es, tag + "eps", [128, 1], F32); b_eps = Buf()
        fw.op("pool", lambda e: e.memset(eps_t[:], 1e-6), writes=[b_eps])
        ident = self.cmb[:, CM_ID:CM_ID + 128]
        st = {"tr": 0}

        def load_x(ci):
            t0, n, w = chunks[ci]
            nt = n // 128
            fw.dma("sp", xt[ci % nx][:, 0:nt, :], src[t0:t0 + n, :].rearrange("(j p) f -> p j f", p=128), writes=[b_xt[ci % nx]])

        def prep(ci):
            t0, n, w = chunks[ci]
            nt = n // 128
            slot = ci % 2
            X, BX = xt[ci % nx], b_xt[ci % nx]
            for j in range(nt):
                fw.op("act", lambda e, j=j: e.activation(out=junk[:], in_=X[:, j, :], func=AF.Square, accum_out=ss[:, j:j + 1]),
                      reads=[BX], writes=[b_junk, b_ss])
            fw.op("act", lambda e: e.activation(out=rstd[:, 0:nt], in_=ss[:, 0:nt], func=AF.Sqrt, bias=eps_t[:], scale=1.0 / D),
                  reads=[b_ss, b_eps], writes=[b_rstd])
            fw.op("dve", lambda e: e.reciprocal(out=rstd[:, 0:nt], in_=rstd[:, 0:nt]), reads=[b_rstd], writes=[b_rstd])
            for j in range(nt):
                fw.op("dve", lambda e, j=j: e.tensor_scalar(out=xn[:, j, :], in0=X[:, j, :], scalar1=rstd[:, j:j + 1],
                                                           scalar2=None, op0=ALU.mult), reads=[BX, b_rstd], writes=[b_xn])
            if ci + 1 < len(chunks):
                load_x(ci + 1)
            for k in range(8):
                p = st["tr"] % 2
                st["tr"] += 1
                for j in range(nt):
                    fw.op("pe", lambda e, p=p, j=j, k=k: e.transpose(ptr[p][:, j * 128:(j + 1) * 128],
                                                                     xn[:, j, k * 128:(k + 1) * 128], ident),
                          reads=[b_xn, self.b_cmb], writes=[b_ptr[p]], sig=(j == nt - 1))
                fw.op("act", lambda e, p=p, k=k: e.activation(
                    out=hT[slot][:, k, 0:n], in_=ptr[p][:, 0:n], func=AF.Identity,
                    scale=self.modp[:, l, kA, w, k:k + 1], bias=self.modp[:, l, kS, w, k:k + 1]),
                    reads=[b_ptr[p], self.b_modp], writes=[b_hT[slot]])

        return dict(hT=hT, b_hT=b_hT, load_x=load_x, prep=prep, xt=xt, b_xt=b_xt)

    def phase1(self, l):
        nc, fw, S, T = self.nc, self.fw, self.S, self.T
        src = self.i["xin"] if l == 0 else self.s["XA"]
        sc = self.s
        with ExitStack() as es:
            W = self.sb(es, "W1", [128, 8, INW], BF16)
            wblocks = [(OFF["qs"], 640), (OFF["qa"], 512), (OFF["ka"], 512), (OFF["xb"], 512), (OFF["gb"], 512),
                       (OFF["va"], 512), (OFF["vs"], 128)] + [(OFF["ga"] + g_ * 512, 512) for g_ in range(6)]
            b_Wb = {}
            for (c0_, cw_) in wblocks:
                bb_ = Buf()
                fw.dma("pool", W[:, :, c0_:c0_ + cw_], self.i["w_in"][l, :, c0_:c0_ + cw_].rearrange("(k p) m -> p k m", p=128), writes=[bb_])
                for c_ in range(c0_, c0_ + cw_, 128):
                    b_Wb[c_] = bb_

            class _BW:
                cur = None
            b_W = None
            np_ = self._norm_pipe(es, l, 0, 1, src, self.chunks, "p1")
            hT, b_hT, load_x, prep = np_["hT"], np_["b_hT"], np_["load_x"], np_["prep"]
            stf = [self.sb(es, "stf%d" % k, [128, 4, 512], BF16) for k in range(3)]; b_stf = [Buf(), Buf(), Buf()]
            stx = self.sb(es, "stx", [128, 4, 512], F32); b_stx = Buf()
            stv = [self.sb(es, "stv%d" % k, [128, 4, 512], BF16) for k in range(2)]; b_stv = [Buf(), Buf()]
            stvs = [self.sb(es, "stvs%d" % k, [128, 4, 128], BF16) for k in range(2)]; b_stvs = [Buf(), Buf()]
            qb = self.sb(es, "qb", [128, 5, 512], BF16); b_qb = [Buf() for _ in range(5)]
            cs = self.sb(es, "cs", [128, 2, 512], F32); b_cs = Buf()
            t1 = self.sb(es, "t1", [128, 512], F32); b_t1 = Buf()
            t2 = self.sb(es, "t2", [128, 512], F32); b_t2 = Buf()
            pac = [self.ps(es, "p1ac%d" % k, [128, 512]) for k in range(4)]; b_pac = [Buf(True) for _ in range(4)]
            prp = self.ps(es, "p1rp", [128, 512]); b_prp = Buf(True)
            ident = self.cmb[:, CM_ID:CM_ID + 128]
            perm = self.cmb[:, CM_PERM:CM_PERM + 128]
            st = {"acc": 0, "stf": 0, "stv": 0}
            chunks = self.chunks

            def acc_fm(ci, col0):
                t0, n, w = chunks[ci]
                slot = ci % 2
                a = st["acc"] % 4
                st["acc"] += 1
                for k in range(8):
                    fw.op("pe", lambda e, a=a, k=k: e.matmul(pac[a][:, 0:n], lhsT=W[:, k, col0:col0 + 128],
                                                             rhs=hT[slot][:, k, 0:n], start=(k == 0), stop=(k == 7)),
                          reads=[b_Wb[col0], b_hT[slot]], writes=[b_pac[a]], sig=(k == 7))
                return pac[a], b_pac[a]

            def fm_group(ci, col0, nch, dst, func=AF.Copy, scale=1.0, f32=False, fh3=True):
                t0, n, w = chunks[ci]
                if f32:
                    stg, bst = stx, b_stx
                else:
                    k_ = st["stf"] % 3
                    st["stf"] += 1
                    stg, bst = stf[k_], b_stf[k_]
                for cc in range(nch):
                    p, bp = acc_fm(ci, col0 + cc * 128)
                    eng = "act" if (func != AF.Copy or cc % 2 == 0) else "dve"
                    if eng == "act":
                        fw.op("act", lambda e, p=p, cc=cc: e.activation(out=stg[:, cc, 0:n], in_=p[:, 0:n], func=func, scale=scale),
                              reads=[bp], writes=[bst])
                    else:
                        fw.op("dve", lambda e, p=p, cc=cc: e.tensor_scalar(out=stg[:, cc, 0:n], in0=p[:, 0:n], scalar1=scale,
                                                                          scalar2=None, op0=ALU.mult),
                              reads=[bp], writes=[bst])
                if fh3:
                    for fh_ in range(2):
                        fw.dma("pool", dst[fh_, :, t0:t0 + n].rearrange("(c p) t -> p c t", p=128), stg[:, 2 * fh_:2 * fh_ + 2, 0:n], reads=[bst])
                else:
                    fw.dma("pool", dst[:, t0:t0 + n].rearrange("(c p) t -> p c t", p=128), stg[:, 0:nch, 0:n], reads=[bst])

            def rope_group(ci):
                t0, n, w = chunks[ci]
                k_ = st["stf"] % 3
                st["stf"] += 1
                stg, bst = stf[k_], b_stf[k_]
                k2 = st["stf"] % 3
                st["stf"] += 1
                stg2, bst2 = stf[k2], b_stf[k2]
                if w == 0:
                    fw.dma("sp", cs[:, 0, :], self.i["ropec"][:, t0:t0 + n], writes=[b_cs])
                    fw.dma("sp", cs[:, 1, :], self.i["ropes"][:, t0:t0 + n], writes=[b_cs])
                for cc in range(5):
                    col, sc_ = (OFF["qs"] + cc * 128, 0.125) if cc < 4 else (OFF["ks"], 1.0)
                    p, bp = acc_fm(ci, col)
                    o_t, o_b, o_c = (stg, bst, cc) if cc < 4 else (stg2, bst2, 0)
                    if w == 1:
                        fw.op("act", lambda e, p=p, o_t=o_t, o_c=o_c, sc_=sc_: e.activation(
                            out=o_t[:, o_c, 0:n], in_=p[:, 0:n], func=AF.Copy, scale=sc_), reads=[bp], writes=[o_b])
                    else:
                        fw.op("act", lambda e, p=p, cc=cc, sc_=sc_: e.activation(out=qb[:, cc, 0:n], in_=p[:, 0:n], func=AF.Copy, scale=sc_),
                              reads=[bp], writes=[b_qb[cc]])
                if w == 0:
                    for cc in range(5):
                        o_t, o_b, o_c = (stg, bst, cc) if cc < 4 else (stg2, bst2, 0)
                        rot(cc, o_t, o_b, o_c, n)
                for fh_ in range(2):
                    fw.dma("pool", sc["QS"][fh_, :, t0:t0 + n].rearrange("(c p) t -> p c t", p=128), stg[:, 2 * fh_:2 * fh_ + 2, 0:n], reads=[bst])
                for fh_ in range(2):
                    fw.dma("pool", sc["KS"][fh_, :, t0:t0 + n], stg2[fh_ * 64:(fh_ + 1) * 64, 0, 0:n], reads=[bst2])

            def rot(q_, ot, ob, oc, n):
                fw.op("pe", lambda e: e.matmul(prp[:, 0:n], lhsT=perm, rhs=qb[:, q_, 0:n], start=True, stop=True),
                      reads=[b_qb[q_], self.b_cmb], writes=[b_prp])
                fw.op("dve", lambda e: e.tensor_tensor(out=t1[:, 0:n], in0=qb[:, q_, 0:n], in1=cs[:, 0, 0:n], op=ALU.mult),
                      reads=[b_qb[q_], b_cs], writes=[b_t1])
                fw.op("dve", lambda e: e.tensor_tensor(out=t2[:, 0:n], in0=prp[:, 0:n], in1=cs[:, 1, 0:n], op=ALU.mult),
                      reads=[b_prp, b_cs], writes=[b_t2])
                fw.op("pool", lambda e: e.tensor_tensor(out=ot[:, oc, 0:n], in0=t1[:, 0:n], in1=t2[:, 0:n], op=ALU.add),
                      reads=[b_t1, b_t2], writes=[ob])

            def tm_group(ci, name, ncol, dst, stgs, bsts):
                t0, n, w = chunks[ci]
                slot = ci % 2
                nt = n // 128
                k_ = st["stv"] % 2
                st["stv"] += 1
                stg, bst = stgs[k_], bsts[k_]
                for j in range(nt):
                    a = st["acc"] % 4
                    st["acc"] += 1
                    for k in range(8):
                        fw.op("pe", lambda e, a=a, k=k, j=j: e.matmul(
                            pac[a][:, 0:ncol], lhsT=hT[slot][:, k, j * 128:(j + 1) * 128],
                            rhs=W[:, k, OFF[name]:OFF[name] + ncol], start=(k == 0), stop=(k == 7)),
                            reads=[b_Wb[OFF[name]], b_hT[slot]], writes=[b_pac[a]], sig=(k == 7))
                    if j % 2 == 0:
                        fw.op("act", lambda e, a=a, j=j: e.activation(out=stg[:, j, :], in_=pac[a][:, 0:ncol], func=AF.Copy),
                              reads=[b_pac[a]], writes=[bst])
                    else:
                        fw.op("dve", lambda e, a=a, j=j: e.tensor_copy(out=stg[:, j, :], in_=pac[a][:, 0:ncol]),
                              reads=[b_pac[a]], writes=[bst])
                hw_ = ncol // 2
                for a_ in range(2):
                    fw.dma("pool", dst[t0:t0 + n, a_, :].rearrange("(j p) f -> p j f", p=128), stg[:, 0:nt, a_ * hw_:(a_ + 1) * hw_], reads=[bst])

            load_x(0)
            prep(0)
            gates = [(gi, half) for gi in range(3) for half in range(2)]
            for ci in range(len(chunks)):
                rope_group(ci)
                fm_group(ci, OFF["qa"], 4, sc["QA"], scale=0.125)
                fm_group(ci, OFF["ka"], 4, sc["KA"])
                fm_group(ci, OFF["xb"], 4, sc["XBC"], f32=True)
                fm_group(ci, OFF["gb"], 4, sc["GB"])
                tm_group(ci, "va", 512, sc["VA"], stv, b_stv)
                tm_group(ci, "vs", 128, sc["VS"], stvs, b_stvs)
                for gn, (gi, half) in enumerate(gates):
                    if gn == 2 and ci + 1 < len(chunks):
                        prep(ci + 1)
                    base = gi * 1024 + half * 512
                    fm_group(ci, OFF["ga"] + base, 4, sc["GT"][base:base + 512], func=AF.Sigmoid, fh3=False)

    def phase_lru(self, l):
        nc, fw, S, T = self.nc, self.fw, self.S, self.T
        sc = self.s
        pvl = self.pv[:, l * NPV:(l + 1) * NPV]

        def rev(ap_):
            n = ap_.ap[-1][1]
            return AP(ap_.tensor, ap_.offset + n - 1, [list(ap_.ap[0]), [-1, n]])

        with ExitStack() as es:
            wbd = self.sb(es, "wbd", [128, 16, 128], BF16); b_wbd = Buf()
            fw.dma("pool", wbd[:], self.i["lruw"][l].rearrange("d a c k m -> k (d a c) m"), writes=[b_wbd])
            sm = self.sb(es, "lsm", [128, 5, 8], F32); b_sm = Buf()
            fw.op("act", lambda e: e.activation(out=sm[:, 4, :], in_=pvl[:, PV_LAM:PV_LAM + 8], func=AF.Exp, scale=-1.0),
                  reads=[self.b_pv], writes=[b_sm])
            sq = self.sb(es, "lsq", [128, 3, 8], F32); b_sq = Buf()
            fw.op("dve", lambda e: e.tensor_scalar(out=sq[:, 0, :], in0=sm[:, 4, :], scalar1=2.0, scalar2=None, op0=ALU.add),
                  reads=[b_sm], writes=[b_sq])
            fw.op("dve", lambda e: e.reciprocal(out=sq[:, 0, :], in_=sq[:, 0, :]), reads=[b_sq], writes=[b_sq])
            fw.op("dve", lambda e: e.tensor_tensor(out=sq[:, 0, :], in0=sq[:, 0, :], in1=sm[:, 4, :], op=ALU.mult),
                  reads=[b_sq, b_sm], writes=[b_sq])
            fw.op("dve", lambda e: e.tensor_tensor(out=sq[:, 1, :], in0=sq[:, 0, :], in1=sq[:, 0, :], op=ALU.mult),
                  reads=[b_sq], writes=[b_sq])
            fw.op("dve", lambda e: e.tensor_scalar(out=sq[:, 2, :], in0=sq[:, 1, :], scalar1=0.2, scalar2=1.0 / 3.0,
                                                   op0=ALU.mult, op1=ALU.add), reads=[b_sq], writes=[b_sq])
            fw.op("dve", lambda e: e.tensor_tensor(out=sq[:, 2, :], in0=sq[:, 2, :], in1=sq[:, 1, :], op=ALU.mult),
                  reads=[b_sq], writes=[b_sq])
            fw.op("dve", lambda e: e.scalar_tensor_tensor(out=sm[:, 4, :], in0=sq[:, 2, :], scalar=1.0, in1=sq[:, 0, :],
                                                          op0=ALU.add, op1=ALU.mult), reads=[b_sq], writes=[b_sm])
            fw.op("dve", lambda e: e.tensor_scalar(out=sm[:, 0, :], in0=sm[:, 4, :], scalar1=-8.0, scalar2=None, op0=ALU.mult),
                  reads=[b_sm], writes=[b_sm])
            fw.op("dve", lambda e: e.tensor_scalar(out=sm[:, 1, :], in0=sm[:, 4, :], scalar1=-16.0, scalar2=None, op0=ALU.mult),
                  reads=[b_sm], writes=[b_sm])
            fw.op("dve", lambda e: e.tensor_scalar(out=sm[:, 2, :], in0=pvl[:, PV_BA:PV_BA + 8], scalar1=0.5, scalar2=None, op0=ALU.mult),
                  reads=[self.b_pv], writes=[b_sm])
            fw.op("dve", lambda e: e.tensor_scalar(out=sm[:, 3, :], in0=pvl[:, PV_BX:PV_BX + 8], scalar1=0.5, scalar2=None, op0=ALU.mult),
                  reads=[self.b_pv], writes=[b_sm])
            self.dump("sm", sm[:].rearrange("p a b -> p (a b)"), [b_sm])
            half = self.sb(es, "lhalf", [128, 512], F32); b_half = Buf()
            fw.op("pool", lambda e: e.memset(half[:], 1.0), writes=[b_half])
            xbt = self.sb(es, "lxb", [128, T], F32); b_xbt = Buf()
            u = self.sb(es, "lu", [128, T], F32); b_u = Buf()
            ubf = self.sb(es, "lubf", [128, T], BF16); b_ubf = Buf()
            hf, b_hf = xbt, b_xbt
            NU = 7
            ut = [[self.sb(es, "lut%d_%d" % (k, j), [128, 512], F32) for j in range(NU)] for k in range(2)]
            b_ut = [[Buf() for _ in range(NU)] for _ in range(2)]
            hr = [self.sb(es, "lhr%d" % k, [128, 512], F32) for k in range(2)]; b_hr = [Buf(), Buf()]
            gbt = [self.sb(es, "lgb%d" % k, [128, 512], BF16) for k in range(2)]; b_gbt = [Buf(), Buf()]
            gt = [self.sb(es, "lgt%d" % k, [128, 512], F32) for k in range(3)]; b_gt = [Buf() for _ in range(3)]
            yst = [self.sb(es, "lys%d" % k, [128, 512], BF16) for k in range(2)]; b_yst = [Buf(), Buf()]
            pp = [[self.ps(es, "lp%d_%d" % (k, j), [128, 512]) for j in range(2)] for k in range(2)]
            b_pp = [[Buf(True), Buf(True)] for _ in range(2)]
            lat = [(c * 512, 512) for c in range(S // 512)]
            ctxs = [(S, CTX)]
            uc = 0
            for ch in range(2):
                rows = slice(ch * 128, (ch + 1) * 128)
                self.ld_nat("sp", lambda a_, n_: xbt[:, a_:a_ + n_], self.m["XBC"], rows, [b_xbt])
                cw = lambda tap: pvl[:, PV_CW + ch * 4 + tap:PV_CW + ch * 4 + tap + 1]
                for (a0, n0) in ((0, S), (S, CTX)):
                    fw.op("dve", lambda e, a0=a0, n0=n0: e.tensor_scalar(
                        out=u[:, a0:a0 + n0], in0=xbt[:, a0:a0 + n0], scalar1=cw(2), scalar2=pvl[:, PV_CB + ch:PV_CB + ch + 1],
                        op0=ALU.mult, op1=ALU.add), reads=[b_xbt, self.b_pv], writes=[b_u])
                    for tap, so, do, ln in ((0, 0, 2, n0 - 2), (1, 0, 1, n0 - 1), (3, 1, 0, n0 - 1)):
                        fw.op("dve", lambda e, a0=a0, tap=tap, so=so, do=do, ln=ln: e.scalar_tensor_tensor(
                            out=u[:, a0 + do:a0 + do + ln], in0=xbt[:, a0 + so:a0 + so + ln], scalar=cw(tap),
                            in1=u[:, a0 + do:a0 + do + ln], op0=ALU.mult, op1=ALU.add),
                            reads=[b_xbt, self.b_pv, b_u], writes=[b_u])
                fw.op("act", lambda e: e.activation(out=ubf[:], in_=u[:], func=AF.Copy), reads=[b_u], writes=[b_ubf])
                ulist = [(d_, t0_, n_) for d_ in range(2) for (t0_, n_) in ((ctxs + lat) if d_ == 0 else (ctxs + lat[::-1]))]

                def gate_mm(ui_):
                    d_, t0_, n_ = ulist[ui_]
                    kk_ = (uc + ui_) % 2
                    for ax in range(2):
                        fw.op("pe", lambda e, ax=ax: e.matmul(
                            pp[kk_][ax][:, 0:n_], lhsT=wbd[:, (d_ * 2 + ax) * 4 + ch, :], rhs=ubf[:, t0_:t0_ + n_],
                            start=True, stop=True), reads=[b_wbd, b_ubf], writes=[b_pp[kk_][ax]])

                gate_mm(0)
                uidx = 0
                for d in range(2):
                    segs = (ctxs + lat) if d == 0 else (ctxs + lat[::-1])
                    prev = None
                    col = d * 4 + ch
                    for (t0, n) in segs:
                        k_ = (uc + uidx) % 2
                        if uidx + 1 < len(ulist):
                            gate_mm(uidx + 1)
                        uidx += 1
                        T_ = ut[k_]
                        B_ = b_ut[k_]
                        fw.op("act", lambda e, k_=k_: e.activation(out=T_[0][:, 0:n], in_=pp[k_][0][:, 0:n], func=AF.Tanh,
                                                                scale=0.5, bias=sm[:, 2, col:col + 1]),
                              reads=[b_pp[k_][0], b_sm], writes=[B_[0]])
                        fw.op("act", lambda e, k_=k_: e.activation(out=T_[1][:, 0:n], in_=pp[k_][1][:, 0:n], func=AF.Tanh,
                                                                scale=0.5, bias=sm[:, 3, col:col + 1]),
                              reads=[b_pp[k_][1], b_sm], writes=[B_[1]])
                        fw.op("act", lambda e: e.activation(out=T_[2][:, 0:n], in_=T_[0][:, 0:n], func=AF.Exp,
                                                            scale=sm[:, 0, col:col + 1], bias=sm[:, 0, col:col + 1]),
                              reads=[B_[0], b_sm], writes=[B_[2]])
                        fw.op("act", lambda e: e.activation(out=T_[3][:, 0:n], in_=T_[0][:, 0:n], func=AF.Exp,
                                                            scale=sm[:, 1, col:col + 1], bias=sm[:, 1, col:col + 1]),
                              reads=[B_[0], b_sm], writes=[B_[3]])
                        fw.op("act", lambda e: e.activation(out=T_[4][:, 0:n], in_=T_[3][:, 0:n], func=AF.Sqrt, scale=-1.0, bias=half[:, 0:1]),
                              reads=[B_[3], b_half], writes=[B_[4]])
                        fw.op("dve", lambda e: e.scalar_tensor_tensor(out=T_[5][:, 0:n], in0=T_[1][:, 0:n], scalar=1.0,
                                                                      in1=u[:, t0:t0 + n], op0=ALU.add, op1=ALU.mult),
                              reads=[B_[1], b_u], writes=[B_[5]])
                        fw.op("dve", lambda e: e.scalar_tensor_tensor(out=T_[6][:, 0:n], in0=T_[5][:, 0:n], scalar=0.5,
                                                                      in1=T_[4][:, 0:n], op0=ALU.mult, op1=ALU.mult),
                              reads=[B_[5], B_[4]], writes=[B_[6]])
                        if uc == 1:
                            for q_ in range(7):
                                self.dump("T%d" % q_, T_[q_][:, 0:n], [B_[q_]])
                            self.dump("u", u[:, t0:t0 + n], [b_u])
                        if d == 0:
                            init = 0.0 if prev is None else hf[:, prev[0] + prev[1] - 1:prev[0] + prev[1]]
                            fw.op("dve", lambda e, init=init: e.tensor_tensor_scan(
                                out=hf[:, t0:t0 + n], data0=T_[2][:, 0:n], data1=T_[6][:, 0:n], initial=init,
                                op0=ALU.mult, op1=ALU.add), reads=[B_[2], B_[6], b_hf], writes=[b_hf])
                        else:
                            init = 0.0 if prev is None else hr[1 - k_][:, 0:1]
                            rd = [B_[2], B_[6]] + ([] if prev is None else [b_hr[1 - k_]])
                            fw.op("dve", lambda e, init=init, k_=k_: e.tensor_tensor_scan(
                                out=rev(hr[k_][:, 0:n]), data0=rev(T_[2][:, 0:n]), data1=rev(T_[6][:, 0:n]), initial=init,
                                op0=ALU.mult, op1=ALU.add), reads=rd, writes=[b_hr[k_]])
                            G, BG = gbt[k_], b_gbt[k_]
                            for (h_, tl_, nn_, o_) in self.tmap(t0, n):
                                fw.dma("sp", G[:, o_:o_ + nn_], self.m["GB"][rows, h_, tl_:tl_ + nn_], writes=[BG])
                            fw.op("pool", lambda e, G=G: e.tensor_tensor(out=gt[0][:, 0:n], in0=G[:, 0:n], in1=G[:, 0:n], op=ALU.mult),
                                  reads=[BG], writes=[b_gt[0]])
                            fw.op("pool", lambda e: e.tensor_scalar(out=gt[0][:, 0:n], in0=gt[0][:, 0:n], scalar1=0.044715, scalar2=1.0,
                                                                    op0=ALU.mult, op1=ALU.add), reads=[b_gt[0]], writes=[b_gt[0]])
                            fw.op("pool", lambda e, G=G: e.tensor_tensor(out=gt[0][:, 0:n], in0=gt[0][:, 0:n], in1=G[:, 0:n], op=ALU.mult),
                                  reads=[b_gt[0], BG], writes=[b_gt[0]])
                            fw.op("act", lambda e: e.activation(out=gt[1][:, 0:n], in_=gt[0][:, 0:n], func=AF.Tanh, scale=0.7978845608028654),
                                  reads=[b_gt[0]], writes=[b_gt[1]])
                            fw.op("pool", lambda e: e.tensor_scalar(out=gt[1][:, 0:n], in0=gt[1][:, 0:n], scalar1=0.5, scalar2=0.5,
                                                                    op0=ALU.mult, op1=ALU.add), reads=[b_gt[1]], writes=[b_gt[1]])
                            fw.op("pool", lambda e, G=G: e.tensor_tensor(out=gt[1][:, 0:n], in0=gt[1][:, 0:n], in1=G[:, 0:n], op=ALU.mult),
                                  reads=[b_gt[1], BG], writes=[b_gt[1]])
                            fw.op("pool", lambda e, k_=k_: e.tensor_tensor(out=gt[2][:, 0:n], in0=hf[:, t0:t0 + n], in1=hr[k_][:, 0:n], op=ALU.add),
                                  reads=[b_hf, b_hr[k_]], writes=[b_gt[2]])
                            Y, BY = yst[k_], b_yst[k_]
                            fw.op("pool", lambda e, Y=Y: e.tensor_tensor(out=Y[:, 0:n], in0=gt[2][:, 0:n], in1=gt[1][:, 0:n], op=ALU.mult),
                                  reads=[b_gt[2], b_gt[1]], writes=[BY])
                            for (h_, tl_, nn_, o_) in self.tmap(t0, n):
                                fw.dma("pool", self.m["YB"][rows, h_, tl_:tl_ + nn_], Y[:, o_:o_ + nn_], reads=[BY])
                        prev = (t0, n)
                        yield
                uc += len(ulist)

    def phase_na(self, l):
        nc, fw, S, T, R = self.nc, self.fw, self.S, self.T, self.R
        sc = self.s
        NT = T // 128
        NQ = R // 2
        upd_ctx = l < self.L - 1
        with ExitStack() as es:
            KT = self.sb(es, "nKT", [128, T], BF16); b_KT = Buf()
            Qm = [self.sb(es, "nQ%d" % k, [128, T], BF16) for k in range(2)]; b_Qm = [Buf(), Buf()]
            Vp = self.sb(es, "nV", [128, NT, 2, 128], BF16); b_Vp = Buf()
            onesp = self.sb(es, "nones", [128, 2, 128], BF16); b_ones = Buf()
            btf = self.sb(es, "nbtf", [128, 1280], F32); b_btf = Buf()
            bt = self.sb(es, "nbt", [128, 5, 1280], BF16); b_bt = Buf()
            P = [[self.sb(es, "nP%d_%d" % (a, b), [128, 7, 128], BF16) for b in range(2)] for a in range(2)]
            b_P = [[Buf(), Buf()] for _ in range(2)]
            rd = self.sb(es, "nrd", [128, 128], F32); b_rd = Buf()
            yst = [self.sb(es, "nys%d" % k, [128, 512], BF16) for k in range(2)]; b_yst = [Buf(), Buf()]
            Sp = [self.ps(es, "nS%d" % k, [128, 1024]) for k in range(2)]; b_Sp = [Buf(True), Buf(True)]
            po = [self.ps(es, "npo%d" % k, [128, 512]) for k in range(2)]; b_po = [Buf(True), Buf(True)]
            pd = [self.ps(es, "npd%d" % k, [128, 512]) for k in range(2)]; b_pd = [Buf(True), Buf(True)]
            ident = self.cmb[:, CM_ID:CM_ID + 128]
            fw.op("pool", lambda e: e.memset(Vp[:], 0.0), writes=[b_Vp])
            fw.op("pool", lambda e: e.memset(onesp[:], 0.0), writes=[b_ones])
            for h2 in range(2):
                fw.op("pool", lambda e, h2=h2: e.memset(onesp[:, h2, h2 * 64:(h2 + 1) * 64], 1.0), writes=[b_ones])
                fw.op("pool", lambda e, h2=h2: e.memset(Qm[h2][(1 - h2) * 64:(2 - h2) * 64, :], 0.0), writes=[b_Qm[h2]])

            def kvtile(q, j):
                if q is None:
                    return S // 128 + j
                kb = min(max(2 * q - 4, 0), R - 10)
                return kb // 2 + j if j < 5 else S // 128 + (j - 5)

            def variant(q):
                return {0: 1, 1: 2, NQ - 2: 3, NQ - 1: 4}.get(q, 0)

            SLT = self.SL // 128
            for hp in range(2):
                self.ld_nat("sp", lambda a_, n_: KT[:, a_:a_ + n_], self.m["KA"], slice(hp * 128, (hp + 1) * 128), [b_KT])
                for h2 in range(2):
                    r0 = hp * 128 + h2 * 64
                    self.ld_nat("sp", lambda a_, n_, h2=h2: Qm[h2][h2 * 64:(h2 + 1) * 64, a_:a_ + n_], self.m["QA"], slice(r0, r0 + 64), [b_Qm[h2]])
                    for h_ in range(2):
                        fw.dma("sp", Vp[:, h_ * SLT:(h_ + 1) * SLT, h2, h2 * 64:(h2 + 1) * 64],
                               self.m["VA"][h_, 0:self.SL, r0:r0 + 64].rearrange("(j p) f -> p j f", p=128), writes=[b_Vp])
                        fw.dma("sp", Vp[:, S // 128 + h_, h2, h2 * 64:(h2 + 1) * 64], self.m["VA"][h_, self.SL:self.TL, r0:r0 + 64], writes=[b_Vp])
                for v_ in range(5):
                    fw.dma("sp", btf[:], self.i["nabias"][l, v_, hp], writes=[b_btf])
                    fw.op("act", lambda e, v_=v_: e.activation(out=bt[:, v_, :], in_=btf[:], func=AF.Exp), reads=[b_btf], writes=[b_bt])
                items = [("lat", q) for q in range(NQ)] + ([("ctx", b) for b in range(2)] if upd_ctx else [])

                def qk(it, pi):
                    kind, q = it
                    q0 = q * 128 if kind == "lat" else S + q * 128
                    ntile = 7 if kind == "lat" else 2
                    for h2 in range(2):
                        for j in range(ntile):
                            kt = kvtile(q if kind == "lat" else None, j)
                            loc = kind == "lat" and j < 5
                            fw.op("pe", lambda e, h2=h2, j=j, kt=kt: e.matmul(
                                Sp[h2][:, j * 128:(j + 1) * 128], lhsT=KT[:, kt * 128:(kt + 1) * 128],
                                rhs=Qm[h2][:, q0:q0 + 128], start=True, stop=True),
                                reads=[b_KT, b_Qm[h2]], writes=[b_Sp[h2]], sig=(j == ntile - 1))
                        fw.op("act", lambda e, h2=h2: e.activation(
                            out=P[pi][h2][:, 0:ntile, :].rearrange("p a b -> p (a b)"), in_=Sp[h2][:, 0:ntile * 128], func=AF.Exp),
                            reads=[b_Sp[h2]], writes=[b_P[pi][h2]])
                        if kind == "lat":
                            v = variant(q)
                            fw.op("dve" if h2 == 0 else "pool", lambda e, h2=h2, v=v: e.tensor_tensor(
                                out=P[pi][h2][:, 0:5, :].rearrange("p a b -> p (a b)"), in0=P[pi][h2][:, 0:5, :].rearrange("p a b -> p (a b)"),
                                in1=bt[:, v, h2 * 640:(h2 + 1) * 640], op=ALU.mult), reads=[b_P[pi][h2], b_bt], writes=[b_P[pi][h2]])

                def pv(it, pi, idx):
                    kind, q = it
                    q0 = q * 128 if kind == "lat" else S + q * 128
                    ntile = 7 if kind == "lat" else 2
                    o = idx % 2
                    nmm = 2 * ntile
                    for (dst, bdst, isden) in ((po[o], b_po[o], False), (pd[o], b_pd[o], True)):
                        c = 0
                        for h2 in range(2):
                            for j in range(ntile):
                                kt = kvtile(q if kind == "lat" else None, j)
                                lh = onesp[:, h2, :] if isden else Vp[:, kt, h2, :]
                                fw.op("pe", lambda e, lh=lh, h2=h2, j=j, c=c, dst=dst: e.matmul(
                                    dst[:, 0:128], lhsT=lh, rhs=P[pi][h2][:, j, :], start=(c == 0), stop=(c == nmm - 1)),
                                    reads=[b_ones if isden else b_Vp, b_P[pi][h2]], writes=[bdst], sig=(c == nmm - 1))
                                c += 1
                    fw.op("dve", lambda e: e.reciprocal(out=rd[:], in_=pd[o][:, 0:128]), reads=[b_pd[o]], writes=[b_rd])
                    ys = (idx // 4) % 2
                    col = (idx % 4) * 128
                    fw.op("dve", lambda e: e.tensor_tensor(out=yst[ys][:, col:col + 128], in0=po[o][:, 0:128], in1=rd[:], op=ALU.mult),
                          reads=[b_po[o], b_rd], writes=[b_yst[ys]])
                    last = (idx == len(items) - 1)
                    if idx % 4 == 3 or last:
                        nb = idx % 4 + 1
                        first_it = items[idx - nb + 1]
                        t0 = first_it[1] * 128 if first_it[0] == "lat" else S + first_it[1] * 128
                        for (h_, tl_, nn_, o_) in self.tmap(t0, nb * 128):
                            fw.dma("pool", self.m["YA"][hp * 128:(hp + 1) * 128, h_, tl_:tl_ + nn_], yst[ys][:, o_:o_ + nn_], reads=[b_yst[ys]])

                qk(items[0], 0)
                for idx, it in enumerate(items):
                    if idx + 1 < len(items):
                        qk(items[idx + 1], (idx + 1) % 2)
                    pv(it, idx % 2, idx)

    def phase_swa(self, l):
        nc, fw, S, T = self.nc, self.fw, self.S, self.T
        sc = self.s
        NT = T // 128
        NB = S // 128
        upd_ctx = l < self.L - 1
        pvl = self.pv[:, l * NPV:(l + 1) * NPV]
        with ExitStack() as es:
            KSb = self.sb(es, "sK", [128, T], BF16); b_K = Buf()
            Qp = [self.sb(es, "sQ%d" % a, [128, 4, 512], BF16) for a in range(2)]; b_Qp = [Buf(), Buf()]
            Vp = self.sb(es, "sV", [128, NT, 128], BF16); b_Vp = Buf()
            onesp = self.sb(es, "sones", [128, 128], BF16); b_ones = Buf()
            esk = self.sb(es, "sesk", [128, 1], F32); b_esk = Buf()
            stab = self.sb(es, "sstab", [128, 2, 256], BF16); b_stab = Buf()
            P = [self.sb(es, "sP%d" % a, [128, 5, 256], BF16) for a in range(2)]; b_P = [Buf(), Buf()]
            rd = self.sb(es, "srd", [128, 256], F32); b_rd = Buf()
            yst = [[self.sb(es, "sys%d_%d" % (a, hh), [128, 2, 512], BF16) for hh in range(2)] for a in range(2)]
            b_yst = [[Buf(), Buf()] for _ in range(2)]
            Sg0 = self.ps(es, "sS0", [128, 1536]); b_Sg0 = Buf(True)
            Sg, b_Sg = [Sg0, Sg0], [b_Sg0, b_Sg0]
            pod = self.ps(es, "spod", [128, 512]); b_pod = Buf(True)
            po, pd, b_po, b_pd = pod[:, 0:256], pod[:, 256:512], b_pod, b_pod
            ident = self.cmb[:, CM_ID:CM_ID + 128]
            esel = self.cmb[:, CM_ESEL:CM_ESEL + 128]
            masks = {0: self.cmb[:, CM_MPREV:CM_MPREV + 256], 2: self.cmb[:, CM_MNEXT:CM_MNEXT + 256]}
            fw.op("pool", lambda e: e.memset(Vp[:], 0.0), writes=[b_Vp])
            fw.op("pool", lambda e: e.memset(onesp[:], 0.0), writes=[b_ones])
            fw.op("pool", lambda e: e.memset(onesp[:, 0:64], 1.0), writes=[b_ones])
            fw.op("pool", lambda e: e.memset(KSb[64:128, :], 0.0), writes=[b_K])
            for a in range(2):
                fw.op("pool", lambda e, a=a: e.memset(Qp[a][64:128, :, :], 0.0), writes=[b_Qp[a]])
            fw.op("act", lambda e: e.activation(out=esk[:], in_=pvl[:, PV_SINK:PV_SINK + 1], func=AF.Exp), reads=[self.b_pv], writes=[b_esk])
            for hh in range(2):
                c0 = CM_SEL0 if hh == 0 else CM_SEL1
                fw.op("dve", lambda e, hh=hh, c0=c0: e.tensor_scalar(out=stab[:, hh, :], in0=self.cm[:, c0:c0 + 256], scalar1=esk[:, 0:1],
                                                                    scalar2=None, op0=ALU.mult), reads=[self.b_cm, b_esk], writes=[b_stab])
            SLT = self.SL // 128
            self.ld_nat("sp", lambda a_, n_: KSb[0:64, a_:a_ + n_], self.m["KS"], slice(0, 64), [b_K])
            for h_ in range(2):
                fw.dma("sp", Vp[:, h_ * SLT:(h_ + 1) * SLT, 0:64], self.m["VS"][h_, 0:self.SL, :].rearrange("(j p) f -> p j f", p=128), writes=[b_Vp])
                fw.dma("sp", Vp[:, S // 128 + h_, 0:64], self.m["VS"][h_, self.SL:self.TL, :], writes=[b_Vp])
            sbs = [(c * 512, 4, False) for c in range(S // 512)] + ([(S, 2, True)] if upd_ctx else [])

            def load_q(si):
                t0, nb, isc = sbs[si]
                a = si % 2
                for r in range(4):
                    for (h_, tl_, nn_, o_) in self.tmap(t0, nb * 128):
                        fw.dma("sp", Qp[a][0:64, r, o_:o_ + nn_], self.m["QS"][r * 64:(r + 1) * 64, h_, tl_:tl_ + nn_], writes=[b_Qp[a]])

            units = []
            for si, (t0, nb, isc) in enumerate(sbs):
                for nl in range(nb):
                    for hh in range(2):
                        units.append((si, nl, hh))

            def tiles(si, nl):
                t0, nb, isc = sbs[si]
                res = []
                if not isc:
                    n = t0 // 128 + nl
                    if n >= 1:
                        res.append((0, n - 1))
                    res.append((1, n))
                    if n <= NB - 2:
                        res.append((2, n + 1))
                res.append((3, S // 128))
                res.append((4, S // 128 + 1))
                return res

            def qk(ui, pi):
                si, nl, hh = units[ui]
                a = si % 2
                tl = tiles(si, nl)
                g = pi
                for (j, kt) in tl:
                    mk = masks.get(j)
                    fw.op("pe", lambda e, j=j, kt=kt, mk=mk: e.matmul(
                        Sg[g][:, j * 256:(j + 1) * 256], lhsT=KSb[:, kt * 128:(kt + 1) * 128],
                        rhs=Qp[a][:, 2 * hh:2 * hh + 2, nl * 128:(nl + 1) * 128], start=True, stop=(mk is None)),
                        reads=[b_K, b_Qp[a]], writes=[b_Sg[g]], sig=(mk is None and j == 4))
                    if mk is not None:
                        fw.op("pe", lambda e, j=j, mk=mk: e.matmul(
                            Sg[g][:, j * 256:(j + 1) * 256], lhsT=ident, rhs=mk, start=False, stop=True),
                            reads=[self.b_cmb], writes=[b_Sg[g]], sig=False)
                js = [j for (j, _) in tl]
                runs = []
                for j in js:
                    if runs and runs[-1][1] == j:
                        runs[-1][1] = j + 1
                    else:
                        runs.append([j, j + 1])
                for (j0, j1) in runs:
                    fw.op("act", lambda e, j0=j0, j1=j1: e.activation(
                        out=P[pi][:, j0:j1, :].rearrange("p a b -> p (a b)"), in_=Sg[g][:, j0 * 256:j1 * 256], func=AF.Exp),
                        reads=[b_Sg[g]], writes=[b_P[pi]])

            def pv(ui, pi):
                si, nl, hh = units[ui]
                t0, nb, isc = sbs[si]
                a = si % 2
                tl = tiles(si, nl)
                nmm = len(tl)
                for isden in (False, True):
                    dst, bdst = (pd, b_pd) if isden else (po, b_po)
                    for c, (j, kt) in enumerate(tl):
                        lh = onesp[:] if isden else Vp[:, kt, :]
                        fw.op("pe", lambda e, lh=lh, j=j, c=c, dst=dst, isden=isden: e.matmul(
                            dst, lhsT=lh, rhs=P[pi][:, j, :], start=(c == 0), stop=(c == nmm - 1 and not isden)),
                            reads=[b_ones if isden else b_Vp, b_P[pi]], writes=[bdst], sig=(c == nmm - 1 and not isden))
                    if isden:
                        fw.op("pe", lambda e: e.matmul(pd, lhsT=esel, rhs=stab[:, hh, :], start=False, stop=True),
                              reads=[self.b_cmb, b_stab], writes=[b_pd])
                fw.op("dve", lambda e: e.reciprocal(out=rd[0:64, :], in_=pod[0:64, 256:512]), reads=[b_pd], writes=[b_rd])
                Y, BY = yst[a][hh], b_yst[a][hh]
                fw.op("dve", lambda e: e.tensor_tensor(out=Y[0:64, :, nl * 128:(nl + 1) * 128],
                                                       in0=pod[0:64, 0:256].rearrange("p (r t) -> p r t", r=2),
                                                       in1=rd[0:64, :].rearrange("p (r t) -> p r t", r=2), op=ALU.mult),
                      reads=[b_po, b_rd], writes=[BY])
                if nl == nb - 1:
                    for rr in range(2):
                        r_ = 2 * hh + rr
                        for (h_, tl_, nn_, o_) in self.tmap(t0, nb * 128):
                            fw.dma("pool", self.m["YC"][r_ * 64:(r_ + 1) * 64, h_, tl_:tl_ + nn_], Y[0:64, rr, o_:o_ + nn_], reads=[BY])

            load_q(0)
            qk(0, 0)
            for ui in range(len(units)):
                si, nl, hh = units[ui]
                if nl == 0 and hh == 0 and si + 1 < len(sbs):
                    load_q(si + 1)
                if ui + 1 < len(units):
                    qk(ui + 1, (ui + 1) % 2)
                pv(ui, ui % 2)
                yield

    def _gate_tiles(self, es, l, grp, tag):
        fw = self.fw
        G = self.sb(es, tag + "G", [128, 2, D], F32); b_G = Buf()
        for w in range(2):
            row = self.s["MODROW"][l, w * 48 + grp * 8:w * 48 + grp * 8 + 8, :].rearrange("j p -> (j p)")
            fw.dma("sp", G[:, w, :], row.partition_broadcast(128), writes=[b_G])
        return G, b_G

    def phase3(self, l):
        nc, fw, S, T = self.nc, self.fw, self.S, self.T
        sc = self.s
        src = self.i["xin"] if l == 0 else sc["XA"]
        chunks = self.chunks if l < self.L - 1 else self.chunks[:-1]
        with ExitStack() as es:
            Wb = self.sb(es, "m3Wb", [128, 12, D], BF16); b_Wb = Buf()
            Wo = self.sb(es, "m3Wo", [128, 8, D], BF16); b_Wo = Buf()
            b_Wbs = [Buf() for _ in range(4)]
            for q4 in range(4):
                fw.dma("pool", Wb[:, :, q4 * 256:(q4 + 1) * 256], self.i["w_branch"][l, :, q4 * 256:(q4 + 1) * 256].rearrange("(k p) m -> p k m", p=128),
                       writes=[b_Wbs[q4]])
            fw.dma("pool", Wo[:], self.i["w_out"][l].rearrange("(k p) m -> p k m", p=128), writes=[b_Wo])
            G, b_G = self._gate_tiles(es, l, 2, "m3")
            Y = [self.sb(es, "m3Y%d" % k, [128, 12, 512], BF16) for k in range(2)]; b_Y = [Buf(), Buf()]
            GT = [self.sb(es, "m3GT%d" % k, [128, 24, 512], BF16) for k in range(2)]; b_GT = [Buf(), Buf()]
            X = [self.sb(es, "m3X%d" % k, [128, 4, D], F32) for k in range(2)]; b_X = [Buf(), Buf()]
            mT = [self.sb(es, "m3mT%d" % k, [128, 8, 512], BF16) for k in range(2)]; b_mT = [Buf(), Buf()]
            tt = [[self.sb(es, "m3t%d_%d" % (a, k), [128, 512], F32) for k in range(3)] for a in range(2)]
            b_tt = [[Buf() for _ in range(3)] for _ in range(2)]
            pb = [self.ps(es, "m3pb%d" % k, [128, 512]) for k in range(6)]; b_pb = [Buf(True) for _ in range(6)]
            pq = [self.ps(es, "m3pq%d" % k, [128, 512]) for k in range(2)]; b_pq = [Buf(True), Buf(True)]

            def load(ci):
                t0, n, w = chunks[ci]
                a = ci % 2
                for bi, nm in enumerate(("YA", "YB", "YC")):
                    for fh_ in range(2):
                        fw.dma("sp", Y[a][:, 4 * bi + 2 * fh_:4 * bi + 2 * fh_ + 2, 0:n],
                               sc[nm][fh_, :, t0:t0 + n].rearrange("(c p) t -> p c t", p=128), writes=[b_Y[a]])
                fw.dma("sp", GT[a][:, :, 0:n], sc["GT"][:, t0:t0 + n].rearrange("(c p) t -> p c t", p=128), writes=[b_GT[a]])
                fw.dma("sp", X[a][:, 0:n // 128, :], src[t0:t0 + n, :].rearrange("(j p) f -> p j f", p=128), writes=[b_X[a]])

            load(0)
            cnt = 0
            for ci, (t0, n, w) in enumerate(chunks):
                a = ci % 2
                nt = n // 128
                if ci + 1 < len(chunks):
                    load(ci + 1)
                for oc in range(8):
                    s3 = (oc % 2) * 3
                    tset = oc % 2
                    for bi in range(3):
                        for k in range(4):
                            fw.op("pe", lambda e, bi=bi, k=k, oc=oc, s3=s3: e.matmul(
                                pb[s3 + bi][:, 0:n], lhsT=Wb[:, 4 * bi + k, oc * 128:(oc + 1) * 128], rhs=Y[a][:, 4 * bi + k, 0:n],
                                start=(k == 0), stop=(k == 3)), reads=[b_Wbs[oc // 2], b_Y[a]], writes=[b_pb[s3 + bi]], sig=(k == 3))
                    for bi in range(3):
                        fw.op("dve", lambda e, bi=bi, oc=oc, s3=s3, tset=tset: e.tensor_tensor(
                            out=tt[tset][bi][:, 0:n], in0=pb[s3 + bi][:, 0:n], in1=GT[a][:, bi * 8 + oc, 0:n], op=ALU.mult),
                            reads=[b_pb[s3 + bi], b_GT[a]], writes=[b_tt[tset][bi]])
                    fw.op("pool", lambda e, tset=tset: e.tensor_tensor(out=tt[tset][0][:, 0:n], in0=tt[tset][0][:, 0:n], in1=tt[tset][1][:, 0:n], op=ALU.add),
                          reads=[b_tt[tset][0], b_tt[tset][1]], writes=[b_tt[tset][0]])
                    fw.op("pool", lambda e, tset=tset, oc=oc: e.tensor_tensor(out=mT[a][:, oc, 0:n], in0=tt[tset][0][:, 0:n], in1=tt[tset][2][:, 0:n], op=ALU.add),
                          reads=[b_tt[tset][0], b_tt[tset][2]], writes=[b_mT[a]])
                for j in range(nt):
                    for hf_ in range(2):
                        q_ = cnt % 2
                        cnt += 1
                        for k in range(8):
                            fw.op("pe", lambda e, q_=q_, k=k, j=j, hf_=hf_: e.matmul(
                                pq[q_][:, :], lhsT=mT[a][:, k, j * 128:(j + 1) * 128], rhs=Wo[:, k, hf_ * 512:(hf_ + 1) * 512],
                                start=(k == 0), stop=(k == 7)), reads=[b_mT[a], b_Wo], writes=[b_pq[q_]], sig=(k == 7))
                        tq = tt[q_][0]
                        fw.op("dve", lambda e, q_=q_, hf_=hf_, tq=tq: e.tensor_tensor(out=tq[:], in0=pq[q_][:], in1=G[:, w, hf_ * 512:(hf_ + 1) * 512], op=ALU.mult),
                              reads=[b_pq[q_], b_G], writes=[b_tt[q_][0]])
                        fw.op("pool", lambda e, j=j, hf_=hf_, tq=tq: e.tensor_tensor(
                            out=X[a][:, j, hf_ * 512:(hf_ + 1) * 512], in0=X[a][:, j, hf_ * 512:(hf_ + 1) * 512], in1=tq[:], op=ALU.add),
                            reads=[b_X[a], b_tt[q_][0]], writes=[b_X[a]])
                fw.dma("pool", sc["XB"][t0:t0 + n, :].rearrange("(j p) f -> p j f", p=128), X[a][:, 0:nt, :], reads=[b_X[a]])

    def phase4a(self, l):
        nc, fw, S, T = self.nc, self.fw, self.S, self.T
        sc = self.s
        chunks = self.chunks if l < self.L - 1 else self.chunks[:-1]
        with ExitStack() as es:
            W = self.sb(es, "f1W", [128, 8, DFF], BF16); b_W = [Buf() for _ in range(8)]
            for og_ in range(8):
                fw.dma("pool", W[:, :, og_ * 512:(og_ + 1) * 512], self.i["w_ff1"][l, :, og_ * 512:(og_ + 1) * 512].rearrange("(k p) m -> p k m", p=128),
                       writes=[b_W[og_]])
            np_ = self._norm_pipe(es, l, 2, 3, sc["XB"], chunks, "f1")
            hT, b_hT, load_x, prep = np_["hT"], np_["b_hT"], np_["load_x"], np_["prep"]
            tmp = [self.sb(es, "f1t%d" % k, [128, 512], BF16) for k in range(2)]; b_tmp = [Buf(), Buf()]
            stg = [self.sb(es, "f1s%d" % k, [128, 4, 512], BF16) for k in range(3)]; b_stg = [Buf() for _ in range(3)]
            pac = [self.ps(es, "f1ac%d" % k, [128, 512]) for k in range(4)]; b_pac = [Buf(True) for _ in range(4)]
            load_x(0)
            prep(0)
            acc = 0
            sg = 0
            for ci, (t0, n, w) in enumerate(chunks):
                slot = ci % 2
                for og in range(8):
                    S_, BS_ = stg[sg % 3], b_stg[sg % 3]
                    sg += 1
                    if og == 4 and ci + 1 < len(chunks):
                        prep(ci + 1)
                    for cc in range(4):
                        oc = og * 4 + cc
                        a = acc % 4
                        acc += 1
                        for k in range(8):
                            fw.op("pe", lambda e, a=a, k=k, oc=oc: e.matmul(pac[a][:, 0:n], lhsT=W[:, k, oc * 128:(oc + 1) * 128],
                                                                            rhs=hT[slot][:, k, 0:n], start=(k == 0), stop=(k == 7)),
                                  reads=[b_W[og], b_hT[slot]], writes=[b_pac[a]], sig=(k == 7))
                        tq = cc % 2
                        if cc % 2 == 0:
                            fw.op("act", lambda e, a=a, tq=tq: e.activation(out=tmp[tq][:, 0:n], in_=pac[a][:, 0:n], func=AF.Relu),
                                  reads=[b_pac[a]], writes=[b_tmp[tq]])
                        else:
                            fw.op("dve", lambda e, a=a, tq=tq: e.tensor_scalar(out=tmp[tq][:, 0:n], in0=pac[a][:, 0:n], scalar1=0.0, scalar2=None,
                                                                              op0=ALU.max), reads=[b_pac[a]], writes=[b_tmp[tq]])
                        fw.op("pool", lambda e, tq=tq, cc=cc, S_=S_: e.tensor_tensor(out=S_[:, cc, 0:n], in0=tmp[tq][:, 0:n], in1=tmp[tq][:, 0:n], op=ALU.mult),
                              reads=[b_tmp[tq]], writes=[BS_])
                    fw.dma("pool", sc["AT"][og * 512:(og + 1) * 512, t0:t0 + n].rearrange("(c p) t -> p c t", p=128), S_[:, :, 0:n], reads=[BS_])

    def phase4b(self, l):
        nc, fw, S, T = self.nc, self.fw, self.S, self.T
        sc = self.s
        last = l == self.L - 1
        chunks = self.chunks if not last else self.chunks[:-1]
        with ExitStack() as es:
            W = self.sb(es, "f2W", [128, 32, D], BF16); b_W = [[Buf() for _ in range(4)] for _ in range(2)]
            for hf2 in range(2):
                for k4 in range(4):
                    fw.dma("pool", W[:, 8 * k4:8 * k4 + 8, hf2 * 512:(hf2 + 1) * 512],
                           self.i["w_ff2"][l, 1024 * k4:1024 * (k4 + 1), hf2 * 512:(hf2 + 1) * 512].rearrange("(k p) m -> p k m", p=128),
                           writes=[b_W[hf2][k4]])
            G, b_G = self._gate_tiles(es, l, 5, "f2")
            A = [self.sb(es, "f2A%d" % k, [128, 32, 512], BF16) for k in range(2)]; b_A = [Buf(), Buf()]
            X = [self.sb(es, "f2X%d" % k, [128, 4, D], F32) for k in range(2)]; b_X = [Buf(), Buf()]
            tq = [self.sb(es, "f2t%d" % k, [128, 512], F32) for k in range(2)]; b_tq = [Buf(), Buf()]
            pq = [self.ps(es, "f2pq%d" % k, [128, 512]) for k in range(4)]; b_pq = [Buf(True) for _ in range(4)]
            if last:
                FG = self.sb(es, "f2FG", [128, D], F32); b_FG = Buf()
                fw.dma("sp", FG[:], self.i["fgrow"].partition_broadcast(128), writes=[b_FG])
                junk = self.sb(es, "f2junk", [128, D], BF16); b_junk = Buf()
                ss = self.sb(es, "f2ss", [128, 4], F32); b_ss = Buf()
                eps_t = self.sb(es, "f2eps", [128, 1], F32); b_eps = Buf()
                fw.op("pool", lambda e: e.memset(eps_t[:], 1e-6), writes=[b_eps])

            def load(ci):
                t0, n, w = chunks[ci]
                a = ci % 2
                for k4 in range(4):
                    fw.dma("sp", A[a][:, 8 * k4:8 * k4 + 8, 0:n], sc["AT"][1024 * k4:1024 * (k4 + 1), t0:t0 + n].rearrange("(c p) t -> p c t", p=128),
                           writes=[b_A[a]])
                fw.dma("sp", X[a][:, 0:n // 128, :], sc["XB"][t0:t0 + n, :].rearrange("(j p) f -> p j f", p=128), writes=[b_X[a]])

            load(0)
            cnt = 0
            for ci, (t0, n, w) in enumerate(chunks):
                a = ci % 2
                nt = n // 128
                if ci + 1 < len(chunks):
                    load(ci + 1)
                for j in range(nt):
                    for hf_ in range(2):
                        q_ = cnt % 4
                        t_ = cnt % 2
                        cnt += 1
                        for k in range(32):
                            fw.op("pe", lambda e, q_=q_, k=k, j=j, hf_=hf_: e.matmul(
                                pq[q_][:, :], lhsT=A[a][:, k, j * 128:(j + 1) * 128], rhs=W[:, k, hf_ * 512:(hf_ + 1) * 512],
                                start=(k == 0), stop=(k == 31)), reads=[b_A[a], b_W[hf_][k // 8]], writes=[b_pq[q_]], sig=(k == 31))
                        fw.op("dve", lambda e, q_=q_, hf_=hf_, t_=t_: e.tensor_tensor(out=tq[t_][:], in0=pq[q_][:], in1=G[:, w, hf_ * 512:(hf_ + 1) * 512], op=ALU.mult),
                              reads=[b_pq[q_], b_G], writes=[b_tq[t_]])
                        fw.op("pool", lambda e, j=j, hf_=hf_, t_=t_: e.tensor_tensor(
                            out=X[a][:, j, hf_ * 512:(hf_ + 1) * 512], in0=X[a][:, j, hf_ * 512:(hf_ + 1) * 512], in1=tq[t_][:], op=ALU.add),
                            reads=[b_X[a], b_tq[t_]], writes=[b_X[a]])
                if not last:
                    fw.dma("pool", sc["XA"][t0:t0 + n, :].rearrange("(j p) f -> p j f", p=128), X[a][:, 0:nt, :], reads=[b_X[a]])
                else:
                    for j in range(nt):
                        fw.op("act", lambda e, j=j: e.activation(out=junk[:], in_=X[a][:, j, :], func=AF.Square, accum_out=ss[:, j:j + 1]),
                              reads=[b_X[a]], writes=[b_junk, b_ss])
                    fw.op("act", lambda e: e.activation(out=ss[:, 0:nt], in_=ss[:, 0:nt], func=AF.Sqrt, bias=eps_t[:], scale=1.0 / D),
                          reads=[b_ss, b_eps], writes=[b_ss])
                    fw.op("dve", lambda e: e.reciprocal(out=ss[:, 0:nt], in_=ss[:, 0:nt]), reads=[b_ss], writes=[b_ss])
                    for j in range(nt):
                        fw.op("dve", lambda e, j=j: e.scalar_tensor_tensor(out=X[a][:, j, :], in0=X[a][:, j, :], scalar=ss[:, j:j + 1], in1=FG[:],
                                                                          op0=ALU.mult, op1=ALU.mult), reads=[b_X[a], b_ss, b_FG], writes=[b_X[a]])
                    fw.dma("pool", self.out[t0:t0 + n, :].rearrange("(j p) f -> p j f", p=128), X[a][:, 0:nt, :], reads=[b_X[a]])


def _fm(v):
    v = np.asarray(v, np.float32)
    return np.ascontiguousarray(v.reshape(-1, 128).T)


def _na_tables(rpb, R):
    qs = [2, 0, 1, R // 2 - 2, R // 2 - 1]
    out = np.empty((5, 4, 128, 2, 5, 128), np.float32)
    part = np.arange(128)
    i2, jc = part // 64, part % 64
    qc = np.arange(128)
    r2, w = qc // 64, qc % 64
    for vi, q in enumerate(qs):
        kb = int(np.clip(2 * q - 4, 0, R - 10))
        r = 2 * q + r2
        rs = np.clip(r - 4, 0, R - 8)
        cs = np.clip(w - 8, 0, GW - 16)
        for j in range(5):
            kr = kb + 2 * j + i2
            valid = ((kr[:, None] >= rs[None]) & (kr[:, None] < rs[None] + 8)
                     & (jc[:, None] >= cs[None]) & (jc[:, None] < cs[None] + 16))
            dr = np.clip(kr[:, None] - r[None] + 7, 0, 14)
            dc = np.clip(jc[:, None] - w[None] + 15, 0, 30)
            for h in range(8):
                tab = np.where(valid, rpb[h][dr, dc], np.float32(MASKV))
                out[vi, h // 2, :, h % 2, j, :] = tab
    return out.reshape(5, 4, 128, 1280)


def host_prep(inp, S, L):
    B = inp["x"].shape[0]
    R = S // GW
    SL = S // 2
    f32 = np.float32
    t = np.arange(S)
    row = (t // GW).astype(f32)
    col = (t % GW).astype(f32)
    inv = (np.float32(10000.0) ** (-np.arange(16, dtype=f32) / np.float32(16))).astype(f32)
    ang = np.concatenate([row[:, None] * inv, col[:, None] * inv], axis=-1).astype(f32)
    cosv, sinv = np.cos(ang.astype(np.float64)), np.sin(ang.astype(np.float64))
    p = np.arange(128)
    ropec = np.ascontiguousarray(cosv[:, p % 32].T).astype(f32)
    sgn = np.where((p % 64) < 32, -1.0, 1.0)
    ropes = np.ascontiguousarray((sinv[:, p % 32] * sgn[None]).T).astype(f32)
    cm = np.zeros((128, NCM), f32)
    cm[:, CM_ID:CM_ID + 128] = np.eye(128, dtype=f32)
    partner = np.where((p % 64) < 32, p + 32, p - 32)
    cm[partner, CM_PERM + p] = 1.0
    kk, qq = np.meshgrid(np.arange(128), np.arange(128), indexing="ij")
    mprev = np.where(kk >= qq, 0.0, MASKV).astype(f32)
    mnext = np.where(kk <= qq, 0.0, MASKV).astype(f32)
    cm[:, CM_MPREV:CM_MPREV + 256] = np.tile(mprev, (1, 2))
    cm[:, CM_MNEXT:CM_MNEXT + 256] = np.tile(mnext, (1, 2))
    for k in range(4):
        cm[k, CM_ESEL:CM_ESEL + 64] = 1.0
        base = CM_SEL0 if k < 2 else CM_SEL1
        cm[k, base + (k % 2) * 128:base + (k % 2) * 128 + 128] = 1.0
    nab_all = [_na_tables(np.asarray(inp["na_rpb"][l], f32), R) for l in range(L)]
    nonce = int(np.random.default_rng().integers(1, 2 ** 30))
    ep = (nonce + np.arange(16)).astype(np.int32)[None]
    shared = dict(w_mod=np.ascontiguousarray(inp["w_mod"][:L], f32), w_in=np.ascontiguousarray(inp["w_in"][:L], f32),
                  w_branch=np.ascontiguousarray(inp["w_branch"][:L], f32), w_out=np.ascontiguousarray(inp["w_out"][:L], f32),
                  w_ff1=np.ascontiguousarray(inp["w_ff1"][:L], f32), w_ff2=np.ascontiguousarray(inp["w_ff2"][:L], f32),
                  fgv=_fm(inp["final_g"]), fgrow=np.ascontiguousarray(inp["final_g"], f32), cmat=cm, ep=ep)
    par_maps = []
    for par in range(2):
        pvec = np.zeros((128, L, NPV), f32)
        lruw = np.zeros((L, 2, 2, 4, 128, 128), f32)
        nab = np.zeros((L, 5, 4, 128, 1280), f32)
        for l in range(L):
            pvec[:, l, PV_BMOD:PV_BMOD + 48] = _fm(inp["b_mod"][l])
            pvec[:, l, PV_N1G:PV_N1G + 8] = _fm(inp["norm1_g"][l])
            pvec[:, l, PV_N2G:PV_N2G + 8] = _fm(inp["norm2_g"][l])
            chs = slice(2 * par, 2 * par + 2)
            for tap in range(4):
                pvec[:, l, PV_CW + tap:PV_CW + 8:4] = _fm(inp["conv_w"][l, tap])[:, chs]
            pvec[:, l, PV_CB:PV_CB + 2] = _fm(inp["conv_b"][l])[:, chs]
            for d in range(2):
                pvec[:, l, PV_BA + 4 * d:PV_BA + 4 * d + 2] = _fm(inp["lru_ba"][l, d])[:, chs]
                pvec[:, l, PV_BX + 4 * d:PV_BX + 4 * d + 2] = _fm(inp["lru_bx"][l, d])[:, chs]
                pvec[:, l, PV_LAM + 4 * d:PV_LAM + 4 * d + 4] = 8.0
                pvec[:, l, PV_LAM + 4 * d:PV_LAM + 4 * d + 2] = _fm(inp["lru_lambda"][l, d])[:, chs]
                for ax, nm in enumerate(("lru_wa", "lru_wx")):
                    for cl in range(2):
                        for hb in range(2):
                            lruw[l, d, ax, cl, hb * 64:(hb + 1) * 64, hb * 64:(hb + 1) * 64] = inp[nm][l, d, 2 * (2 * par + cl) + hb]
            pvec[0:4, l, PV_SINK] = inp["swa_sink"][l][4 * par:4 * par + 4]
            nab[l, :, 0:2] = nab_all[l][:, 2 * par:2 * par + 2]
        par_maps.append(dict(pvec=pvec.reshape(128, L * NPV), lruw=lruw, nabias=nab,
                             ropec=np.ascontiguousarray(ropec[:, par * SL:(par + 1) * SL]),
                             ropes=np.ascontiguousarray(ropes[:, par * SL:(par + 1) * SL])))
    maps = []
    for b in range(B):
        for par in range(2):
            m = dict(shared)
            m.update(par_maps[par])
            m["xin"] = np.ascontiguousarray(np.concatenate([inp["x"][b][par * SL:(par + 1) * SL],
                                                           inp["ctx"][b][par * 128:(par + 1) * 128]], axis=0), f32)
            cv = np.empty((128, 16), f32)
            cv[:, 0::2] = _fm(inp["c"][b])
            cv[:, 1::2] = _fm(inp["c_ctx"])
            m["cvec"] = cv
            maps.append(m)
    return maps


_CACHE = {}


def run(inp, S, L, dbg=()):
    key = (S, L, tuple(dbg))
    maps = host_prep(inp, S, L)
    prog = Prog(S, L, dbg)
    nc = prog.build()
    res = run_bass_kernel_spmd(nc, maps, core_ids=list(range(len(maps))))
    return res.results, prog


def kernel(**inputs):
    inp = {k: np.asarray(v) for k, v in inputs.items()}
    S = inp["x"].shape[1]
    L = inp["w_in"].shape[0]
    results, _ = run(inp, S, L)
    B = inp["x"].shape[0]
    return np.stack([np.concatenate([results[2 * b]["out"], results[2 * b + 1]["out"]], axis=0) for b in range(B)], axis=0).astype(np.float32)
```

```python
import numpy as np
from contextlib import ExitStack
import concourse.bass as bass
import concourse.mybir as mybir
from concourse.bass_utils import run_bass_kernel_spmd
from concourse.ap import AP

F32 = mybir.dt.float32
BF16 = mybir.dt.bfloat16
I32 = mybir.dt.int32
AF = mybir.ActivationFunctionType
ALU = mybir.AluOpType

D = 1024
CTX = 256
GW = 64
HD = 64
DFF = 4096
INW = 6400
MODW = 6144
MASKV = -30000.0
NPV = 112
PV_BMOD, PV_N1G, PV_N2G, PV_CW, PV_CB, PV_BA, PV_BX, PV_LAM, PV_SINK = 0, 48, 56, 64, 80, 84, 92, 100, 108
CM_ID, CM_PERM, CM_MPREV, CM_MNEXT, CM_ESEL, CM_SEL0, CM_SEL1 = 0, 128, 256, 512, 768, 896, 1152
NCM = 1408
OFF = dict(qa=0, ka=512, va=1024, xb=1536, gb=2048, qs=2560, ks=3072, vs=3200, ga=3328, gr=4352, gs=5376)


class Buf:
    __slots__ = ("w", "r", "excl")

    def __init__(self, excl=False):
        self.w = None
        self.r = []
        self.excl = excl


class FW:
    ENG = ("pe", "act", "dve", "pool", "sp")

    def __init__(self, nc, es, ndma=12):
        self.nc = nc
        self.e = {"pe": nc.tensor, "act": nc.scalar, "dve": nc.vector, "pool": nc.gpsimd, "sp": nc.sync}
        self.sem = {k: es.enter_context(nc.semaphore("s_" + k)) for k in self.ENG}
        self.cnt = {k: 0 for k in self.ENG}
        self.seen = {k: {} for k in self.ENG}
        self.pend = {k: [] for k in self.ENG}
        self.dq = {}
        for q in ("sp", "act", "pool"):
            self.dq[q] = {"sems": [es.enter_context(nc.semaphore("d_%s%d" % (q, i))) for i in range(ndma)],
                          "uses": [0] * ndma, "next": 0}
        self.nins = 0
        self.nwait = 0

    def _wait(self, eng, ev):
        key, val = ev
        if self.seen[eng].get(key, 0) >= val:
            return
        self.seen[eng][key] = val
        sem = self.sem[key] if isinstance(key, str) else self.dq[key[1]]["sems"][key[2]]
        self.e[eng].wait_ge(sem, val)
        self.nwait += 1

    def _deps(self, eng, reads, writes):
        for b in reads:
            if b.w is not None and (b.w[0] != eng or eng != "pe"):
                self._wait(eng, b.w)
            if b.excl:
                for ev in b.r:
                    if ev[0] != eng:
                        self._wait(eng, ev)
        for b in writes:
            if b.w is not None and b.w[0] != eng:
                self._wait(eng, b.w)
            for ev in b.r:
                if ev[0] != eng:
                    self._wait(eng, ev)

    def _reg(self, ev, reads, writes):
        for b in reads:
            if len(b.r) > 12:
                last = {}
                for k, v in b.r:
                    if last.get(k, 0) < v:
                        last[k] = v
                b.r = list(last.items())
            b.r.append(ev)
        for b in writes:
            b.w = ev
            b.r = []

    def op(self, eng, fn, reads=(), writes=(), sig=True):
        self._deps(eng, reads, writes)
        self.nins += 1
        ins = fn(self.e[eng])
        if sig:
            self.cnt[eng] += 1
            ev = (eng, self.cnt[eng])
            ins.then_inc(self.sem[eng], 1)
            for (r, w) in self.pend[eng]:
                self._reg(ev, r, w)
            self.pend[eng] = []
            self._reg(ev, reads, writes)
            return ev
        self.pend[eng].append((reads, writes))
        return None

    def dma(self, q, out, in_, reads=(), writes=()):
        d = self.dq[q]
        i = d["next"]
        d["next"] = (i + 1) % len(d["sems"])
        key = ("d", q, i)
        if d["uses"][i] > 0:
            self._wait(q, (key, 16 * d["uses"][i]))
        self._deps(q, reads, writes)
        d["uses"][i] += 1
        ev = (key, 16 * d["uses"][i])
        self.e[q].dma_start(out=out, in_=in_).then_inc(d["sems"][i], 16)
        self.nins += 1
        self._reg(ev, reads, writes)
        return ev

    def barrier(self):
        evs = [(k, self.cnt[k]) for k in self.ENG if self.cnt[k] > 0]
        for q, d in self.dq.items():
            for i, u in enumerate(d["uses"]):
                if u > 0:
                    evs.append((("d", q, i), 16 * u))
        for eng in self.ENG:
            for ev in evs:
                if ev[0] != eng:
                    self._wait(eng, ev)


class Prog:
    def __init__(self, S, depth, dbg=()):
        self.S = S
        self.T = S + CTX
        self.R = S // GW
        self.L = depth
        self.dbg = set(dbg)
        self.SL = S // 2
        self.TL = self.SL + 128
        self.chunks = [(c * 512, 512, 0) for c in range(self.SL // 512)] + [(self.SL, 128, 1)]
        self.nc = bass.Bass("TRN2", target_bir_lowering=False)
        self.par = {"sp": self.nc.sync.partition_id() % 2, "pool": self.nc.gpsimd.partition_id() % 2,
                    "act": self.nc.scalar.partition_id() % 2}

    def tmap(self, t0, n):
        S, SL = self.S, self.SL
        if t0 < S:
            assert (t0 % SL) + n <= SL
            return [(t0 // SL, t0 % SL, n, 0)]
        res, o = [], 0
        a, b = t0 - S, t0 - S + n
        for h in range(2):
            lo, hi = max(a, h * 128), min(b, (h + 1) * 128)
            if lo < hi:
                res.append((h, SL + lo - h * 128, hi - lo, lo - a))
        return res

    def ld_nat(self, q, tile_fn, loc, rows, writes):
        S, SL, TL = self.S, self.SL, self.TL
        for h in range(2):
            self.fw.dma(q, tile_fn(h * SL, SL), loc[rows, h, 0:SL], writes=writes)
            self.fw.dma(q, tile_fn(S + h * 128, 128), loc[rows, h, SL:TL], writes=writes)

    def din(self, name, shape, dt=F32):
        return self.nc.dram_tensor(name, list(shape), dt, kind="ExternalInput").ap()

    def dscr(self, name, shape, dt, shared=False):
        kind = "ExternalOutput" if name in self.dbg else "Internal"
        if shared and kind == "Internal":
            return self.nc.dram_tensor(name, list(shape), dt, kind=kind, addr_space="Shared").ap()
        return self.nc.dram_tensor(name, list(shape), dt, kind=kind).ap()

    def _uniq(self, name):
        self._nid = getattr(self, "_nid", 0) + 1
        return "%s_%d" % (name, self._nid)

    def sb(self, es, name, shape, dt):
        t = es.enter_context(self.nc.sbuf_tensor(self._uniq(name), list(shape), dt))
        return t

    def dump(self, name, ap_, reads):
        if "dump" not in self.dbg:
            return
        shp = [a[1] for a in ap_.ap]
        d = self.nc.dram_tensor("dbg_" + name, shp, ap_.dtype, kind="ExternalOutput").ap()
        self.fw.dma("sp", d, ap_, reads=reads)

    def ps(self, es, name, shape, dt=F32):
        return es.enter_context(self.nc.psum_tensor(self._uniq(name), list(shape), dt))

    def build(self):
        nc, S, T, L = self.nc, self.S, self.T, self.L
        i = self.i = {}
        TL, SL = self.TL, self.SL
        i["xin"] = self.din("xin", [TL, D])
        i["ep"] = self.din("ep", [1, 16], I32)
        i["cvec"] = self.din("cvec", [128, 16])
        Lh = self.Lh = (L + 1) // 2
        i["w_mod"] = self.din("w_mod", [Lh, D, MODW])
        i["pvec0"] = self.din("pvec0", [128, Lh * 64])
        i["w_in"] = self.din("w_in", [L, D, INW])
        i["w_branch"] = self.din("w_branch", [L, 1536, D])
        i["w_out"] = self.din("w_out", [L, D, D])
        i["w_ff1"] = self.din("w_ff1", [L, D, DFF])
        i["w_ff2"] = self.din("w_ff2", [L, DFF, D])
        i["pvec"] = self.din("pvec", [128, L * NPV])
        i["fgv"] = self.din("fgv", [128, 8])
        i["lruw"] = self.din("lruw", [L, 2, 2, 4, 128, 128])
        i["nabias"] = self.din("nabias", [L, 5, 2, 128, 1280])
        i["ropec"] = self.din("ropec", [128, SL])
        i["ropes"] = self.din("ropes", [128, SL])
        i["cmat"] = self.din("cmat", [128, NCM])
        i["fgrow"] = self.din("fgrow", [D])
        self.out = nc.dram_tensor("out", [SL, D], F32, kind="ExternalOutput").ap()
        s = self.s = {}
        m = self.m = {}
        sh = self.sh = {}
        s["XA"] = self.dscr("XA", [TL, D], F32)
        s["XB"] = self.dscr("XB", [TL, D], F32)
        FM_l = self.dscr("FM_l", [2, 1088, TL], BF16)
        FM_m = self.dscr("FM_m", [1088, 2, TL], BF16)
        sh["FM"] = self.dscr("FM_s", [2, 1088, TL], BF16, shared=True)
        r0 = 0
        for n, f in (("QA", 256), ("KA", 256), ("GB", 256), ("QS", 256), ("KS", 64)):
            s[n] = FM_l[:, r0:r0 + f, :]
            m[n] = FM_m[r0:r0 + f]
            r0 += f
        s["XBC"] = self.dscr("XBC", [2, 256, TL], F32)
        m["XBC"] = self.dscr("XBC_m", [256, 2, TL], F32)
        sh["XBC"] = self.dscr("XBC_s", [2, 256, TL], F32, shared=True)
        V_l = self.dscr("V_l", [TL, 2, 320], BF16)
        V_m = self.dscr("V_m", [2, TL, 320], BF16)
        sh["V"] = self.dscr("V_s", [2, TL, 320], BF16, shared=True)
        s["VA"], s["VS"] = V_l[:, :, 0:256], V_l[:, :, 256:320]
        m["VA"], m["VS"] = V_m[:, :, 0:256], V_m[:, :, 256:320]
        Y_m = self.dscr("Y_m", [768, 2, TL], BF16)
        Y_t = self.dscr("Y_t", [2, 768, TL], BF16)
        sh["Y"] = self.dscr("Y_s", [2, 768, 2, TL], BF16, shared=True)
        for k_, n in enumerate(("YA", "YB", "YC")):
            m[n] = Y_m[k_ * 256:(k_ + 1) * 256]
            s[n] = Y_t[:, k_ * 256:(k_ + 1) * 256, :]
        self.bund = dict(FM_l=FM_l, FM_m=FM_m, V_l=V_l, V_m=V_m, Y_m=Y_m, Y_t=Y_t)
        s["GT"] = self.dscr("GT", [3072, TL], BF16)
        s["AT"] = self.dscr("AT", [DFF, TL], BF16)
        s["MODROW"] = self.dscr("MODROW", [2, Lh, 96, 128], F32, shared=True)
        s["MODP"] = self.dscr("MODP", [2, 128, Lh * 64], F32, shared=True)
        s["FLAG"] = nc.dram_tensor("FLAG", [2, 16], I32, kind="Internal", addr_space="Shared").ap()
        with ExitStack() as es:
            self.fw = FW(nc, es)
            self.ft = self.sb(es, "ft", [1, 32], I32)
            self.fsem = es.enter_context(nc.semaphore("fsem"))
            self.psem = es.enter_context(nc.semaphore("psem"))
            nc.sync.dma_start(out=self.ft[0:1, 0:16], in_=i["ep"]).then_inc(self.fsem, 16)
            nc.sync.wait_ge(self.fsem, 16)
            self.fn = 16
            self.pv = self.sb(es, "pv", [128, L * NPV], F32)
            self.modp = self.sb(es, "modp", [128, L, 4, 2, 8], F32)
            self.cm = self.sb(es, "cm", [128, NCM], F32)
            self.cmb = self.sb(es, "cmb", [128, NCM], BF16)
            self.b_pv, self.b_modp, self.b_cm, self.b_cmb = Buf(), Buf(), Buf(), Buf()
            fw = self.fw
            fw.dma("sp", self.pv[:], i["pvec"], writes=[self.b_pv])
            fw.dma("sp", self.cm[:], i["cmat"], writes=[self.b_cm])
            fw.op("dve", lambda e: e.tensor_copy(out=self.cmb[:], in_=self.cm[:]), reads=[self.b_cm], writes=[self.b_cmb])
            self.phase0()
            fw.barrier()
            for l in range(L):
                self.phase1(l)
                self.exchange1(l)
                if "stop1" in self.dbg:
                    break
                self.phase_na(l)
                fw.barrier()
                g1, g2 = self.phase_lru(l), self.phase_swa(l)
                live = [g1, g2]
                while live:
                    for g_, reps in ((g1, 1), (g2, 2)):
                        for _ in range(reps):
                            if g_ in live:
                                try:
                                    next(g_)
                                except StopIteration:
                                    live.remove(g_)
                self.exchange2(l)
                if "stop2c" in self.dbg:
                    break
                self.phase3(l)
                fw.barrier()
                if "stop3" in self.dbg:
                    break
                self.phase4a(l)
                fw.barrier()
                self.phase4b(l)
                fw.barrier()
            fw.barrier()
        return nc

    def pair_barrier(self, slot):
        nc, fw = self.nc, self.fw
        fw.barrier()
        q = ("sp", "act")[slot % 2]
        e = fw.e[q]
        flag = self.s["FLAG"]
        fs = slot % 2
        e.dma_start(out=flag[bass.ts(self.par[q], 1), fs:fs + 1], in_=self.ft[0:1, slot:slot + 1]).then_inc(self.fsem, 16)
        self.fn += 16
        e.wait_ge(self.fsem, self.fn)
        ft, psem, partner = self.ft, self.psem, 1 - self.par[q]

        def body(e2):
            with e2.register("rp%d" % slot) as rp, e2.register("rd%d" % slot) as rd, e2.register("rw%d" % slot) as rw:
                e2.load(rw, ft[0:1, slot:slot + 1])

                def cond():
                    e2.sem_clear(psem)
                    e2.dma_start(out=ft[0:1, 16:17], in_=flag[bass.ts(partner, 1), fs:fs + 1]).then_inc(psem, 16)
                    e2.wait_ge(psem, 16)
                    e2.load(rp, ft[0:1, 16:17])
                    e2.reg_sub(rd, rp, rw)
                    return rd
                with e2.While(cond):
                    e2.nop()

        with nc.Block() as block:
            {"sp": block.sync, "act": block.scalar, "pool": block.gpsimd}[q](body)

    def exchange1(self, l):
        fw, s, m, sh, bd = self.fw, self.s, self.m, self.sh, self.bund
        fw.barrier()
        pp = self.par["pool"]
        me, ot = bass.ts(pp, 1), bass.ts(1 - pp, 1)
        fw.dma("pool", sh["FM"][me].rearrange("o f t -> (o f) t"), bd["FM_l"][ot].rearrange("o f t -> (o f) t"))
        fw.dma("pool", sh["XBC"][me].rearrange("o f t -> (o f) t"), s["XBC"][ot].rearrange("o f t -> (o f) t"))
        fw.dma("pool", sh["V"][me].rearrange("o t f -> (o t) f"), bd["V_l"][:, ot, :].rearrange("t o f -> t (o f)"))
        fw.dma("pool", bd["FM_m"][:, me, :].rearrange("f o t -> f (o t)"), bd["FM_l"][me].rearrange("o f t -> (o f) t"))
        fw.dma("pool", m["XBC"][:, me, :].rearrange("f o t -> f (o t)"), s["XBC"][me].rearrange("o f t -> (o f) t"))
        fw.dma("pool", bd["V_m"][me].rearrange("o t f -> (o t) f"), bd["V_l"][:, me, :].rearrange("t o f -> t (o f)"))
        self.pair_barrier(2 * l)
        ps_ = self.par["sp"]
        ot2 = bass.ts(1 - ps_, 1)
        fw.dma("sp", bd["FM_m"][:, ot2, :].rearrange("f o t -> f (o t)"), sh["FM"][ot2].rearrange("o f t -> (o f) t"))
        fw.dma("sp", m["XBC"][:, ot2, :].rearrange("f o t -> f (o t)"), sh["XBC"][ot2].rearrange("o f t -> (o f) t"))
        fw.dma("sp", bd["V_m"][ot2].rearrange("o t f -> (o t) f"), sh["V"][ot2].rearrange("o t f -> (o t) f"))
        fw.barrier()

    def exchange2(self, l):
        fw, s, m, sh, bd = self.fw, self.s, self.m, self.sh, self.bund
        fw.barrier()
        pa = self.par["act"]
        fw.dma("act", sh["Y"][bass.ts(pa, 1)].rearrange("o f a t -> (o f) a t"), bd["Y_m"])
        self.pair_barrier(2 * l + 1)
        for fh in range(2):
            fw.dma("act", bd["Y_t"][fh], sh["Y"][fh, :, bass.ts(pa, 1), :].rearrange("f o t -> f (o t)"))
        fw.barrier()

    def phase0(self):
        nc, fw, L, Lh = self.nc, self.fw, self.L, self.Lh
        with ExitStack() as es:
            cv = self.sb(es, "cv", [128, 16], F32)
            sc = self.sb(es, "sc", [128, 16], F32)
            pv0 = self.sb(es, "pv0", [128, Lh * 64], F32)
            wm = [self.sb(es, "wm%d" % k, [128, 8, 512], F32) for k in range(2)]
            mfm = self.sb(es, "mfm", [128, 2, 48], F32)
            mrow = self.sb(es, "mrow", [96, Lh, 128], F32)
            mp = self.sb(es, "mpl", [128, Lh, 4, 2, 8], F32)
            pmod = self.ps(es, "pmod", [128, 48, 2])
            ptr = self.ps(es, "ptr", [96, 128])
            b_cv, b_sc, b_mfm, b_mrow, b_pv0, b_mp = Buf(), Buf(), Buf(), Buf(), Buf(), Buf()
            b_wm = [Buf(), Buf()]
            b_pmod, b_ptr = Buf(True), Buf(True)
            fw.dma("sp", cv[:], self.i["cvec"], writes=[b_cv])
            fw.dma("sp", pv0[:], self.i["pvec0"], writes=[b_pv0])
            fw.op("act", lambda e: e.activation(out=sc[:], in_=cv[:], func=AF.Silu), reads=[b_cv], writes=[b_sc])
            for li in range(Lh):
                for g in range(12):
                    w_, bw = wm[g % 2], b_wm[g % 2]
                    fw.dma("sp", w_[:], self.i["w_mod"][li, :, g * 512:(g + 1) * 512].rearrange("(k p) m -> p k m", p=128),
                           writes=[bw])
                    for jj in range(4):
                        j = g * 4 + jj
                        for k in range(8):
                            fw.op("pe", lambda e, w_=w_, jj=jj, k=k, j=j: e.matmul(
                                pmod[:, j, :], lhsT=w_[:, k, jj * 128:(jj + 1) * 128], rhs=sc[:, 2 * k:2 * k + 2],
                                start=(k == 0), stop=(k == 7)),
                                reads=[bw, b_sc], writes=[b_pmod], sig=(k == 7))
                pvl = pv0[:, li * 64:(li + 1) * 64]
                for w in range(2):
                    fw.op("dve", lambda e, w=w, pvl=pvl: e.tensor_tensor(
                        out=mfm[:, w, :], in0=pmod[:, :, w], in1=pvl[:, 0:48], op=ALU.add),
                        reads=[b_pmod, b_pv0], writes=[b_mfm])
                for w in range(2):
                    fw.op("dve", lambda e, w=w, pvl=pvl, li=li: e.scalar_tensor_tensor(
                        out=mp[:, li, 0, w, :], in0=mfm[:, w, 8:16], scalar=1.0, in1=pvl[:, 48:56],
                        op0=ALU.add, op1=ALU.mult), reads=[b_mfm, b_pv0], writes=[b_mp])
                    fw.op("dve", lambda e, w=w, li=li: e.tensor_copy(out=mp[:, li, 1, w, :], in_=mfm[:, w, 0:8]),
                          reads=[b_mfm], writes=[b_mp])
                    fw.op("dve", lambda e, w=w, pvl=pvl, li=li: e.scalar_tensor_tensor(
                        out=mp[:, li, 2, w, :], in0=mfm[:, w, 32:40], scalar=1.0, in1=pvl[:, 56:64],
                        op0=ALU.add, op1=ALU.mult), reads=[b_mfm, b_pv0], writes=[b_mp])
                    fw.op("dve", lambda e, w=w, li=li: e.tensor_copy(out=mp[:, li, 3, w, :], in_=mfm[:, w, 24:32]),
                          reads=[b_mfm], writes=[b_mp])
                fw.op("pe", lambda e: e.transpose(ptr[:], mfm[:].rearrange("p w j -> p (w j)"), self.cm[:, CM_ID:CM_ID + 128]),
                      reads=[b_mfm, self.b_cm], writes=[b_ptr])
                fw.op("act", lambda e, li=li: e.activation(out=mrow[:, li, :], in_=ptr[:], func=AF.Copy), reads=[b_ptr], writes=[b_mrow])
            pp = self.par["pool"]
            fw.dma("pool", self.s["MODROW"][bass.ts(pp, 1)].rearrange("o l r c -> r (o l) c"), mrow[:], reads=[b_mrow])
            fw.dma("pool", self.s["MODP"][bass.ts(pp, 1)].rearrange("o p c -> p (o c)"), mp[:].rearrange("p l a b c -> p (l a b c)"), reads=[b_mp])
            self.pair_barrier(9)
            for l in range(L):
                fw.dma("sp", self.modp[:, l].rearrange("p a b c -> p (a b c)"), self.s["MODP"][l % 2, :, (l // 2) * 64:(l // 2 + 1) * 64],
                       writes=[self.b_modp])
            fw.barrier()

    def _norm_pipe(self, es, l, kA, kS, src, chunks, tag, keep_x=False):
        fw = self.fw
        nx = 2 if keep_x else 1
        xt = [self.sb(es, tag + "xt%d" % k, [128, 4, D], F32) for k in range(nx)]; b_xt = [Buf() for _ in range(nx)]
        junk = self.sb(es, tag + "junk", [128, D], BF16); b_junk = Buf()
        ss = self.sb(es, tag + "ss", [128, 4], F32); b_ss = Buf()
        rstd = self.sb(es, tag + "rstd", [128, 4], F32); b_rstd = Buf()
        xn = self.sb(es, tag + "xn", [128, 4, D], BF16); b_xn = Buf()
        hT = [self.sb(es, tag + "hT%d" % k, [128, 8, 512], BF16) for k in range(2)]; b_hT = [Buf(), Buf()]
        ptr = [self.ps(es, tag + "tr%d" % k, [128, 1024], BF16) for k in range(2)]; b_ptr = [Buf(True), Buf(True)]
        eps_t = self.sb(es, tag + "eps", [128, 1], F32); b_eps = Buf()
        fw.op("pool", lambda e: e.memset(eps_t[:], 1e-6), writes=[b_eps])
        ident = self.cmb[:, CM_ID:CM_ID + 128]
        st = {"tr": 0}

        def load_x(ci):
            t0, n, w = chunks[ci]
            nt = n // 128
            fw.dma("sp", xt[ci % nx][:, 0:nt, :], src[t0:t0 + n, :].rearrange("(j p) f -> p j f", p=128), writes=[b_xt[ci % nx]])

        def prep(ci):
            t0, n, w = chunks[ci]
            nt = n // 128
            slot = ci % 2
            X, BX = xt[ci % nx], b_xt[ci % nx]
            for j in range(nt):
                fw.op("act", lambda e, j=j: e.activation(out=junk[:], in_=X[:, j, :], func=AF.Square, accum_out=ss[:, j:j + 1]),
                      reads=[BX], writes=[b_junk, b_ss])
            fw.op("act", lambda e: e.activation(out=rstd[:, 0:nt], in_=ss[:, 0:nt], func=AF.Sqrt, bias=eps_t[:], scale=1.0 / D),
                  reads=[b_ss, b_eps], writes=[b_rstd])
            fw.op("dve", lambda e: e.reciprocal(out=rstd[:, 0:nt], in_=rstd[:, 0:nt]), reads=[b_rstd], writes=[b_rstd])
            for j in range(nt):
                fw.op("dve", lambda e, j=j: e.tensor_scalar(out=xn[:, j, :], in0=X[:, j, :], scalar1=rstd[:, j:j + 1],
                                                           scalar2=None, op0=ALU.mult), reads=[BX, b_rstd], writes=[b_xn])
            if ci + 1 < len(chunks):
                load_x(ci + 1)
            for k in range(8):
                p = st["tr"] % 2
                st["tr"] += 1
                for j in range(nt):
                    fw.op("pe", lambda e, p=p, j=j, k=k: e.transpose(ptr[p][:, j * 128:(j + 1) * 128],
                                                                     xn[:, j, k * 128:(k + 1) * 128], ident),
                          reads=[b_xn, self.b_cmb], writes=[b_ptr[p]], sig=(j == nt - 1))
                fw.op("act", lambda e, p=p, k=k: e.activation(
                    out=hT[slot][:, k, 0:n], in_=ptr[p][:, 0:n], func=AF.Identity,
                    scale=self.modp[:, l, kA, w, k:k + 1], bias=self.modp[:, l, kS, w, k:k + 1]),
                    reads=[b_ptr[p], self.b_modp], writes=[b_hT[slot]])

        return dict(hT=hT, b_hT=b_hT, load_x=load_x, prep=prep, xt=xt, b_xt=b_xt)

    def phase1(self, l):
        nc, fw, S, T = self.nc, self.fw, self.S, self.T
        src = self.i["xin"] if l == 0 else self.s["XA"]
        sc = self.s
        with ExitStack() as es:
            W = self.sb(es, "W1", [128, 8, INW], BF16)
            wblocks = [(OFF["qs"], 640), (OFF["qa"], 512), (OFF["ka"], 512), (OFF["xb"], 512), (OFF["gb"], 512),
                       (OFF["va"], 512), (OFF["vs"], 128)] + [(OFF["ga"] + g_ * 512, 512) for g_ in range(6)]
            b_Wb = {}
            for (c0_, cw_) in wblocks:
                bb_ = Buf()
                fw.dma("pool", W[:, :, c0_:c0_ + cw_], self.i["w_in"][l, :, c0_:c0_ + cw_].rearrange("(k p) m -> p k m", p=128), writes=[bb_])
                for c_ in range(c0_, c0_ + cw_, 128):
                    b_Wb[c_] = bb_

            class _BW:
                cur = None
            b_W = None
            np_ = self._norm_pipe(es, l, 0, 1, src, self.chunks, "p1")
            hT, b_hT, load_x, prep = np_["hT"], np_["b_hT"], np_["load_x"], np_["prep"]
            stf = [self.sb(es, "stf%d" % k, [128, 4, 512], BF16) for k in range(3)]; b_stf = [Buf(), Buf(), Buf()]
            stx = self.sb(es, "stx", [128, 4, 512], F32); b_stx = Buf()
            stv = [self.sb(es, "stv%d" % k, [128, 4, 512], BF16) for k in range(2)]; b_stv = [Buf(), Buf()]
            stvs = [self.sb(es, "stvs%d" % k, [128, 4, 128], BF16) for k in range(2)]; b_stvs = [Buf(), Buf()]
            qb = self.sb(es, "qb", [128, 5, 512], BF16); b_qb = [Buf() for _ in range(5)]
            cs = self.sb(es, "cs", [128, 2, 512], F32); b_cs = Buf()
            t1 = self.sb(es, "t1", [128, 512], F32); b_t1 = Buf()
            t2 = self.sb(es, "t2", [128, 512], F32); b_t2 = Buf()
            pac = [self.ps(es, "p1ac%d" % k, [128, 512]) for k in range(4)]; b_pac = [Buf(True) for _ in range(4)]
            prp = self.ps(es, "p1rp", [128, 512]); b_prp = Buf(True)
            ident = self.cmb[:, CM_ID:CM_ID + 128]
            perm = self.cmb[:, CM_PERM:CM_PERM + 128]
            st = {"acc": 0, "stf": 0, "stv": 0}
            chunks = self.chunks

            def acc_fm(ci, col0):
                t0, n, w = chunks[ci]
                slot = ci % 2
                a = st["acc"] % 4
                st["acc"] += 1
                for k in range(8):
                    fw.op("pe", lambda e, a=a, k=k: e.matmul(pac[a][:, 0:n], lhsT=W[:, k, col0:col0 + 128],
                                                             rhs=hT[slot][:, k, 0:n], start=(k == 0), stop=(k == 7)),
                          reads=[b_Wb[col0], b_hT[slot]], writes=[b_pac[a]], sig=(k == 7))
                return pac[a], b_pac[a]

            def fm_group(ci, col0, nch, dst, func=AF.Copy, scale=1.0, f32=False, fh3=True):
                t0, n, w = chunks[ci]
                if f32:
                    stg, bst = stx, b_stx
                else:
                    k_ = st["stf"] % 3
                    st["stf"] += 1
                    stg, bst = stf[k_], b_stf[k_]
                for cc in range(nch):
                    p, bp = acc_fm(ci, col0 + cc * 128)
                    eng = "act" if (func != AF.Copy or cc % 2 == 0) else "dve"
                    if eng == "act":
                        fw.op("act", lambda e, p=p, cc=cc: e.activation(out=stg[:, cc, 0:n], in_=p[:, 0:n], func=func, scale=scale),
                              reads=[bp], writes=[bst])
                    else:
                        fw.op("dve", lambda e, p=p, cc=cc: e.tensor_scalar(out=stg[:, cc, 0:n], in0=p[:, 0:n], scalar1=scale,
                                                                          scalar2=None, op0=ALU.mult),
                              reads=[bp], writes=[bst])
                if fh3:
                    for fh_ in range(2):
                        fw.dma("pool", dst[fh_, :, t0:t0 + n].rearrange("(c p) t -> p c t", p=128), stg[:, 2 * fh_:2 * fh_ + 2, 0:n], reads=[bst])
                else:
                    fw.dma("pool", dst[:, t0:t0 + n].rearrange("(c p) t -> p c t", p=128), stg[:, 0:nch, 0:n], reads=[bst])

            def rope_group(ci):
                t0, n, w = chunks[ci]
                k_ = st["stf"] % 3
                st["stf"] += 1
                stg, bst = stf[k_], b_stf[k_]
                k2 = st["stf"] % 3
                st["stf"] += 1
                stg2, bst2 = stf[k2], b_stf[k2]
                if w == 0:
                    fw.dma("sp", cs[:, 0, :], self.i["ropec"][:, t0:t0 + n], writes=[b_cs])
                    fw.dma("sp", cs[:, 1, :], self.i["ropes"][:, t0:t0 + n], writes=[b_cs])
                for cc in range(5):
                    col, sc_ = (OFF["qs"] + cc * 128, 0.125) if cc < 4 else (OFF["ks"], 1.0)
                    p, bp = acc_fm(ci, col)
                    o_t, o_b, o_c = (stg, bst, cc) if cc < 4 else (stg2, bst2, 0)
                    if w == 1:
                        fw.op("act", lambda e, p=p, o_t=o_t, o_c=o_c, sc_=sc_: e.activation(
                            out=o_t[:, o_c, 0:n], in_=p[:, 0:n], func=AF.Copy, scale=sc_), reads=[bp], writes=[o_b])
                    else:
                        fw.op("act", lambda e, p=p, cc=cc, sc_=sc_: e.activation(out=qb[:, cc, 0:n], in_=p[:, 0:n], func=AF.Copy, scale=sc_),
                              reads=[bp], writes=[b_qb[cc]])
                if w == 0:
                    for cc in range(5):
                        o_t, o_b, o_c = (stg, bst, cc) if cc < 4 else (stg2, bst2, 0)
                        rot(cc, o_t, o_b, o_c, n)
                for fh_ in range(2):
                    fw.dma("pool", sc["QS"][fh_, :, t0:t0 + n].rearrange("(c p) t -> p c t", p=128), stg[:, 2 * fh_:2 * fh_ + 2, 0:n], reads=[bst])
                for fh_ in range(2):
                    fw.dma("pool", sc["KS"][fh_, :, t0:t0 + n], stg2[fh_ * 64:(fh_ + 1) * 64, 0, 0:n], reads=[bst2])

            def rot(q_, ot, ob, oc, n):
                fw.op("pe", lambda e: e.matmul(prp[:, 0:n], lhsT=perm, rhs=qb[:, q_, 0:n], start=True, stop=True),
                      reads=[b_qb[q_], self.b_cmb], writes=[b_prp])
                fw.op("dve", lambda e: e.tensor_tensor(out=t1[:, 0:n], in0=qb[:, q_, 0:n], in1=cs[:, 0, 0:n], op=ALU.mult),
                      reads=[b_qb[q_], b_cs], writes=[b_t1])
                fw.op("dve", lambda e: e.tensor_tensor(out=t2[:, 0:n], in0=prp[:, 0:n], in1=cs[:, 1, 0:n], op=ALU.mult),
                      reads=[b_prp, b_cs], writes=[b_t2])
                fw.op("pool", lambda e: e.tensor_tensor(out=ot[:, oc, 0:n], in0=t1[:, 0:n], in1=t2[:, 0:n], op=ALU.add),
                      reads=[b_t1, b_t2], writes=[ob])

            def tm_group(ci, name, ncol, dst, stgs, bsts):
                t0, n, w = chunks[ci]
                slot = ci % 2
                nt = n // 128
                k_ = st["stv"] % 2
                st["stv"] += 1
                stg, bst = stgs[k_], bsts[k_]
                for j in range(nt):
                    a = st["acc"] % 4
                    st["acc"] += 1
                    for k in range(8):
                        fw.op("pe", lambda e, a=a, k=k, j=j: e.matmul(
                            pac[a][:, 0:ncol], lhsT=hT[slot][:, k, j * 128:(j + 1) * 128],
                            rhs=W[:, k, OFF[name]:OFF[name] + ncol], start=(k == 0), stop=(k == 7)),
                            reads=[b_Wb[OFF[name]], b_hT[slot]], writes=[b_pac[a]], sig=(k == 7))
                    if j % 2 == 0:
                        fw.op("act", lambda e, a=a, j=j: e.activation(out=stg[:, j, :], in_=pac[a][:, 0:ncol], func=AF.Copy),
                              reads=[b_pac[a]], writes=[bst])
                    else:
                        fw.op("dve", lambda e, a=a, j=j: e.tensor_copy(out=stg[:, j, :], in_=pac[a][:, 0:ncol]),
                              reads=[b_pac[a]], writes=[bst])
                hw_ = ncol // 2
                for a_ in range(2):
                    fw.dma("pool", dst[t0:t0 + n, a_, :].rearrange("(j p) f -> p j f", p=128), stg[:, 0:nt, a_ * hw_:(a_ + 1) * hw_], reads=[bst])

            load_x(0)
            prep(0)
            gates = [(gi, half) for gi in range(3) for half in range(2)]
            for ci in range(len(chunks)):
                rope_group(ci)
                fm_group(ci, OFF["qa"], 4, sc["QA"], scale=0.125)
                fm_group(ci, OFF["ka"], 4, sc["KA"])
                fm_group(ci, OFF["xb"], 4, sc["XBC"], f32=True)
                fm_group(ci, OFF["gb"], 4, sc["GB"])
                tm_group(ci, "va", 512, sc["VA"], stv, b_stv)
                tm_group(ci, "vs", 128, sc["VS"], stvs, b_stvs)
                for gn, (gi, half) in enumerate(gates):
                    if gn == 2 and ci + 1 < len(chunks):
                        prep(ci + 1)
                    base = gi * 1024 + half * 512
                    fm_group(ci, OFF["ga"] + base, 4, sc["GT"][base:base + 512], func=AF.Sigmoid, fh3=False)

    def phase_lru(self, l):
        nc, fw, S, T = self.nc, self.fw, self.S, self.T
        sc = self.s
        pvl = self.pv[:, l * NPV:(l + 1) * NPV]

        def rev(ap_):
            n = ap_.ap[-1][1]
            return AP(ap_.tensor, ap_.offset + n - 1, [list(ap_.ap[0]), [-1, n]])

        with ExitStack() as es:
            wbd = self.sb(es, "wbd", [128, 16, 128], BF16); b_wbd = Buf()
            fw.dma("pool", wbd[:], self.i["lruw"][l].rearrange("d a c k m -> k (d a c) m"), writes=[b_wbd])
            sm = self.sb(es, "lsm", [128, 5, 8], F32); b_sm = Buf()
            fw.op("act", lambda e: e.activation(out=sm[:, 4, :], in_=pvl[:, PV_LAM:PV_LAM + 8], func=AF.Exp, scale=-1.0),
                  reads=[self.b_pv], writes=[b_sm])
            sq = self.sb(es, "lsq", [128, 3, 8], F32); b_sq = Buf()
            fw.op("dve", lambda e: e.tensor_scalar(out=sq[:, 0, :], in0=sm[:, 4, :], scalar1=2.0, scalar2=None, op0=ALU.add),
                  reads=[b_sm], writes=[b_sq])
            fw.op("dve", lambda e: e.reciprocal(out=sq[:, 0, :], in_=sq[:, 0, :]), reads=[b_sq], writes=[b_sq])
            fw.op("dve", lambda e: e.tensor_tensor(out=sq[:, 0, :], in0=sq[:, 0, :], in1=sm[:, 4, :], op=ALU.mult),
                  reads=[b_sq, b_sm], writes=[b_sq])
            fw.op("dve", lambda e: e.tensor_tensor(out=sq[:, 1, :], in0=sq[:, 0, :], in1=sq[:, 0, :], op=ALU.mult),
                  reads=[b_sq], writes=[b_sq])
            fw.op("dve", lambda e: e.tensor_scalar(out=sq[:, 2, :], in0=sq[:, 1, :], scalar1=0.2, scalar2=1.0 / 3.0,
                                                   op0=ALU.mult, op1=ALU.add), reads=[b_sq], writes=[b_sq])
            fw.op("dve", lambda e: e.tensor_tensor(out=sq[:, 2, :], in0=sq[:, 2, :], in1=sq[:, 1, :], op=ALU.mult),
                  reads=[b_sq], writes=[b_sq])
            fw.op("dve", lambda e: e.scalar_tensor_tensor(out=sm[:, 4, :], in0=sq[:, 2, :], scalar=1.0, in1=sq[:, 0, :],
                                                          op0=ALU.add, op1=ALU.mult), reads=[b_sq], writes=[b_sm])
            fw.op("dve", lambda e: e.tensor_scalar(out=sm[:, 0, :], in0=sm[:, 4, :], scalar1=-8.0, scalar2=None, op0=ALU.mult),
                  reads=[b_sm], writes=[b_sm])
            fw.op("dve", lambda e: e.tensor_scalar(out=sm[:, 1, :], in0=sm[:, 4, :], scalar1=-16.0, scalar2=None, op0=ALU.mult),
                  reads=[b_sm], writes=[b_sm])
            fw.op("dve", lambda e: e.tensor_scalar(out=sm[:, 2, :], in0=pvl[:, PV_BA:PV_BA + 8], scalar1=0.5, scalar2=None, op0=ALU.mult),
                  reads=[self.b_pv], writes=[b_sm])
            fw.op("dve", lambda e: e.tensor_scalar(out=sm[:, 3, :], in0=pvl[:, PV_BX:PV_BX + 8], scalar1=0.5, scalar2=None, op0=ALU.mult),
                  reads=[self.b_pv], writes=[b_sm])
            self.dump("sm", sm[:].rearrange("p a b -> p (a b)"), [b_sm])
            half = self.sb(es, "lhalf", [128, 512], F32); b_half = Buf()
            fw.op("pool", lambda e: e.memset(half[:], 1.0), writes=[b_half])
            xbt = self.sb(es, "lxb", [128, T], F32); b_xbt = Buf()
            u = self.sb(es, "lu", [128, T], F32); b_u = Buf()
            ubf = self.sb(es, "lubf", [128, T], BF16); b_ubf = Buf()
            hf, b_hf = xbt, b_xbt
            NU = 7
            ut = [[self.sb(es, "lut%d_%d" % (k, j), [128, 512], F32) for j in range(NU)] for k in range(2)]
            b_ut = [[Buf() for _ in range(NU)] for _ in range(2)]
            hr = [self.sb(es, "lhr%d" % k, [128, 512], F32) for k in range(2)]; b_hr = [Buf(), Buf()]
            gbt = [self.sb(es, "lgb%d" % k, [128, 512], BF16) for k in range(2)]; b_gbt = [Buf(), Buf()]
            gt = [self.sb(es, "lgt%d" % k, [128, 512], F32) for k in range(3)]; b_gt = [Buf() for _ in range(3)]
            yst = [self.sb(es, "lys%d" % k, [128, 512], BF16) for k in range(2)]; b_yst = [Buf(), Buf()]
            pp = [[self.ps(es, "lp%d_%d" % (k, j), [128, 512]) for j in range(2)] for k in range(2)]
            b_pp = [[Buf(True), Buf(True)] for _ in range(2)]
            lat = [(c * 512, 512) for c in range(S // 512)]
            ctxs = [(S, CTX)]
            uc = 0
            for ch in range(2):
                rows = slice(ch * 128, (ch + 1) * 128)
                self.ld_nat("sp", lambda a_, n_: xbt[:, a_:a_ + n_], self.m["XBC"], rows, [b_xbt])
                cw = lambda tap: pvl[:, PV_CW + ch * 4 + tap:PV_CW + ch * 4 + tap + 1]
                for (a0, n0) in ((0, S), (S, CTX)):
                    fw.op("dve", lambda e, a0=a0, n0=n0: e.tensor_scalar(
                        out=u[:, a0:a0 + n0], in0=xbt[:, a0:a0 + n0], scalar1=cw(2), scalar2=pvl[:, PV_CB + ch:PV_CB + ch + 1],
                        op0=ALU.mult, op1=ALU.add), reads=[b_xbt, self.b_pv], writes=[b_u])
                    for tap, so, do, ln in ((0, 0, 2, n0 - 2), (1, 0, 1, n0 - 1), (3, 1, 0, n0 - 1)):
                        fw.op("dve", lambda e, a0=a0, tap=tap, so=so, do=do, ln=ln: e.scalar_tensor_tensor(
                            out=u[:, a0 + do:a0 + do + ln], in0=xbt[:, a0 + so:a0 + so + ln], scalar=cw(tap),
                            in1=u[:, a0 + do:a0 + do + ln], op0=ALU.mult, op1=ALU.add),
                            reads=[b_xbt, self.b_pv, b_u], writes=[b_u])
                fw.op("act", lambda e: e.activation(out=ubf[:], in_=u[:], func=AF.Copy), reads=[b_u], writes=[b_ubf])
                ulist = [(d_, t0_, n_) for d_ in range(2) for (t0_, n_) in ((ctxs + lat) if d_ == 0 else (ctxs + lat[::-1]))]

                def gate_mm(ui_):
                    d_, t0_, n_ = ulist[ui_]
                    kk_ = (uc + ui_) % 2
                    for ax in range(2):
                        fw.op("pe", lambda e, ax=ax: e.matmul(
                            pp[kk_][ax][:, 0:n_], lhsT=wbd[:, (d_ * 2 + ax) * 4 + ch, :], rhs=ubf[:, t0_:t0_ + n_],
                            start=True, stop=True), reads=[b_wbd, b_ubf], writes=[b_pp[kk_][ax]])

                gate_mm(0)
                uidx = 0
                for d in range(2):
                    segs = (ctxs + lat) if d == 0 else (ctxs + lat[::-1])
                    prev = None
                    col = d * 4 + ch
                    for (t0, n) in segs:
                        k_ = (uc + uidx) % 2
                        if uidx + 1 < len(ulist):
                            gate_mm(uidx + 1)
                        uidx += 1
                        T_ = ut[k_]
                        B_ = b_ut[k_]
                        fw.op("act", lambda e, k_=k_: e.activation(out=T_[0][:, 0:n], in_=pp[k_][0][:, 0:n], func=AF.Tanh,
                                                                scale=0.5, bias=sm[:, 2, col:col + 1]),
                              reads=[b_pp[k_][0], b_sm], writes=[B_[0]])
                        fw.op("act", lambda e, k_=k_: e.activation(out=T_[1][:, 0:n], in_=pp[k_][1][:, 0:n], func=AF.Tanh,
                                                                scale=0.5, bias=sm[:, 3, col:col + 1]),
                              reads=[b_pp[k_][1], b_sm], writes=[B_[1]])
                        fw.op("act", lambda e: e.activation(out=T_[2][:, 0:n], in_=T_[0][:, 0:n], func=AF.Exp,
                                                            scale=sm[:, 0, col:col + 1], bias=sm[:, 0, col:col + 1]),
                              reads=[B_[0], b_sm], writes=[B_[2]])
                        fw.op("act", lambda e: e.activation(out=T_[3][:, 0:n], in_=T_[0][:, 0:n], func=AF.Exp,
                                                            scale=sm[:, 1, col:col + 1], bias=sm[:, 1, col:col + 1]),
                              reads=[B_[0], b_sm], writes=[B_[3]])
                        fw.op("act", lambda e: e.activation(out=T_[4][:, 0:n], in_=T_[3][:, 0:n], func=AF.Sqrt, scale=-1.0, bias=half[:, 0:1]),
                              reads=[B_[3], b_half], writes=[B_[4]])
                        fw.op("dve", lambda e: e.scalar_tensor_tensor(out=T_[5][:, 0:n], in0=T_[1][:, 0:n], scalar=1.0,
                                                                      in1=u[:, t0:t0 + n], op0=ALU.add, op1=ALU.mult),
                              reads=[B_[1], b_u], writes=[B_[5]])
                        fw.op("dve", lambda e: e.scalar_tensor_tensor(out=T_[6][:, 0:n], in0=T_[5][:, 0:n], scalar=0.5,
                                                                      in1=T_[4][:, 0:n], op0=ALU.mult, op1=ALU.mult),
                              reads=[B_[5], B_[4]], writes=[B_[6]])
                        if uc == 1:
                            for q_ in range(7):
                                self.dump("T%d" % q_, T_[q_][:, 0:n], [B_[q_]])
                            self.dump("u", u[:, t0:t0 + n], [b_u])
                        if d == 0:
                            init = 0.0 if prev is None else hf[:, prev[0] + prev[1] - 1:prev[0] + prev[1]]
                            fw.op("dve", lambda e, init=init: e.tensor_tensor_scan(
                                out=hf[:, t0:t0 + n], data0=T_[2][:, 0:n], data1=T_[6][:, 0:n], initial=init,
                                op0=ALU.mult, op1=ALU.add), reads=[B_[2], B_[6], b_hf], writes=[b_hf])
                        else:
                            init = 0.0 if prev is None else hr[1 - k_][:, 0:1]
                            rd = [B_[2], B_[6]] + ([] if prev is None else [b_hr[1 - k_]])
                            fw.op("dve", lambda e, init=init, k_=k_: e.tensor_tensor_scan(
                                out=rev(hr[k_][:, 0:n]), data0=rev(T_[2][:, 0:n]), data1=rev(T_[6][:, 0:n]), initial=init,
                                op0=ALU.mult, op1=ALU.add), reads=rd, writes=[b_hr[k_]])
                            G, BG = gbt[k_], b_gbt[k_]
                            for (h_, tl_, nn_, o_) in self.tmap(t0, n):
                                fw.dma("sp", G[:, o_:o_ + nn_], self.m["GB"][rows, h_, tl_:tl_ + nn_], writes=[BG])
                            fw.op("pool", lambda e, G=G: e.tensor_tensor(out=gt[0][:, 0:n], in0=G[:, 0:n], in1=G[:, 0:n], op=ALU.mult),
                                  reads=[BG], writes=[b_gt[0]])
                            fw.op("pool", lambda e: e.tensor_scalar(out=gt[0][:, 0:n], in0=gt[0][:, 0:n], scalar1=0.044715, scalar2=1.0,
                                                                    op0=ALU.mult, op1=ALU.add), reads=[b_gt[0]], writes=[b_gt[0]])
                            fw.op("pool", lambda e, G=G: e.tensor_tensor(out=gt[0][:, 0:n], in0=gt[0][:, 0:n], in1=G[:, 0:n], op=ALU.mult),
                                  reads=[b_gt[0], BG], writes=[b_gt[0]])
                            fw.op("act", lambda e: e.activation(out=gt[1][:, 0:n], in_=gt[0][:, 0:n], func=AF.Tanh, scale=0.7978845608028654),
                                  reads=[b_gt[0]], writes=[b_gt[1]])
                            fw.op("pool", lambda e: e.tensor_scalar(out=gt[1][:, 0:n], in0=gt[1][:, 0:n], scalar1=0.5, scalar2=0.5,
                                                                    op0=ALU.mult, op1=ALU.add), reads=[b_gt[1]], writes=[b_gt[1]])
                            fw.op("pool", lambda e, G=G: e.tensor_tensor(out=gt[1][:, 0:n], in0=gt[1][:, 0:n], in1=G[:, 0:n], op=ALU.mult),
                                  reads=[b_gt[1], BG], writes=[b_gt[1]])
                            fw.op("pool", lambda e, k_=k_: e.tensor_tensor(out=gt[2][:, 0:n], in0=hf[:, t0:t0 + n], in1=hr[k_][:, 0:n], op=ALU.add),
                                  reads=[b_hf, b_hr[k_]], writes=[b_gt[2]])
                            Y, BY = yst[k_], b_yst[k_]
                            fw.op("pool", lambda e, Y=Y: e.tensor_tensor(out=Y[:, 0:n], in0=gt[2][:, 0:n], in1=gt[1][:, 0:n], op=ALU.mult),
                                  reads=[b_gt[2], b_gt[1]], writes=[BY])
                            for (h_, tl_, nn_, o_) in self.tmap(t0, n):
                                fw.dma("pool", self.m["YB"][rows, h_, tl_:tl_ + nn_], Y[:, o_:o_ + nn_], reads=[BY])
                        prev = (t0, n)
                        yield
                uc += len(ulist)

    def phase_na(self, l):
        nc, fw, S, T, R = self.nc, self.fw, self.S, self.T, self.R
        sc = self.s
        NT = T // 128
        NQ = R // 2
        upd_ctx = l < self.L - 1
        with ExitStack() as es:
            KT = self.sb(es, "nKT", [128, T], BF16); b_KT = Buf()
            Qm = [self.sb(es, "nQ%d" % k, [128, T], BF16) for k in range(2)]; b_Qm = [Buf(), Buf()]
            Vp = self.sb(es, "nV", [128, NT, 2, 128], BF16); b_Vp = Buf()
            onesp = self.sb(es, "nones", [128, 2, 128], BF16); b_ones = Buf()
            btf = self.sb(es, "nbtf", [128, 1280], F32); b_btf = Buf()
            bt = self.sb(es, "nbt", [128, 5, 1280], BF16); b_bt = Buf()
            P = [[self.sb(es, "nP%d_%d" % (a, b), [128, 7, 128], BF16) for b in range(2)] for a in range(2)]
            b_P = [[Buf(), Buf()] for _ in range(2)]
            rd = self.sb(es, "nrd", [128, 128], F32); b_rd = Buf()
            yst = [self.sb(es, "nys%d" % k, [128, 512], BF16) for k in range(2)]; b_yst = [Buf(), Buf()]
            Sp = [self.ps(es, "nS%d" % k, [128, 1024]) for k in range(2)]; b_Sp = [Buf(True), Buf(True)]
            po = [self.ps(es, "npo%d" % k, [128, 512]) for k in range(2)]; b_po = [Buf(True), Buf(True)]
            pd = [self.ps(es, "npd%d" % k, [128, 512]) for k in range(2)]; b_pd = [Buf(True), Buf(True)]
            ident = self.cmb[:, CM_ID:CM_ID + 128]
            fw.op("pool", lambda e: e.memset(Vp[:], 0.0), writes=[b_Vp])
            fw.op("pool", lambda e: e.memset(onesp[:], 0.0), writes=[b_ones])
            for h2 in range(2):
                fw.op("pool", lambda e, h2=h2: e.memset(onesp[:, h2, h2 * 64:(h2 + 1) * 64], 1.0), writes=[b_ones])
                fw.op("pool", lambda e, h2=h2: e.memset(Qm[h2][(1 - h2) * 64:(2 - h2) * 64, :], 0.0), writes=[b_Qm[h2]])

            def kvtile(q, j):
                if q is None:
                    return S // 128 + j
                kb = min(max(2 * q - 4, 0), R - 10)
                return kb // 2 + j if j < 5 else S // 128 + (j - 5)

            def variant(q):
                return {0: 1, 1: 2, NQ - 2: 3, NQ - 1: 4}.get(q, 0)

            SLT = self.SL // 128
            for hp in range(2):
                self.ld_nat("sp", lambda a_, n_: KT[:, a_:a_ + n_], self.m["KA"], slice(hp * 128, (hp + 1) * 128), [b_KT])
                for h2 in range(2):
                    r0 = hp * 128 + h2 * 64
                    self.ld_nat("sp", lambda a_, n_, h2=h2: Qm[h2][h2 * 64:(h2 + 1) * 64, a_:a_ + n_], self.m["QA"], slice(r0, r0 + 64), [b_Qm[h2]])
                    for h_ in range(2):
                        fw.dma("sp", Vp[:, h_ * SLT:(h_ + 1) * SLT, h2, h2 * 64:(h2 + 1) * 64],
                               self.m["VA"][h_, 0:self.SL, r0:r0 + 64].rearrange("(j p) f -> p j f", p=128), writes=[b_Vp])
                        fw.dma("sp", Vp[:, S // 128 + h_, h2, h2 * 64:(h2 + 1) * 64], self.m["VA"][h_, self.SL:self.TL, r0:r0 + 64], writes=[b_Vp])
                for v_ in range(5):
                    fw.dma("sp", btf[:], self.i["nabias"][l, v_, hp], writes=[b_btf])
                    fw.op("act", lambda e, v_=v_: e.activation(out=bt[:, v_, :], in_=btf[:], func=AF.Exp), reads=[b_btf], writes=[b_bt])
                items = [("lat", q) for q in range(NQ)] + ([("ctx", b) for b in range(2)] if upd_ctx else [])

                def qk(it, pi):
                    kind, q = it
                    q0 = q * 128 if kind == "lat" else S + q * 128
                    ntile = 7 if kind == "lat" else 2
                    for h2 in range(2):
                        for j in range(ntile):
                            kt = kvtile(q if kind == "lat" else None, j)
                            loc = kind == "lat" and j < 5
                            fw.op("pe", lambda e, h2=h2, j=j, kt=kt: e.matmul(
                                Sp[h2][:, j * 128:(j + 1) * 128], lhsT=KT[:, kt * 128:(kt + 1) * 128],
                                rhs=Qm[h2][:, q0:q0 + 128], start=True, stop=True),
                                reads=[b_KT, b_Qm[h2]], writes=[b_Sp[h2]], sig=(j == ntile - 1))
                        fw.op("act", lambda e, h2=h2: e.activation(
                            out=P[pi][h2][:, 0:ntile, :].rearrange("p a b -> p (a b)"), in_=Sp[h2][:, 0:ntile * 128], func=AF.Exp),
                            reads=[b_Sp[h2]], writes=[b_P[pi][h2]])
                        if kind == "lat":
                            v = variant(q)
                            fw.op("dve" if h2 == 0 else "pool", lambda e, h2=h2, v=v: e.tensor_tensor(
                                out=P[pi][h2][:, 0:5, :].rearrange("p a b -> p (a b)"), in0=P[pi][h2][:, 0:5, :].rearrange("p a b -> p (a b)"),
                                in1=bt[:, v, h2 * 640:(h2 + 1) * 640], op=ALU.mult), reads=[b_P[pi][h2], b_bt], writes=[b_P[pi][h2]])

                def pv(it, pi, idx):
                    kind, q = it
                    q0 = q * 128 if kind == "lat" else S + q * 128
                    ntile = 7 if kind == "lat" else 2
                    o = idx % 2
                    nmm = 2 * ntile
                    for (dst, bdst, isden) in ((po[o], b_po[o], False), (pd[o], b_pd[o], True)):
                        c = 0
                        for h2 in range(2):
                            for j in range(ntile):
                                kt = kvtile(q if kind == "lat" else None, j)
                                lh = onesp[:, h2, :] if isden else Vp[:, kt, h2, :]
                                fw.op("pe", lambda e, lh=lh, h2=h2, j=j, c=c, dst=dst: e.matmul(
                                    dst[:, 0:128], lhsT=lh, rhs=P[pi][h2][:, j, :], start=(c == 0), stop=(c == nmm - 1)),
                                    reads=[b_ones if isden else b_Vp, b_P[pi][h2]], writes=[bdst], sig=(c == nmm - 1))
                                c += 1
                    fw.op("dve", lambda e: e.reciprocal(out=rd[:], in_=pd[o][:, 0:128]), reads=[b_pd[o]], writes=[b_rd])
                    ys = (idx // 4) % 2
                    col = (idx % 4) * 128
                    fw.op("dve", lambda e: e.tensor_tensor(out=yst[ys][:, col:col + 128], in0=po[o][:, 0:128], in1=rd[:], op=ALU.mult),
                          reads=[b_po[o], b_rd], writes=[b_yst[ys]])
                    last = (idx == len(items) - 1)
                    if idx % 4 == 3 or last:
                        nb = idx % 4 + 1
                        first_it = items[idx - nb + 1]
                        t0 = first_it[1] * 128 if first_it[0] == "lat" else S + first_it[1] * 128
                        for (h_, tl_, nn_, o_) in self.tmap(t0, nb * 128):
                            fw.dma("pool", self.m["YA"][hp * 128:(hp + 1) * 128, h_, tl_:tl_ + nn_], yst[ys][:, o_:o_ + nn_], reads=[b_yst[ys]])

                qk(items[0], 0)
                for idx, it in enumerate(items):
                    if idx + 1 < len(items):
                        qk(items[idx + 1], (idx + 1) % 2)
                    pv(it, idx % 2, idx)

    def phase_swa(self, l):
        nc, fw, S, T = self.nc, self.fw, self.S, self.T
        sc = self.s
        NT = T // 128
        NB = S // 128
        upd_ctx = l < self.L - 1
        pvl = self.pv[:, l * NPV:(l + 1) * NPV]
        with ExitStack() as es:
            KSb = self.sb(es, "sK", [128, T], BF16); b_K = Buf()
            Qp = [self.sb(es, "sQ%d" % a, [128, 4, 512], BF16) for a in range(2)]; b_Qp = [Buf(), Buf()]
            Vp = self.sb(es, "sV", [128, NT, 128], BF16); b_Vp = Buf()
            onesp = self.sb(es, "sones", [128, 128], BF16); b_ones = Buf()
            esk = self.sb(es, "sesk", [128, 1], F32); b_esk = Buf()
            stab = self.sb(es, "sstab", [128, 2, 256], BF16); b_stab = Buf()
            P = [self.sb(es, "sP%d" % a, [128, 5, 256], BF16) for a in range(2)]; b_P = [Buf(), Buf()]
            rd = self.sb(es, "srd", [128, 256], F32); b_rd = Buf()
            yst = [[self.sb(es, "sys%d_%d" % (a, hh), [128, 2, 512], BF16) for hh in range(2)] for a in range(2)]
            b_yst = [[Buf(), Buf()] for _ in range(2)]
            Sg0 = self.ps(es, "sS0", [128, 1536]); b_Sg0 = Buf(True)
            Sg, b_Sg = [Sg0, Sg0], [b_Sg0, b_Sg0]
            pod = self.ps(es, "spod", [128, 512]); b_pod = Buf(True)
            po, pd, b_po, b_pd = pod[:, 0:256], pod[:, 256:512], b_pod, b_pod
            ident = self.cmb[:, CM_ID:CM_ID + 128]
            esel = self.cmb[:, CM_ESEL:CM_ESEL + 128]
            masks = {0: self.cmb[:, CM_MPREV:CM_MPREV + 256], 2: self.cmb[:, CM_MNEXT:CM_MNEXT + 256]}
            fw.op("pool", lambda e: e.memset(Vp[:], 0.0), writes=[b_Vp])
            fw.op("pool", lambda e: e.memset(onesp[:], 0.0), writes=[b_ones])
            fw.op("pool", lambda e: e.memset(onesp[:, 0:64], 1.0), writes=[b_ones])
            fw.op("pool", lambda e: e.memset(KSb[64:128, :], 0.0), writes=[b_K])
            for a in range(2):
                fw.op("pool", lambda e, a=a: e.memset(Qp[a][64:128, :, :], 0.0), writes=[b_Qp[a]])
            fw.op("act", lambda e: e.activation(out=esk[:], in_=pvl[:, PV_SINK:PV_SINK + 1], func=AF.Exp), reads=[self.b_pv], writes=[b_esk])
            for hh in range(2):
                c0 = CM_SEL0 if hh == 0 else CM_SEL1
                fw.op("dve", lambda e, hh=hh, c0=c0: e.tensor_scalar(out=stab[:, hh, :], in0=self.cm[:, c0:c0 + 256], scalar1=esk[:, 0:1],
                                                                    scalar2=None, op0=ALU.mult), reads=[self.b_cm, b_esk], writes=[b_stab])
            SLT = self.SL // 128
            self.ld_nat("sp", lambda a_, n_: KSb[0:64, a_:a_ + n_], self.m["KS"], slice(0, 64), [b_K])
            for h_ in range(2):
                fw.dma("sp", Vp[:, h_ * SLT:(h_ + 1) * SLT, 0:64], self.m["VS"][h_, 0:self.SL, :].rearrange("(j p) f -> p j f", p=128), writes=[b_Vp])
                fw.dma("sp", Vp[:, S // 128 + h_, 0:64], self.m["VS"][h_, self.SL:self.TL, :], writes=[b_Vp])
            sbs = [(c * 512, 4, False) for c in range(S // 512)] + ([(S, 2, True)] if upd_ctx else [])

            def load_q(si):
                t0, nb, isc = sbs[si]
                a = si % 2
                for r in range(4):
                    for (h_, tl_, nn_, o_) in self.tmap(t0, nb * 128):
                        fw.dma("sp", Qp[a][0:64, r, o_:o_ + nn_], self.m["QS"][r * 64:(r + 1) * 64, h_, tl_:tl_ + nn_], writes=[b_Qp[a]])

            units = []
            for si, (t0, nb, isc) in enumerate(sbs):
                for nl in range(nb):
                    for hh in range(2):
                        units.append((si, nl, hh))

            def tiles(si, nl):
                t0, nb, isc = sbs[si]
                res = []
                if not isc:
                    n = t0 // 128 + nl
                    if n >= 1:
                        res.append((0, n - 1))
                    res.append((1, n))
                    if n <= NB - 2:
                        res.append((2, n + 1))
                res.append((3, S // 128))
                res.append((4, S // 128 + 1))
                return res

            def qk(ui, pi):
                si, nl, hh = units[ui]
                a = si % 2
                tl = tiles(si, nl)
                g = pi
                for (j, kt) in tl:
                    mk = masks.get(j)
                    fw.op("pe", lambda e, j=j, kt=kt, mk=mk: e.matmul(
                        Sg[g][:, j * 256:(j + 1) * 256], lhsT=KSb[:, kt * 128:(kt + 1) * 128],
                        rhs=Qp[a][:, 2 * hh:2 * hh + 2, nl * 128:(nl + 1) * 128], start=True, stop=(mk is None)),
                        reads=[b_K, b_Qp[a]], writes=[b_Sg[g]], sig=(mk is None and j == 4))
                    if mk is not None:
                        fw.op("pe", lambda e, j=j, mk=mk: e.matmul(
                            Sg[g][:, j * 256:(j + 1) * 256], lhsT=ident, rhs=mk, start=False, stop=True),
                            reads=[self.b_cmb], writes=[b_Sg[g]], sig=False)
                js = [j for (j, _) in tl]
                runs = []
                for j in js:
                    if runs and runs[-1][1] == j:
                        runs[-1][1] = j + 1
                    else:
                        runs.append([j, j + 1])
                for (j0, j1) in runs:
                    fw.op("act", lambda e, j0=j0, j1=j1: e.activation(
                        out=P[pi][:, j0:j1, :].rearrange("p a b -> p (a b)"), in_=Sg[g][:, j0 * 256:j1 * 256], func=AF.Exp),
                        reads=[b_Sg[g]], writes=[b_P[pi]])

            def pv(ui, pi):
                si, nl, hh = units[ui]
                t0, nb, isc = sbs[si]
                a = si % 2
                tl = tiles(si, nl)
                nmm = len(tl)
                for isden in (False, True):
                    dst, bdst = (pd, b_pd) if isden else (po, b_po)
                    for c, (j, kt) in enumerate(tl):
                        lh = onesp[:] if isden else Vp[:, kt, :]
                        fw.op("pe", lambda e, lh=lh, j=j, c=c, dst=dst, isden=isden: e.matmul(
                            dst, lhsT=lh, rhs=P[pi][:, j, :], start=(c == 0), stop=(c == nmm - 1 and not isden)),
                            reads=[b_ones if isden else b_Vp, b_P[pi]], writes=[bdst], sig=(c == nmm - 1 and not isden))
                    if isden:
                        fw.op("pe", lambda e: e.matmul(pd, lhsT=esel, rhs=stab[:, hh, :], start=False, stop=True),
                              reads=[self.b_cmb, b_stab], writes=[b_pd])
                fw.op("dve", lambda e: e.reciprocal(out=rd[0:64, :], in_=pod[0:64, 256:512]), reads=[b_pd], writes=[b_rd])
                Y, BY = yst[a][hh], b_yst[a][hh]
                fw.op("dve", lambda e: e.tensor_tensor(out=Y[0:64, :, nl * 128:(nl + 1) * 128],
                                                       in0=pod[0:64, 0:256].rearrange("p (r t) -> p r t", r=2),
                                                       in1=rd[0:64, :].rearrange("p (r t) -> p r t", r=2), op=ALU.mult),
                      reads=[b_po, b_rd], writes=[BY])
                if nl == nb - 1:
                    for rr in range(2):
                        r_ = 2 * hh + rr
                        for (h_, tl_, nn_, o_) in self.tmap(t0, nb * 128):
                            fw.dma("pool", self.m["YC"][r_ * 64:(r_ + 1) * 64, h_, tl_:tl_ + nn_], Y[0:64, rr, o_:o_ + nn_], reads=[BY])

            load_q(0)
            qk(0, 0)
            for ui in range(len(units)):
                si, nl, hh = units[ui]
                if nl == 0 and hh == 0 and si + 1 < len(sbs):
                    load_q(si + 1)
                if ui + 1 < len(units):
                    qk(ui + 1, (ui + 1) % 2)
                pv(ui, ui % 2)
                yield

    def _gate_tiles(self, es, l, grp, tag):
        fw = self.fw
        G = self.sb(es, tag + "G", [128, 2, D], F32); b_G = Buf()
        for w in range(2):
            row = self.s["MODROW"][l % 2, l // 2, w * 48 + grp * 8:w * 48 + grp * 8 + 8, :].rearrange("j p -> (j p)")
            fw.dma("sp", G[:, w, :], row.partition_broadcast(128), writes=[b_G])
        return G, b_G

    def phase3(self, l):
        nc, fw, S, T = self.nc, self.fw, self.S, self.T
        sc = self.s
        src = self.i["xin"] if l == 0 else sc["XA"]
        chunks = self.chunks if l < self.L - 1 else self.chunks[:-1]
        with ExitStack() as es:
            Wb = self.sb(es, "m3Wb", [128, 12, D], BF16); b_Wb = Buf()
            Wo = self.sb(es, "m3Wo", [128, 8, D], BF16); b_Wo = Buf()
            b_Wbs = [Buf() for _ in range(4)]
            for q4 in range(4):
                fw.dma("pool", Wb[:, :, q4 * 256:(q4 + 1) * 256], self.i["w_branch"][l, :, q4 * 256:(q4 + 1) * 256].rearrange("(k p) m -> p k m", p=128),
                       writes=[b_Wbs[q4]])
            fw.dma("pool", Wo[:], self.i["w_out"][l].rearrange("(k p) m -> p k m", p=128), writes=[b_Wo])
            G, b_G = self._gate_tiles(es, l, 2, "m3")
            Y = [self.sb(es, "m3Y%d" % k, [128, 12, 512], BF16) for k in range(2)]; b_Y = [Buf(), Buf()]
            GT = [self.sb(es, "m3GT%d" % k, [128, 24, 512], BF16) for k in range(2)]; b_GT = [Buf(), Buf()]
            X = [self.sb(es, "m3X%d" % k, [128, 4, D], F32) for k in range(2)]; b_X = [Buf(), Buf()]
            mT = [self.sb(es, "m3mT%d" % k, [128, 8, 512], BF16) for k in range(2)]; b_mT = [Buf(), Buf()]
            tt = [[self.sb(es, "m3t%d_%d" % (a, k), [128, 512], F32) for k in range(3)] for a in range(2)]
            b_tt = [[Buf() for _ in range(3)] for _ in range(2)]
            pb = [self.ps(es, "m3pb%d" % k, [128, 512]) for k in range(6)]; b_pb = [Buf(True) for _ in range(6)]
            pq = [self.ps(es, "m3pq%d" % k, [128, 512]) for k in range(2)]; b_pq = [Buf(True), Buf(True)]

            def load(ci):
                t0, n, w = chunks[ci]
                a = ci % 2
                for bi, nm in enumerate(("YA", "YB", "YC")):
                    for fh_ in range(2):
                        fw.dma("sp", Y[a][:, 4 * bi + 2 * fh_:4 * bi + 2 * fh_ + 2, 0:n],
                               sc[nm][fh_, :, t0:t0 + n].rearrange("(c p) t -> p c t", p=128), writes=[b_Y[a]])
                fw.dma("sp", GT[a][:, :, 0:n], sc["GT"][:, t0:t0 + n].rearrange("(c p) t -> p c t", p=128), writes=[b_GT[a]])
                fw.dma("sp", X[a][:, 0:n // 128, :], src[t0:t0 + n, :].rearrange("(j p) f -> p j f", p=128), writes=[b_X[a]])

            load(0)
            cnt = 0
            for ci, (t0, n, w) in enumerate(chunks):
                a = ci % 2
                nt = n // 128
                if ci + 1 < len(chunks):
                    load(ci + 1)
                for oc in range(8):
                    s3 = (oc % 2) * 3
                    tset = oc % 2
                    for bi in range(3):
                        for k in range(4):
                            fw.op("pe", lambda e, bi=bi, k=k, oc=oc, s3=s3: e.matmul(
                                pb[s3 + bi][:, 0:n], lhsT=Wb[:, 4 * bi + k, oc * 128:(oc + 1) * 128], rhs=Y[a][:, 4 * bi + k, 0:n],
                                start=(k == 0), stop=(k == 3)), reads=[b_Wbs[oc // 2], b_Y[a]], writes=[b_pb[s3 + bi]], sig=(k == 3))
                    for bi in range(3):
                        fw.op("dve", lambda e, bi=bi, oc=oc, s3=s3, tset=tset: e.tensor_tensor(
                            out=tt[tset][bi][:, 0:n], in0=pb[s3 + bi][:, 0:n], in1=GT[a][:, bi * 8 + oc, 0:n], op=ALU.mult),
                            reads=[b_pb[s3 + bi], b_GT[a]], writes=[b_tt[tset][bi]])
                    fw.op("pool", lambda e, tset=tset: e.tensor_tensor(out=tt[tset][0][:, 0:n], in0=tt[tset][0][:, 0:n], in1=tt[tset][1][:, 0:n], op=ALU.add),
                          reads=[b_tt[tset][0], b_tt[tset][1]], writes=[b_tt[tset][0]])
                    fw.op("pool", lambda e, tset=tset, oc=oc: e.tensor_tensor(out=mT[a][:, oc, 0:n], in0=tt[tset][0][:, 0:n], in1=tt[tset][2][:, 0:n], op=ALU.add),
                          reads=[b_tt[tset][0], b_tt[tset][2]], writes=[b_mT[a]])
                for j in range(nt):
                    for hf_ in range(2):
                        q_ = cnt % 2
                        cnt += 1
                        for k in range(8):
                            fw.op("pe", lambda e, q_=q_, k=k, j=j, hf_=hf_: e.matmul(
                                pq[q_][:, :], lhsT=mT[a][:, k, j * 128:(j + 1) * 128], rhs=Wo[:, k, hf_ * 512:(hf_ + 1) * 512],
                                start=(k == 0), stop=(k == 7)), reads=[b_mT[a], b_Wo], writes=[b_pq[q_]], sig=(k == 7))
                        tq = tt[q_][0]
                        fw.op("dve", lambda e, q_=q_, hf_=hf_, tq=tq: e.tensor_tensor(out=tq[:], in0=pq[q_][:], in1=G[:, w, hf_ * 512:(hf_ + 1) * 512], op=ALU.mult),
                              reads=[b_pq[q_], b_G], writes=[b_tt[q_][0]])
                        fw.op("pool", lambda e, j=j, hf_=hf_, tq=tq: e.tensor_tensor(
                            out=X[a][:, j, hf_ * 512:(hf_ + 1) * 512], in0=X[a][:, j, hf_ * 512:(hf_ + 1) * 512], in1=tq[:], op=ALU.add),
                            reads=[b_X[a], b_tt[q_][0]], writes=[b_X[a]])
                fw.dma("pool", sc["XB"][t0:t0 + n, :].rearrange("(j p) f -> p j f", p=128), X[a][:, 0:nt, :], reads=[b_X[a]])

    def phase4a(self, l):
        nc, fw, S, T = self.nc, self.fw, self.S, self.T
        sc = self.s
        chunks = self.chunks if l < self.L - 1 else self.chunks[:-1]
        with ExitStack() as es:
            W = self.sb(es, "f1W", [128, 8, DFF], BF16); b_W = [Buf() for _ in range(8)]
            for og_ in range(8):
                fw.dma("pool", W[:, :, og_ * 512:(og_ + 1) * 512], self.i["w_ff1"][l, :, og_ * 512:(og_ + 1) * 512].rearrange("(k p) m -> p k m", p=128),
                       writes=[b_W[og_]])
            np_ = self._norm_pipe(es, l, 2, 3, sc["XB"], chunks, "f1")
            hT, b_hT, load_x, prep = np_["hT"], np_["b_hT"], np_["load_x"], np_["prep"]
            tmp = [self.sb(es, "f1t%d" % k, [128, 512], BF16) for k in range(2)]; b_tmp = [Buf(), Buf()]
            stg = [self.sb(es, "f1s%d" % k, [128, 4, 512], BF16) for k in range(3)]; b_stg = [Buf() for _ in range(3)]
            pac = [self.ps(es, "f1ac%d" % k, [128, 512]) for k in range(4)]; b_pac = [Buf(True) for _ in range(4)]
            load_x(0)
            prep(0)
            acc = 0
            sg = 0
            for ci, (t0, n, w) in enumerate(chunks):
                slot = ci % 2
                for og in range(8):
                    S_, BS_ = stg[sg % 3], b_stg[sg % 3]
                    sg += 1
                    if og == 4 and ci + 1 < len(chunks):
                        prep(ci + 1)
                    for cc in range(4):
                        oc = og * 4 + cc
                        a = acc % 4
                        acc += 1
                        for k in range(8):
                            fw.op("pe", lambda e, a=a, k=k, oc=oc: e.matmul(pac[a][:, 0:n], lhsT=W[:, k, oc * 128:(oc + 1) * 128],
                                                                            rhs=hT[slot][:, k, 0:n], start=(k == 0), stop=(k == 7)),
                                  reads=[b_W[og], b_hT[slot]], writes=[b_pac[a]], sig=(k == 7))
                        tq = cc % 2
                        if cc % 2 == 0:
                            fw.op("act", lambda e, a=a, tq=tq: e.activation(out=tmp[tq][:, 0:n], in_=pac[a][:, 0:n], func=AF.Relu),
                                  reads=[b_pac[a]], writes=[b_tmp[tq]])
                        else:
                            fw.op("dve", lambda e, a=a, tq=tq: e.tensor_scalar(out=tmp[tq][:, 0:n], in0=pac[a][:, 0:n], scalar1=0.0, scalar2=None,
                                                                              op0=ALU.max), reads=[b_pac[a]], writes=[b_tmp[tq]])
                        fw.op("pool", lambda e, tq=tq, cc=cc, S_=S_: e.tensor_tensor(out=S_[:, cc, 0:n], in0=tmp[tq][:, 0:n], in1=tmp[tq][:, 0:n], op=ALU.mult),
                              reads=[b_tmp[tq]], writes=[BS_])
                    fw.dma("pool", sc["AT"][og * 512:(og + 1) * 512, t0:t0 + n].rearrange("(c p) t -> p c t", p=128), S_[:, :, 0:n], reads=[BS_])

    def phase4b(self, l):
        nc, fw, S, T = self.nc, self.fw, self.S, self.T
        sc = self.s
        last = l == self.L - 1
        chunks = self.chunks if not last else self.chunks[:-1]
        with ExitStack() as es:
            W = self.sb(es, "f2W", [128, 32, D], BF16); b_W = [[Buf() for _ in range(4)] for _ in range(2)]
            for hf2 in range(2):
                for k4 in range(4):
                    fw.dma("pool", W[:, 8 * k4:8 * k4 + 8, hf2 * 512:(hf2 + 1) * 512],
                           self.i["w_ff2"][l, 1024 * k4:1024 * (k4 + 1), hf2 * 512:(hf2 + 1) * 512].rearrange("(k p) m -> p k m", p=128),
                           writes=[b_W[hf2][k4]])
            G, b_G = self._gate_tiles(es, l, 5, "f2")
            A = [self.sb(es, "f2A%d" % k, [128, 32, 512], BF16) for k in range(2)]; b_A = [Buf(), Buf()]
            X = [self.sb(es, "f2X%d" % k, [128, 4, D], F32) for k in range(2)]; b_X = [Buf(), Buf()]
            tq = [self.sb(es, "f2t%d" % k, [128, 512], F32) for k in range(2)]; b_tq = [Buf(), Buf()]
            pq = [self.ps(es, "f2pq%d" % k, [128, 512]) for k in range(4)]; b_pq = [Buf(True) for _ in range(4)]
            if last:
                FG = self.sb(es, "f2FG", [128, D], F32); b_FG = Buf()
                fw.dma("sp", FG[:], self.i["fgrow"].partition_broadcast(128), writes=[b_FG])
                junk = self.sb(es, "f2junk", [128, D], BF16); b_junk = Buf()
                ss = self.sb(es, "f2ss", [128, 4], F32); b_ss = Buf()
                eps_t = self.sb(es, "f2eps", [128, 1], F32); b_eps = Buf()
                fw.op("pool", lambda e: e.memset(eps_t[:], 1e-6), writes=[b_eps])

            def load(ci):
                t0, n, w = chunks[ci]
                a = ci % 2
                for k4 in range(4):
                    fw.dma("sp", A[a][:, 8 * k4:8 * k4 + 8, 0:n], sc["AT"][1024 * k4:1024 * (k4 + 1), t0:t0 + n].rearrange("(c p) t -> p c t", p=128),
                           writes=[b_A[a]])
                fw.dma("sp", X[a][:, 0:n // 128, :], sc["XB"][t0:t0 + n, :].rearrange("(j p) f -> p j f", p=128), writes=[b_X[a]])

            load(0)
            cnt = 0
            for ci, (t0, n, w) in enumerate(chunks):
                a = ci % 2
                nt = n // 128
                if ci + 1 < len(chunks):
                    load(ci + 1)
                for j in range(nt):
                    for hf_ in range(2):
                        q_ = cnt % 4
                        t_ = cnt % 2
                        cnt += 1
                        for k in range(32):
                            fw.op("pe", lambda e, q_=q_, k=k, j=j, hf_=hf_: e.matmul(
                                pq[q_][:, :], lhsT=A[a][:, k, j * 128:(j + 1) * 128], rhs=W[:, k, hf_ * 512:(hf_ + 1) * 512],
                                start=(k == 0), stop=(k == 31)), reads=[b_A[a], b_W[hf_][k // 8]], writes=[b_pq[q_]], sig=(k == 31))
                        fw.op("dve", lambda e, q_=q_, hf_=hf_, t_=t_: e.tensor_tensor(out=tq[t_][:], in0=pq[q_][:], in1=G[:, w, hf_ * 512:(hf_ + 1) * 512], op=ALU.mult),
                              reads=[b_pq[q_], b_G], writes=[b_tq[t_]])
                        fw.op("pool", lambda e, j=j, hf_=hf_, t_=t_: e.tensor_tensor(
                            out=X[a][:, j, hf_ * 512:(hf_ + 1) * 512], in0=X[a][:, j, hf_ * 512:(hf_ + 1) * 512], in1=tq[t_][:], op=ALU.add),
                            reads=[b_X[a], b_tq[t_]], writes=[b_X[a]])
                if not last:
                    fw.dma("pool", sc["XA"][t0:t0 + n, :].rearrange("(j p) f -> p j f", p=128), X[a][:, 0:nt, :], reads=[b_X[a]])
                else:
                    for j in range(nt):
                        fw.op("act", lambda e, j=j: e.activation(out=junk[:], in_=X[a][:, j, :], func=AF.Square, accum_out=ss[:, j:j + 1]),
                              reads=[b_X[a]], writes=[b_junk, b_ss])
                    fw.op("act", lambda e: e.activation(out=ss[:, 0:nt], in_=ss[:, 0:nt], func=AF.Sqrt, bias=eps_t[:], scale=1.0 / D),
                          reads=[b_ss, b_eps], writes=[b_ss])
                    fw.op("dve", lambda e: e.reciprocal(out=ss[:, 0:nt], in_=ss[:, 0:nt]), reads=[b_ss], writes=[b_ss])
                    for j in range(nt):
                        fw.op("dve", lambda e, j=j: e.scalar_tensor_tensor(out=X[a][:, j, :], in0=X[a][:, j, :], scalar=ss[:, j:j + 1], in1=FG[:],
                                                                          op0=ALU.mult, op1=ALU.mult), reads=[b_X[a], b_ss, b_FG], writes=[b_X[a]])
                    fw.dma("pool", self.out[t0:t0 + n, :].rearrange("(j p) f -> p j f", p=128), X[a][:, 0:nt, :], reads=[b_X[a]])


def _fm(v):
    v = np.asarray(v, np.float32)
    return np.ascontiguousarray(v.reshape(-1, 128).T)


def _na_tables(rpb, R):
    qs = [2, 0, 1, R // 2 - 2, R // 2 - 1]
    out = np.empty((5, 4, 128, 2, 5, 128), np.float32)
    part = np.arange(128)
    i2, jc = part // 64, part % 64
    qc = np.arange(128)
    r2, w = qc // 64, qc % 64
    for vi, q in enumerate(qs):
        kb = int(np.clip(2 * q - 4, 0, R - 10))
        r = 2 * q + r2
        rs = np.clip(r - 4, 0, R - 8)
        cs = np.clip(w - 8, 0, GW - 16)
        for j in range(5):
            kr = kb + 2 * j + i2
            valid = ((kr[:, None] >= rs[None]) & (kr[:, None] < rs[None] + 8)
                     & (jc[:, None] >= cs[None]) & (jc[:, None] < cs[None] + 16))
            dr = np.clip(kr[:, None] - r[None] + 7, 0, 14)
            dc = np.clip(jc[:, None] - w[None] + 15, 0, 30)
            for h in range(8):
                tab = np.where(valid, rpb[h][dr, dc], np.float32(MASKV))
                out[vi, h // 2, :, h % 2, j, :] = tab
    return out.reshape(5, 4, 128, 1280)


def host_prep(inp, S, L):
    B = inp["x"].shape[0]
    R = S // GW
    SL = S // 2
    f32 = np.float32
    t = np.arange(S)
    row = (t // GW).astype(f32)
    col = (t % GW).astype(f32)
    inv = (np.float32(10000.0) ** (-np.arange(16, dtype=f32) / np.float32(16))).astype(f32)
    ang = np.concatenate([row[:, None] * inv, col[:, None] * inv], axis=-1).astype(f32)
    cosv, sinv = np.cos(ang.astype(np.float64)), np.sin(ang.astype(np.float64))
    p = np.arange(128)
    ropec = np.ascontiguousarray(cosv[:, p % 32].T).astype(f32)
    sgn = np.where((p % 64) < 32, -1.0, 1.0)
    ropes = np.ascontiguousarray((sinv[:, p % 32] * sgn[None]).T).astype(f32)
    cm = np.zeros((128, NCM), f32)
    cm[:, CM_ID:CM_ID + 128] = np.eye(128, dtype=f32)
    partner = np.where((p % 64) < 32, p + 32, p - 32)
    cm[partner, CM_PERM + p] = 1.0
    kk, qq = np.meshgrid(np.arange(128), np.arange(128), indexing="ij")
    mprev = np.where(kk >= qq, 0.0, MASKV).astype(f32)
    mnext = np.where(kk <= qq, 0.0, MASKV).astype(f32)
    cm[:, CM_MPREV:CM_MPREV + 256] = np.tile(mprev, (1, 2))
    cm[:, CM_MNEXT:CM_MNEXT + 256] = np.tile(mnext, (1, 2))
    for k in range(4):
        cm[k, CM_ESEL:CM_ESEL + 64] = 1.0
        base = CM_SEL0 if k < 2 else CM_SEL1
        cm[k, base + (k % 2) * 128:base + (k % 2) * 128 + 128] = 1.0
    nab_all = [_na_tables(np.asarray(inp["na_rpb"][l], f32), R) for l in range(L)]
    nonce = int(np.random.default_rng().integers(1, 2 ** 30))
    ep = (nonce + np.arange(16)).astype(np.int32)[None]
    Lh = (L + 1) // 2
    shared = dict(w_in=np.ascontiguousarray(inp["w_in"][:L], f32),
                  w_branch=np.ascontiguousarray(inp["w_branch"][:L], f32), w_out=np.ascontiguousarray(inp["w_out"][:L], f32),
                  w_ff1=np.ascontiguousarray(inp["w_ff1"][:L], f32), w_ff2=np.ascontiguousarray(inp["w_ff2"][:L], f32),
                  fgv=_fm(inp["final_g"]), fgrow=np.ascontiguousarray(inp["final_g"], f32), cmat=cm, ep=ep)
    par_maps = []
    for par in range(2):
        pvec = np.zeros((128, L, NPV), f32)
        lruw = np.zeros((L, 2, 2, 4, 128, 128), f32)
        nab = np.zeros((L, 5, 2, 128, 1280), f32)
        for l in range(L):
            pvec[:, l, PV_BMOD:PV_BMOD + 48] = _fm(inp["b_mod"][l])
            pvec[:, l, PV_N1G:PV_N1G + 8] = _fm(inp["norm1_g"][l])
            pvec[:, l, PV_N2G:PV_N2G + 8] = _fm(inp["norm2_g"][l])
            chs = slice(2 * par, 2 * par + 2)
            for tap in range(4):
                pvec[:, l, PV_CW + tap:PV_CW + 8:4] = _fm(inp["conv_w"][l, tap])[:, chs]
            pvec[:, l, PV_CB:PV_CB + 2] = _fm(inp["conv_b"][l])[:, chs]
            for d in range(2):
                pvec[:, l, PV_BA + 4 * d:PV_BA + 4 * d + 2] = _fm(inp["lru_ba"][l, d])[:, chs]
                pvec[:, l, PV_BX + 4 * d:PV_BX + 4 * d + 2] = _fm(inp["lru_bx"][l, d])[:, chs]
                pvec[:, l, PV_LAM + 4 * d:PV_LAM + 4 * d + 4] = 8.0
                pvec[:, l, PV_LAM + 4 * d:PV_LAM + 4 * d + 2] = _fm(inp["lru_lambda"][l, d])[:, chs]
                for ax, nm in enumerate(("lru_wa", "lru_wx")):
                    for cl in range(2):
                        for hb in range(2):
                            lruw[l, d, ax, cl, hb * 64:(hb + 1) * 64, hb * 64:(hb + 1) * 64] = inp[nm][l, d, 2 * (2 * par + cl) + hb]
            pvec[0:4, l, PV_SINK] = inp["swa_sink"][l][4 * par:4 * par + 4]
            nab[l] = nab_all[l][:, 2 * par:2 * par + 2]
        mine = [2 * li + par if 2 * li + par < L else 0 for li in range(Lh)]
        pvec0 = np.zeros((128, Lh, 64), f32)
        for li, l_ in enumerate(mine):
            pvec0[:, li, 0:48] = _fm(inp["b_mod"][l_])
            pvec0[:, li, 48:56] = _fm(inp["norm1_g"][l_])
            pvec0[:, li, 56:64] = _fm(inp["norm2_g"][l_])
        par_maps.append(dict(pvec=pvec.reshape(128, L * NPV), lruw=lruw, nabias=nab,
                             w_mod=np.ascontiguousarray(np.asarray(inp["w_mod"], f32)[mine]), pvec0=pvec0.reshape(128, Lh * 64),
                             ropec=np.ascontiguousarray(ropec[:, par * SL:(par + 1) * SL]),
                             ropes=np.ascontiguousarray(ropes[:, par * SL:(par + 1) * SL])))
    maps = []
    for b in range(B):
        for par in range(2):
            m = dict(shared)
            m.update(par_maps[par])
            m["xin"] = np.ascontiguousarray(np.concatenate([inp["x"][b][par * SL:(par + 1) * SL],
                                                           inp["ctx"][b][par * 128:(par + 1) * 128]], axis=0), f32)
            cv = np.empty((128, 16), f32)
            cv[:, 0::2] = _fm(inp["c"][b])
            cv[:, 1::2] = _fm(inp["c_ctx"])
            m["cvec"] = cv
            maps.append(m)
    return maps


_CACHE = {}


def run(inp, S, L, dbg=()):
    key = (S, L, tuple(dbg))
    maps = host_prep(inp, S, L)
    prog = Prog(S, L, dbg)
    nc = prog.build()
    res = run_bass_kernel_spmd(nc, maps, core_ids=list(range(len(maps))))
    return res.results, prog


def kernel(**inputs):
    inp = {k: np.asarray(v) for k, v in inputs.items()}
    S = inp["x"].shape[1]
    L = inp["w_in"].shape[0]
    results, _ = run(inp, S, L)
    B = inp["x"].shape[0]
    return np.stack([np.concatenate([results[2 * b]["out"], results[2 * b + 1]["out"]], axis=0) for b in range(B)], axis=0).astype(np.float32)
```

```python
import numpy as np
from contextlib import ExitStack
import concourse.bass as bass
import concourse.mybir as mybir
from concourse.bass_utils import run_bass_kernel_spmd
from concourse.ap import AP

F32 = mybir.dt.float32
BF16 = mybir.dt.bfloat16
I32 = mybir.dt.int32
AF = mybir.ActivationFunctionType
ALU = mybir.AluOpType

D = 1024
CTX = 256
GW = 64
HD = 64
DFF = 4096
INW = 6400
MODW = 6144
MASKV = -30000.0
NPV = 112
PV_BMOD, PV_N1G, PV_N2G, PV_CW, PV_CB, PV_BA, PV_BX, PV_LAM, PV_SINK = 0, 48, 56, 64, 80, 84, 92, 100, 108
CM_ID, CM_PERM, CM_MPREV, CM_MNEXT, CM_ESEL, CM_SEL0, CM_SEL1 = 0, 128, 256, 512, 768, 896, 1152
NCM = 1408
OFF = dict(qa=0, ka=512, va=1024, xb=1536, gb=2048, qs=2560, ks=3072, vs=3200, ga=3328, gr=4352, gs=5376)


class Buf:
    __slots__ = ("w", "r", "excl")

    def __init__(self, excl=False):
        self.w = None
        self.r = []
        self.excl = excl


class FW:
    ENG = ("pe", "act", "dve", "pool", "sp")

    def __init__(self, nc, es, ndma=12):
        self.nc = nc
        self.e = {"pe": nc.tensor, "act": nc.scalar, "dve": nc.vector, "pool": nc.gpsimd, "sp": nc.sync}
        self.sem = {k: es.enter_context(nc.semaphore("s_" + k)) for k in self.ENG}
        self.cnt = {k: 0 for k in self.ENG}
        self.seen = {k: {} for k in self.ENG}
        self.pend = {k: [] for k in self.ENG}
        self.dq = {}
        for q in ("sp", "act", "pool"):
            self.dq[q] = {"sems": [es.enter_context(nc.semaphore("d_%s%d" % (q, i))) for i in range(ndma)],
                          "uses": [0] * ndma, "next": 0}
        self.nins = 0
        self.nwait = 0

    def _wait(self, eng, ev):
        key, val = ev
        if self.seen[eng].get(key, 0) >= val:
            return
        self.seen[eng][key] = val
        sem = self.sem[key] if isinstance(key, str) else self.dq[key[1]]["sems"][key[2]]
        self.e[eng].wait_ge(sem, val)
        self.nwait += 1

    def _deps(self, eng, reads, writes):
        for b in reads:
            if b.w is not None and (b.w[0] != eng or eng != "pe"):
                self._wait(eng, b.w)
            if b.excl:
                for ev in b.r:
                    if ev[0] != eng:
                        self._wait(eng, ev)
        for b in writes:
            if b.w is not None and b.w[0] != eng:
                self._wait(eng, b.w)
            for ev in b.r:
                if ev[0] != eng:
                    self._wait(eng, ev)

    def _reg(self, ev, reads, writes):
        for b in reads:
            if len(b.r) > 12:
                last = {}
                for k, v in b.r:
                    if last.get(k, 0) < v:
                        last[k] = v
                b.r = list(last.items())
            b.r.append(ev)
        for b in writes:
            b.w = ev
            b.r = []

    def op(self, eng, fn, reads=(), writes=(), sig=True):
        self._deps(eng, reads, writes)
        self.nins += 1
        ins = fn(self.e[eng])
        if sig:
            self.cnt[eng] += 1
            ev = (eng, self.cnt[eng])
            ins.then_inc(self.sem[eng], 1)
            for (r, w) in self.pend[eng]:
                self._reg(ev, r, w)
            self.pend[eng] = []
            self._reg(ev, reads, writes)
            return ev
        self.pend[eng].append((reads, writes))
        return None

    def dma(self, q, out, in_, reads=(), writes=()):
        d = self.dq[q]
        i = d["next"]
        d["next"] = (i + 1) % len(d["sems"])
        key = ("d", q, i)
        if d["uses"][i] > 0:
            self._wait(q, (key, 16 * d["uses"][i]))
        self._deps(q, reads, writes)
        d["uses"][i] += 1
        ev = (key, 16 * d["uses"][i])
        self.e[q].dma_start(out=out, in_=in_).then_inc(d["sems"][i], 16)
        self.nins += 1
        self._reg(ev, reads, writes)
        return ev

    def barrier(self):
        evs = [(k, self.cnt[k]) for k in self.ENG if self.cnt[k] > 0]
        for q, d in self.dq.items():
            for i, u in enumerate(d["uses"]):
                if u > 0:
                    evs.append((("d", q, i), 16 * u))
        for eng in self.ENG:
            for ev in evs:
                if ev[0] != eng:
                    self._wait(eng, ev)


class Prog:
    def __init__(self, S, depth, dbg=()):
        self.S = S
        self.T = S + CTX
        self.R = S // GW
        self.L = depth
        self.dbg = set(dbg)
        self.SL = S // 2
        self.TL = self.SL + 128
        self.chunks = [(c * 512, 512, 0) for c in range(self.SL // 512)] + [(self.SL, 128, 1)]
        self.nc = bass.Bass("TRN2", target_bir_lowering=False)
        self.par = {"sp": self.nc.sync.partition_id() % 2, "pool": self.nc.gpsimd.partition_id() % 2,
                    "act": self.nc.scalar.partition_id() % 2}

    def tmap(self, t0, n):
        S, SL = self.S, self.SL
        if t0 < S:
            assert (t0 % SL) + n <= SL
            return [(t0 // SL, t0 % SL, n, 0)]
        res, o = [], 0
        a, b = t0 - S, t0 - S + n
        for h in range(2):
            lo, hi = max(a, h * 128), min(b, (h + 1) * 128)
            if lo < hi:
                res.append((h, SL + lo - h * 128, hi - lo, lo - a))
        return res

    def ld_nat(self, q, tile_fn, loc, rows, writes):
        S, SL, TL = self.S, self.SL, self.TL
        for h in range(2):
            self.fw.dma(q, tile_fn(h * SL, SL), loc[rows, h, 0:SL], writes=writes)
            self.fw.dma(q, tile_fn(S + h * 128, 128), loc[rows, h, SL:TL], writes=writes)

    def din(self, name, shape, dt=F32):
        return self.nc.dram_tensor(name, list(shape), dt, kind="ExternalInput").ap()

    def dscr(self, name, shape, dt, shared=False):
        kind = "ExternalOutput" if name in self.dbg else "Internal"
        if shared and kind == "Internal":
            return self.nc.dram_tensor(name, list(shape), dt, kind=kind, addr_space="Shared").ap()
        return self.nc.dram_tensor(name, list(shape), dt, kind=kind).ap()

    def _uniq(self, name):
        self._nid = getattr(self, "_nid", 0) + 1
        return "%s_%d" % (name, self._nid)

    def sb(self, es, name, shape, dt):
        t = es.enter_context(self.nc.sbuf_tensor(self._uniq(name), list(shape), dt))
        return t

    def dump(self, name, ap_, reads):
        if "dump" not in self.dbg:
            return
        shp = [a[1] for a in ap_.ap]
        d = self.nc.dram_tensor("dbg_" + name, shp, ap_.dtype, kind="ExternalOutput").ap()
        self.fw.dma("sp", d, ap_, reads=reads)

    def ps(self, es, name, shape, dt=F32):
        return es.enter_context(self.nc.psum_tensor(self._uniq(name), list(shape), dt))

    def build(self):
        nc, S, T, L = self.nc, self.S, self.T, self.L
        i = self.i = {}
        TL, SL = self.TL, self.SL
        i["xin"] = self.din("xin", [TL, D])
        i["ep"] = self.din("ep", [1, 16], I32)
        i["cvec"] = self.din("cvec", [128, 16])
        Lh = self.Lh = (L + 1) // 2
        i["w_mod"] = self.din("w_mod", [Lh, D, MODW])
        i["pvec0"] = self.din("pvec0", [128, Lh * 64])
        i["w_in"] = self.din("w_in", [L, D, INW])
        i["w_branch"] = self.din("w_branch", [L, 1536, D])
        i["w_out"] = self.din("w_out", [L, D, D])
        i["w_ff1"] = self.din("w_ff1", [L, D, DFF])
        i["w_ff2"] = self.din("w_ff2", [L, DFF, D])
        i["pvec"] = self.din("pvec", [128, L * NPV])
        i["fgv"] = self.din("fgv", [128, 8])
        i["lruw"] = self.din("lruw", [L, 2, 2, 4, 128, 128])
        i["nabias"] = self.din("nabias", [L, 5, 2, 128, 1280])
        i["ropec"] = self.din("ropec", [128, SL])
        i["ropes"] = self.din("ropes", [128, SL])
        i["cmat"] = self.din("cmat", [128, NCM])
        i["fgrow"] = self.din("fgrow", [D])
        self.out = nc.dram_tensor("out", [SL, D], F32, kind="ExternalOutput").ap()
        s = self.s = {}
        m = self.m = {}
        sh = self.sh = {}
        s["XA"] = self.dscr("XA", [TL, D], F32)
        s["XB"] = self.dscr("XB", [TL, D], F32)
        FM_l = self.dscr("FM_l", [2, 1088, TL], BF16)
        FM_m = self.dscr("FM_m", [1088, 2, TL], BF16)
        sh["FM"] = self.dscr("FM_s", [2, 1088, TL], BF16, shared=True)
        r0 = 0
        for n, f in (("QA", 256), ("KA", 256), ("GB", 256), ("QS", 256), ("KS", 64)):
            s[n] = FM_l[:, r0:r0 + f, :]
            m[n] = FM_m[r0:r0 + f]
            r0 += f
        s["XBC"] = self.dscr("XBC", [2, 256, TL], F32)
        m["XBC"] = self.dscr("XBC_m", [256, 2, TL], F32)
        sh["XBC"] = self.dscr("XBC_s", [2, 256, TL], F32, shared=True)
        V_l = self.dscr("V_l", [TL, 2, 320], BF16)
        V_m = self.dscr("V_m", [2, TL, 320], BF16)
        sh["V"] = self.dscr("V_s", [2, TL, 320], BF16, shared=True)
        s["VA"], s["VS"] = V_l[:, :, 0:256], V_l[:, :, 256:320]
        m["VA"], m["VS"] = V_m[:, :, 0:256], V_m[:, :, 256:320]
        Y_m = self.dscr("Y_m", [768, 2, TL], BF16)
        Y_t = self.dscr("Y_t", [2, 768, TL], BF16)
        sh["Y"] = self.dscr("Y_s", [2, 768, TL], BF16, shared=True)
        for k_, n in enumerate(("YA", "YB", "YC")):
            m[n] = Y_m[k_ * 256:(k_ + 1) * 256]
            s[n] = Y_t[:, k_ * 256:(k_ + 1) * 256, :]
        self.bund = dict(FM_l=FM_l, FM_m=FM_m, V_l=V_l, V_m=V_m, Y_m=Y_m, Y_t=Y_t)
        s["GT"] = self.dscr("GT", [3072, TL], BF16)
        s["AT"] = self.dscr("AT", [DFF, TL], BF16)
        s["MODROW"] = self.dscr("MODROW", [2, Lh, 96, 128], F32, shared=True)
        s["MODP"] = self.dscr("MODP", [2, 128, Lh * 64], F32, shared=True)
        s["FLAG"] = nc.dram_tensor("FLAG", [2, 16], I32, kind="Internal", addr_space="Shared").ap()
        with ExitStack() as es:
            self.fw = FW(nc, es)
            self.ft = self.sb(es, "ft", [1, 32], I32)
            self.fsem = es.enter_context(nc.semaphore("fsem"))
            self.psem = es.enter_context(nc.semaphore("psem"))
            nc.sync.dma_start(out=self.ft[0:1, 0:16], in_=i["ep"]).then_inc(self.fsem, 16)
            nc.sync.wait_ge(self.fsem, 16)
            self.fn = 16
            self.pv = self.sb(es, "pv", [128, L * NPV], F32)
            self.modp = self.sb(es, "modp", [128, L, 4, 2, 8], F32)
            self.cm = self.sb(es, "cm", [128, NCM], F32)
            self.cmb = self.sb(es, "cmb", [128, NCM], BF16)
            self.b_pv, self.b_modp, self.b_cm, self.b_cmb = Buf(), Buf(), Buf(), Buf()
            fw = self.fw
            fw.dma("sp", self.pv[:], i["pvec"], writes=[self.b_pv])
            fw.dma("sp", self.cm[:], i["cmat"], writes=[self.b_cm])
            fw.op("dve", lambda e: e.tensor_copy(out=self.cmb[:], in_=self.cm[:]), reads=[self.b_cm], writes=[self.b_cmb])
            self.phase0()
            fw.barrier()
            for l in range(L):
                self.phase1(l)
                self.exchange1(l)
                if "stop1" in self.dbg:
                    break
                self.phase_na(l)
                fw.barrier()
                g1, g2 = self.phase_lru(l), self.phase_swa(l)
                live = [g1, g2]
                while live:
                    for g_, reps in ((g1, 1), (g2, 2)):
                        for _ in range(reps):
                            if g_ in live:
                                try:
                                    next(g_)
                                except StopIteration:
                                    live.remove(g_)
                self.exchange2(l)
                if "stop2c" in self.dbg:
                    break
                self.phase3(l)
                fw.barrier()
                if "stop3" in self.dbg:
                    break
                self.phase4a(l)
                fw.barrier()
                self.phase4b(l)
                fw.barrier()
            fw.barrier()
        return nc

    def pair_barrier(self, slot):
        nc, fw = self.nc, self.fw
        fw.barrier()
        q = ("sp", "act")[slot % 2]
        e = fw.e[q]
        flag = self.s["FLAG"]
        fs = slot % 2
        e.dma_start(out=flag[bass.ts(self.par[q], 1), fs:fs + 1], in_=self.ft[0:1, slot:slot + 1]).then_inc(self.fsem, 16)
        self.fn += 16
        e.wait_ge(self.fsem, self.fn)
        ft, psem, partner = self.ft, self.psem, 1 - self.par[q]

        def body(e2):
            with e2.register("rp%d" % slot) as rp, e2.register("rd%d" % slot) as rd, e2.register("rw%d" % slot) as rw:
                e2.load(rw, ft[0:1, slot:slot + 1])

                def cond():
                    e2.sem_clear(psem)
                    e2.dma_start(out=ft[0:1, 16:17], in_=flag[bass.ts(partner, 1), fs:fs + 1]).then_inc(psem, 16)
                    e2.wait_ge(psem, 16)
                    e2.load(rp, ft[0:1, 16:17])
                    e2.reg_sub(rd, rp, rw)
                    return rd
                with e2.While(cond):
                    e2.nop()

        with nc.Block() as block:
            {"sp": block.sync, "act": block.scalar, "pool": block.gpsimd}[q](body)

    def exchange1(self, l):
        fw, s, m, sh, bd = self.fw, self.s, self.m, self.sh, self.bund
        fw.barrier()
        pp = self.par["pool"]
        me, ot = bass.ts(pp, 1), bass.ts(1 - pp, 1)
        fw.dma("pool", sh["FM"][me].rearrange("o f t -> (o f) t"), bd["FM_l"][ot].rearrange("o f t -> (o f) t"))
        fw.dma("pool", sh["XBC"][me].rearrange("o f t -> (o f) t"), s["XBC"][ot].rearrange("o f t -> (o f) t"))
        fw.dma("pool", sh["V"][me].rearrange("o t f -> (o t) f"), bd["V_l"][:, ot, :].rearrange("t o f -> t (o f)"))
        fw.dma("pool", bd["FM_m"][:, me, :].rearrange("f o t -> f (o t)"), bd["FM_l"][me].rearrange("o f t -> (o f) t"))
        fw.dma("pool", m["XBC"][:, me, :].rearrange("f o t -> f (o t)"), s["XBC"][me].rearrange("o f t -> (o f) t"))
        fw.dma("pool", bd["V_m"][me].rearrange("o t f -> (o t) f"), bd["V_l"][:, me, :].rearrange("t o f -> t (o f)"))
        self.pair_barrier(2 * l)
        ps_ = self.par["sp"]
        ot2 = bass.ts(1 - ps_, 1)
        fw.dma("sp", bd["FM_m"][:, ot2, :].rearrange("f o t -> f (o t)"), sh["FM"][ot2].rearrange("o f t -> (o f) t"))
        fw.dma("sp", m["XBC"][:, ot2, :].rearrange("f o t -> f (o t)"), sh["XBC"][ot2].rearrange("o f t -> (o f) t"))
        fw.dma("sp", bd["V_m"][ot2].rearrange("o t f -> (o t) f"), sh["V"][ot2].rearrange("o t f -> (o t) f"))
        fw.barrier()

    def exchange2(self, l):
        fw, s, m, sh, bd = self.fw, self.s, self.m, self.sh, self.bund
        fw.barrier()
        pa = self.par["act"]
        me, ot = bass.ts(pa, 1), bass.ts(1 - pa, 1)
        fw.dma("act", sh["Y"][me].rearrange("o f t -> (o f) t"), bd["Y_m"][:, ot, :].rearrange("f o t -> f (o t)"))
        fw.dma("act", bd["Y_t"][me].rearrange("o f t -> (o f) t"), bd["Y_m"][:, me, :].rearrange("f o t -> f (o t)"))
        self.pair_barrier(2 * l + 1)
        fw.dma("act", bd["Y_t"][ot].rearrange("o f t -> (o f) t"), sh["Y"][ot].rearrange("o f t -> (o f) t"))
        fw.barrier()

    def phase0(self):
        nc, fw, L, Lh = self.nc, self.fw, self.L, self.Lh
        with ExitStack() as es:
            cv = self.sb(es, "cv", [128, 16], F32)
            sc = self.sb(es, "sc", [128, 16], F32)
            pv0 = self.sb(es, "pv0", [128, Lh * 64], F32)
            wm = [self.sb(es, "wm%d" % k, [128, 8, 512], F32) for k in range(2)]
            mfm = self.sb(es, "mfm", [128, 2, 48], F32)
            mrow = self.sb(es, "mrow", [96, Lh, 128], F32)
            mp = self.sb(es, "mpl", [128, Lh, 4, 2, 8], F32)
            pmod = self.ps(es, "pmod", [128, 48, 2])
            ptr = self.ps(es, "ptr", [96, 128])
            b_cv, b_sc, b_mfm, b_mrow, b_pv0, b_mp = Buf(), Buf(), Buf(), Buf(), Buf(), Buf()
            b_wm = [Buf(), Buf()]
            b_pmod, b_ptr = Buf(True), Buf(True)
            fw.dma("sp", cv[:], self.i["cvec"], writes=[b_cv])
            fw.dma("sp", pv0[:], self.i["pvec0"], writes=[b_pv0])
            fw.op("act", lambda e: e.activation(out=sc[:], in_=cv[:], func=AF.Silu), reads=[b_cv], writes=[b_sc])
            for li in range(Lh):
                for g in range(12):
                    w_, bw = wm[g % 2], b_wm[g % 2]
                    fw.dma("sp", w_[:], self.i["w_mod"][li, :, g * 512:(g + 1) * 512].rearrange("(k p) m -> p k m", p=128),
                           writes=[bw])
                    for jj in range(4):
                        j = g * 4 + jj
                        for k in range(8):
                            fw.op("pe", lambda e, w_=w_, jj=jj, k=k, j=j: e.matmul(
                                pmod[:, j, :], lhsT=w_[:, k, jj * 128:(jj + 1) * 128], rhs=sc[:, 2 * k:2 * k + 2],
                                start=(k == 0), stop=(k == 7)),
                                reads=[bw, b_sc], writes=[b_pmod], sig=(k == 7))
                pvl = pv0[:, li * 64:(li + 1) * 64]
                for w in range(2):
                    fw.op("dve", lambda e, w=w, pvl=pvl: e.tensor_tensor(
                        out=mfm[:, w, :], in0=pmod[:, :, w], in1=pvl[:, 0:48], op=ALU.add),
                        reads=[b_pmod, b_pv0], writes=[b_mfm])
                for w in range(2):
                    fw.op("dve", lambda e, w=w, pvl=pvl, li=li: e.scalar_tensor_tensor(
                        out=mp[:, li, 0, w, :], in0=mfm[:, w, 8:16], scalar=1.0, in1=pvl[:, 48:56],
                        op0=ALU.add, op1=ALU.mult), reads=[b_mfm, b_pv0], writes=[b_mp])
                    fw.op("dve", lambda e, w=w, li=li: e.tensor_copy(out=mp[:, li, 1, w, :], in_=mfm[:, w, 0:8]),
                          reads=[b_mfm], writes=[b_mp])
                    fw.op("dve", lambda e, w=w, pvl=pvl, li=li: e.scalar_tensor_tensor(
                        out=mp[:, li, 2, w, :], in0=mfm[:, w, 32:40], scalar=1.0, in1=pvl[:, 56:64],
                        op0=ALU.add, op1=ALU.mult), reads=[b_mfm, b_pv0], writes=[b_mp])
                    fw.op("dve", lambda e, w=w, li=li: e.tensor_copy(out=mp[:, li, 3, w, :], in_=mfm[:, w, 24:32]),
                          reads=[b_mfm], writes=[b_mp])
                fw.op("pe", lambda e: e.transpose(ptr[:], mfm[:].rearrange("p w j -> p (w j)"), self.cm[:, CM_ID:CM_ID + 128]),
                      reads=[b_mfm, self.b_cm], writes=[b_ptr])
                fw.op("act", lambda e, li=li: e.activation(out=mrow[:, li, :], in_=ptr[:], func=AF.Copy), reads=[b_ptr], writes=[b_mrow])
            pp = self.par["pool"]
            fw.dma("pool", self.s["MODROW"][bass.ts(pp, 1)].rearrange("o l r c -> r (o l) c"), mrow[:], reads=[b_mrow])
            fw.dma("pool", self.s["MODP"][bass.ts(pp, 1)].rearrange("o p c -> p (o c)"), mp[:].rearrange("p l a b c -> p (l a b c)"), reads=[b_mp])
            self.pair_barrier(9)
            for l in range(L):
                fw.dma("sp", self.modp[:, l].rearrange("p a b c -> p (a b c)"), self.s["MODP"][l % 2, :, (l // 2) * 64:(l // 2 + 1) * 64],
                       writes=[self.b_modp])
            fw.barrier()

    def _norm_pipe(self, es, l, kA, kS, src, chunks, tag, keep_x=False):
        fw = self.fw
        nx = 2 if keep_x else 1
        xt = [self.sb(es, tag + "xt%d" % k, [128, 4, D], F32) for k in range(nx)]; b_xt = [Buf() for _ in range(nx)]
        junk = self.sb(es, tag + "junk", [128, D], BF16); b_junk = Buf()
        ss = self.sb(es, tag + "ss", [128, 4], F32); b_ss = Buf()
        rstd = self.sb(es, tag + "rstd", [128, 4], F32); b_rstd = Buf()
        xn = self.sb(es, tag + "xn", [128, 4, D], BF16); b_xn = Buf()
        hT = [self.sb(es, tag + "hT%d" % k, [128, 8, 512], BF16) for k in range(2)]; b_hT = [Buf(), Buf()]
        ptr = [self.ps(es, tag + "tr%d" % k, [128, 1024], BF16) for k in range(2)]; b_ptr = [Buf(True), Buf(True)]
        eps_t = self.sb(es, tag + "eps", [128, 1], F32); b_eps = Buf()
        fw.op("pool", lambda e: e.memset(eps_t[:], 1e-6), writes=[b_eps])
        ident = self.cmb[:, CM_ID:CM_ID + 128]
        st = {"tr": 0}

        def load_x(ci):
            t0, n, w = chunks[ci]
            nt = n // 128
            fw.dma("sp", xt[ci % nx][:, 0:nt, :], src[t0:t0 + n, :].rearrange("(j p) f -> p j f", p=128), writes=[b_xt[ci % nx]])

        def prep(ci):
            t0, n, w = chunks[ci]
            nt = n // 128
            slot = ci % 2
            X, BX = xt[ci % nx], b_xt[ci % nx]
            for j in range(nt):
                fw.op("act", lambda e, j=j: e.activation(out=junk[:], in_=X[:, j, :], func=AF.Square, accum_out=ss[:, j:j + 1]),
                      reads=[BX], writes=[b_junk, b_ss])
            fw.op("act", lambda e: e.activation(out=rstd[:, 0:nt], in_=ss[:, 0:nt], func=AF.Sqrt, bias=eps_t[:], scale=1.0 / D),
                  reads=[b_ss, b_eps], writes=[b_rstd])
            fw.op("dve", lambda e: e.reciprocal(out=rstd[:, 0:nt], in_=rstd[:, 0:nt]), reads=[b_rstd], writes=[b_rstd])
            for j in range(nt):
                fw.op("dve", lambda e, j=j: e.tensor_scalar(out=xn[:, j, :], in0=X[:, j, :], scalar1=rstd[:, j:j + 1],
                                                           scalar2=None, op0=ALU.mult), reads=[BX, b_rstd], writes=[b_xn])
            if ci + 1 < len(chunks):
                load_x(ci + 1)
            for k in range(8):
                p = st["tr"] % 2
                st["tr"] += 1
                for j in range(nt):
                    fw.op("pe", lambda e, p=p, j=j, k=k: e.transpose(ptr[p][:, j * 128:(j + 1) * 128],
                                                                     xn[:, j, k * 128:(k + 1) * 128], ident),
                          reads=[b_xn, self.b_cmb], writes=[b_ptr[p]], sig=(j == nt - 1))
                fw.op("act", lambda e, p=p, k=k: e.activation(
                    out=hT[slot][:, k, 0:n], in_=ptr[p][:, 0:n], func=AF.Identity,
                    scale=self.modp[:, l, kA, w, k:k + 1], bias=self.modp[:, l, kS, w, k:k + 1]),
                    reads=[b_ptr[p], self.b_modp], writes=[b_hT[slot]])

        return dict(hT=hT, b_hT=b_hT, load_x=load_x, prep=prep, xt=xt, b_xt=b_xt)

    def phase1(self, l):
        nc, fw, S, T = self.nc, self.fw, self.S, self.T
        src = self.i["xin"] if l == 0 else self.s["XA"]
        sc = self.s
        with ExitStack() as es:
            W = self.sb(es, "W1", [128, 8, INW], BF16)
            wblocks = [(OFF["qs"], 640), (OFF["qa"], 512), (OFF["ka"], 512), (OFF["xb"], 512), (OFF["gb"], 512),
                       (OFF["va"], 512), (OFF["vs"], 128)] + [(OFF["ga"] + g_ * 512, 512) for g_ in range(6)]
            b_Wb = {}
            for (c0_, cw_) in wblocks:
                bb_ = Buf()
                fw.dma("pool", W[:, :, c0_:c0_ + cw_], self.i["w_in"][l, :, c0_:c0_ + cw_].rearrange("(k p) m -> p k m", p=128), writes=[bb_])
                for c_ in range(c0_, c0_ + cw_, 128):
                    b_Wb[c_] = bb_

            class _BW:
                cur = None
            b_W = None
            np_ = self._norm_pipe(es, l, 0, 1, src, self.chunks, "p1")
            hT, b_hT, load_x, prep = np_["hT"], np_["b_hT"], np_["load_x"], np_["prep"]
            stf = [self.sb(es, "stf%d" % k, [128, 4, 512], BF16) for k in range(3)]; b_stf = [Buf(), Buf(), Buf()]
            stx = self.sb(es, "stx", [128, 4, 512], F32); b_stx = Buf()
            stv = [self.sb(es, "stv%d" % k, [128, 4, 512], BF16) for k in range(2)]; b_stv = [Buf(), Buf()]
            stvs = [self.sb(es, "stvs%d" % k, [128, 4, 128], BF16) for k in range(2)]; b_stvs = [Buf(), Buf()]
            qb = self.sb(es, "qb", [128, 5, 512], BF16); b_qb = [Buf() for _ in range(5)]
            cs = self.sb(es, "cs", [128, 2, 512], F32); b_cs = Buf()
            t1 = self.sb(es, "t1", [128, 512], F32); b_t1 = Buf()
            t2 = self.sb(es, "t2", [128, 512], F32); b_t2 = Buf()
            pac = [self.ps(es, "p1ac%d" % k, [128, 512]) for k in range(4)]; b_pac = [Buf(True) for _ in range(4)]
            prp = self.ps(es, "p1rp", [128, 512]); b_prp = Buf(True)
            ident = self.cmb[:, CM_ID:CM_ID + 128]
            perm = self.cmb[:, CM_PERM:CM_PERM + 128]
            st = {"acc": 0, "stf": 0, "stv": 0}
            chunks = self.chunks

            def acc_fm(ci, col0):
                t0, n, w = chunks[ci]
                slot = ci % 2
                a = st["acc"] % 4
                st["acc"] += 1
                for k in range(8):
                    fw.op("pe", lambda e, a=a, k=k: e.matmul(pac[a][:, 0:n], lhsT=W[:, k, col0:col0 + 128],
                                                             rhs=hT[slot][:, k, 0:n], start=(k == 0), stop=(k == 7)),
                          reads=[b_Wb[col0], b_hT[slot]], writes=[b_pac[a]], sig=(k == 7))
                return pac[a], b_pac[a]

            def fm_group(ci, col0, nch, dst, func=AF.Copy, scale=1.0, f32=False, fh3=True):
                t0, n, w = chunks[ci]
                if f32:
                    stg, bst = stx, b_stx
                else:
                    k_ = st["stf"] % 3
                    st["stf"] += 1
                    stg, bst = stf[k_], b_stf[k_]
                for cc in range(nch):
                    p, bp = acc_fm(ci, col0 + cc * 128)
                    eng = "act" if (func != AF.Copy or cc % 2 == 0) else "dve"
                    if eng == "act":
                        fw.op("act", lambda e, p=p, cc=cc: e.activation(out=stg[:, cc, 0:n], in_=p[:, 0:n], func=func, scale=scale),
                              reads=[bp], writes=[bst])
                    else:
                        fw.op("dve", lambda e, p=p, cc=cc: e.tensor_scalar(out=stg[:, cc, 0:n], in0=p[:, 0:n], scalar1=scale,
                                                                          scalar2=None, op0=ALU.mult),
                              reads=[bp], writes=[bst])
                if fh3:
                    for fh_ in range(2):
                        fw.dma("pool", dst[fh_, :, t0:t0 + n].rearrange("(c p) t -> p c t", p=128), stg[:, 2 * fh_:2 * fh_ + 2, 0:n], reads=[bst])
                else:
                    fw.dma("pool", dst[:, t0:t0 + n].rearrange("(c p) t -> p c t", p=128), stg[:, 0:nch, 0:n], reads=[bst])

            def rope_group(ci):
                t0, n, w = chunks[ci]
                k_ = st["stf"] % 3
                st["stf"] += 1
                stg, bst = stf[k_], b_stf[k_]
                k2 = st["stf"] % 3
                st["stf"] += 1
                stg2, bst2 = stf[k2], b_stf[k2]
                if w == 0:
                    fw.dma("sp", cs[:, 0, :], self.i["ropec"][:, t0:t0 + n], writes=[b_cs])
                    fw.dma("sp", cs[:, 1, :], self.i["ropes"][:, t0:t0 + n], writes=[b_cs])
                for cc in range(5):
                    col, sc_ = (OFF["qs"] + cc * 128, 0.125) if cc < 4 else (OFF["ks"], 1.0)
                    p, bp = acc_fm(ci, col)
                    o_t, o_b, o_c = (stg, bst, cc) if cc < 4 else (stg2, bst2, 0)
                    if w == 1:
                        fw.op("act", lambda e, p=p, o_t=o_t, o_c=o_c, sc_=sc_: e.activation(
                            out=o_t[:, o_c, 0:n], in_=p[:, 0:n], func=AF.Copy, scale=sc_), reads=[bp], writes=[o_b])
                    else:
                        fw.op("act", lambda e, p=p, cc=cc, sc_=sc_: e.activation(out=qb[:, cc, 0:n], in_=p[:, 0:n], func=AF.Copy, scale=sc_),
                              reads=[bp], writes=[b_qb[cc]])
                if w == 0:
                    for cc in range(5):
                        o_t, o_b, o_c = (stg, bst, cc) if cc < 4 else (stg2, bst2, 0)
                        rot(cc, o_t, o_b, o_c, n)
                for fh_ in range(2):
                    fw.dma("pool", sc["QS"][fh_, :, t0:t0 + n].rearrange("(c p) t -> p c t", p=128), stg[:, 2 * fh_:2 * fh_ + 2, 0:n], reads=[bst])
                for fh_ in range(2):
                    fw.dma("pool", sc["KS"][fh_, :, t0:t0 + n], stg2[fh_ * 64:(fh_ + 1) * 64, 0, 0:n], reads=[bst2])

            def rot(q_, ot, ob, oc, n):
                fw.op("pe", lambda e: e.matmul(prp[:, 0:n], lhsT=perm, rhs=qb[:, q_, 0:n], start=True, stop=True),
                      reads=[b_qb[q_], self.b_cmb], writes=[b_prp])
                fw.op("dve", lambda e: e.tensor_tensor(out=t1[:, 0:n], in0=qb[:, q_, 0:n], in1=cs[:, 0, 0:n], op=ALU.mult),
                      reads=[b_qb[q_], b_cs], writes=[b_t1])
                fw.op("dve", lambda e: e.tensor_tensor(out=t2[:, 0:n], in0=prp[:, 0:n], in1=cs[:, 1, 0:n], op=ALU.mult),
                      reads=[b_prp, b_cs], writes=[b_t2])
                fw.op("pool", lambda e: e.tensor_tensor(out=ot[:, oc, 0:n], in0=t1[:, 0:n], in1=t2[:, 0:n], op=ALU.add),
                      reads=[b_t1, b_t2], writes=[ob])

            def tm_group(ci, name, ncol, dst, stgs, bsts):
                t0, n, w = chunks[ci]
                slot = ci % 2
                nt = n // 128
                k_ = st["stv"] % 2
                st["stv"] += 1
                stg, bst = stgs[k_], bsts[k_]
                for j in range(nt):
                    a = st["acc"] % 4
                    st["acc"] += 1
                    for k in range(8):
                        fw.op("pe", lambda e, a=a, k=k, j=j: e.matmul(
                            pac[a][:, 0:ncol], lhsT=hT[slot][:, k, j * 128:(j + 1) * 128],
                            rhs=W[:, k, OFF[name]:OFF[name] + ncol], start=(k == 0), stop=(k == 7)),
                            reads=[b_Wb[OFF[name]], b_hT[slot]], writes=[b_pac[a]], sig=(k == 7))
                    if j % 2 == 0:
                        fw.op("act", lambda e, a=a, j=j: e.activation(out=stg[:, j, :], in_=pac[a][:, 0:ncol], func=AF.Copy),
                              reads=[b_pac[a]], writes=[bst])
                    else:
                        fw.op("dve", lambda e, a=a, j=j: e.tensor_copy(out=stg[:, j, :], in_=pac[a][:, 0:ncol]),
                              reads=[b_pac[a]], writes=[bst])
                hw_ = ncol // 2
                for a_ in range(2):
                    fw.dma("pool", dst[t0:t0 + n, a_, :].rearrange("(j p) f -> p j f", p=128), stg[:, 0:nt, a_ * hw_:(a_ + 1) * hw_], reads=[bst])

            load_x(0)
            prep(0)
            gates = [(gi, half) for gi in range(3) for half in range(2)]
            for ci in range(len(chunks)):
                rope_group(ci)
                fm_group(ci, OFF["qa"], 4, sc["QA"], scale=0.125)
                fm_group(ci, OFF["ka"], 4, sc["KA"])
                fm_group(ci, OFF["xb"], 4, sc["XBC"], f32=True)
                fm_group(ci, OFF["gb"], 4, sc["GB"])
                tm_group(ci, "va", 512, sc["VA"], stv, b_stv)
                tm_group(ci, "vs", 128, sc["VS"], stvs, b_stvs)
                for gn, (gi, half) in enumerate(gates):
                    if gn == 2 and ci + 1 < len(chunks):
                        prep(ci + 1)
                    base = gi * 1024 + half * 512
                    fm_group(ci, OFF["ga"] + base, 4, sc["GT"][base:base + 512], func=AF.Sigmoid, fh3=False)

    def phase_lru(self, l):
        nc, fw, S, T = self.nc, self.fw, self.S, self.T
        sc = self.s
        pvl = self.pv[:, l * NPV:(l + 1) * NPV]

        def rev(ap_):
            n = ap_.ap[-1][1]
            return AP(ap_.tensor, ap_.offset + n - 1, [list(ap_.ap[0]), [-1, n]])

        with ExitStack() as es:
            wbd = self.sb(es, "wbd", [128, 16, 128], BF16); b_wbd = Buf()
            fw.dma("pool", wbd[:], self.i["lruw"][l].rearrange("d a c k m -> k (d a c) m"), writes=[b_wbd])
            sm = self.sb(es, "lsm", [128, 5, 8], F32); b_sm = Buf()
            fw.op("act", lambda e: e.activation(out=sm[:, 4, :], in_=pvl[:, PV_LAM:PV_LAM + 8], func=AF.Exp, scale=-1.0),
                  reads=[self.b_pv], writes=[b_sm])
            sq = self.sb(es, "lsq", [128, 3, 8], F32); b_sq = Buf()
            fw.op("dve", lambda e: e.tensor_scalar(out=sq[:, 0, :], in0=sm[:, 4, :], scalar1=2.0, scalar2=None, op0=ALU.add),
                  reads=[b_sm], writes=[b_sq])
            fw.op("dve", lambda e: e.reciprocal(out=sq[:, 0, :], in_=sq[:, 0, :]), reads=[b_sq], writes=[b_sq])
            fw.op("dve", lambda e: e.tensor_tensor(out=sq[:, 0, :], in0=sq[:, 0, :], in1=sm[:, 4, :], op=ALU.mult),
                  reads=[b_sq, b_sm], writes=[b_sq])
            fw.op("dve", lambda e: e.tensor_tensor(out=sq[:, 1, :], in0=sq[:, 0, :], in1=sq[:, 0, :], op=ALU.mult),
                  reads=[b_sq], writes=[b_sq])
            fw.op("dve", lambda e: e.tensor_scalar(out=sq[:, 2, :], in0=sq[:, 1, :], scalar1=0.2, scalar2=1.0 / 3.0,
                                                   op0=ALU.mult, op1=ALU.add), reads=[b_sq], writes=[b_sq])
            fw.op("dve", lambda e: e.tensor_tensor(out=sq[:, 2, :], in0=sq[:, 2, :], in1=sq[:, 1, :], op=ALU.mult),
                  reads=[b_sq], writes=[b_sq])
            fw.op("dve", lambda e: e.scalar_tensor_tensor(out=sm[:, 4, :], in0=sq[:, 2, :], scalar=1.0, in1=sq[:, 0, :],
                                                          op0=ALU.add, op1=ALU.mult), reads=[b_sq], writes=[b_sm])
            fw.op("dve", lambda e: e.tensor_scalar(out=sm[:, 0, :], in0=sm[:, 4, :], scalar1=-8.0, scalar2=None, op0=ALU.mult),
                  reads=[b_sm], writes=[b_sm])
            fw.op("dve", lambda e: e.tensor_scalar(out=sm[:, 1, :], in0=sm[:, 4, :], scalar1=-16.0, scalar2=None, op0=ALU.mult),
                  reads=[b_sm], writes=[b_sm])
            fw.op("dve", lambda e: e.tensor_scalar(out=sm[:, 2, :], in0=pvl[:, PV_BA:PV_BA + 8], scalar1=0.5, scalar2=None, op0=ALU.mult),
                  reads=[self.b_pv], writes=[b_sm])
            fw.op("dve", lambda e: e.tensor_scalar(out=sm[:, 3, :], in0=pvl[:, PV_BX:PV_BX + 8], scalar1=0.5, scalar2=None, op0=ALU.mult),
                  reads=[self.b_pv], writes=[b_sm])
            self.dump("sm", sm[:].rearrange("p a b -> p (a b)"), [b_sm])
            half = self.sb(es, "lhalf", [128, 512], F32); b_half = Buf()
            fw.op("pool", lambda e: e.memset(half[:], 1.0), writes=[b_half])
            xbt = self.sb(es, "lxb", [128, T], F32); b_xbt = Buf()
            u = self.sb(es, "lu", [128, T], F32); b_u = Buf()
            ubf = self.sb(es, "lubf", [128, T], BF16); b_ubf = Buf()
            hf, b_hf = xbt, b_xbt
            NU = 7
            ut = [[self.sb(es, "lut%d_%d" % (k, j), [128, 512], F32) for j in range(NU)] for k in range(2)]
            b_ut = [[Buf() for _ in range(NU)] for _ in range(2)]
            hr = [self.sb(es, "lhr%d" % k, [128, 512], F32) for k in range(2)]; b_hr = [Buf(), Buf()]
            gbt = [self.sb(es, "lgb%d" % k, [128, 512], BF16) for k in range(2)]; b_gbt = [Buf(), Buf()]
            gt = [self.sb(es, "lgt%d" % k, [128, 512], F32) for k in range(3)]; b_gt = [Buf() for _ in range(3)]
            yst = [self.sb(es, "lys%d" % k, [128, 512], BF16) for k in range(2)]; b_yst = [Buf(), Buf()]
            pp = [[self.ps(es, "lp%d_%d" % (k, j), [128, 512]) for j in range(2)] for k in range(2)]
            b_pp = [[Buf(True), Buf(True)] for _ in range(2)]
            lat = [(c * 512, 512) for c in range(S // 512)]
            ctxs = [(S, CTX)]
            uc = 0
            for ch in range(2):
                rows = slice(ch * 128, (ch + 1) * 128)
                self.ld_nat("sp", lambda a_, n_: xbt[:, a_:a_ + n_], self.m["XBC"], rows, [b_xbt])
                cw = lambda tap: pvl[:, PV_CW + ch * 4 + tap:PV_CW + ch * 4 + tap + 1]
                for (a0, n0) in ((0, S), (S, CTX)):
                    fw.op("dve", lambda e, a0=a0, n0=n0: e.tensor_scalar(
                        out=u[:, a0:a0 + n0], in0=xbt[:, a0:a0 + n0], scalar1=cw(2), scalar2=pvl[:, PV_CB + ch:PV_CB + ch + 1],
                        op0=ALU.mult, op1=ALU.add), reads=[b_xbt, self.b_pv], writes=[b_u])
                    for tap, so, do, ln in ((0, 0, 2, n0 - 2), (1, 0, 1, n0 - 1), (3, 1, 0, n0 - 1)):
                        fw.op("dve", lambda e, a0=a0, tap=tap, so=so, do=do, ln=ln: e.scalar_tensor_tensor(
                            out=u[:, a0 + do:a0 + do + ln], in0=xbt[:, a0 + so:a0 + so + ln], scalar=cw(tap),
                            in1=u[:, a0 + do:a0 + do + ln], op0=ALU.mult, op1=ALU.add),
                            reads=[b_xbt, self.b_pv, b_u], writes=[b_u])
                fw.op("act", lambda e: e.activation(out=ubf[:], in_=u[:], func=AF.Copy), reads=[b_u], writes=[b_ubf])
                ulist = [(d_, t0_, n_) for d_ in range(2) for (t0_, n_) in ((ctxs + lat) if d_ == 0 else (ctxs + lat[::-1]))]

                def gate_mm(ui_):
                    d_, t0_, n_ = ulist[ui_]
                    kk_ = (uc + ui_) % 2
                    for ax in range(2):
                        fw.op("pe", lambda e, ax=ax: e.matmul(
                            pp[kk_][ax][:, 0:n_], lhsT=wbd[:, (d_ * 2 + ax) * 4 + ch, :], rhs=ubf[:, t0_:t0_ + n_],
                            start=True, stop=True), reads=[b_wbd, b_ubf], writes=[b_pp[kk_][ax]])

                gate_mm(0)
                uidx = 0
                for d in range(2):
                    segs = (ctxs + lat) if d == 0 else (ctxs + lat[::-1])
                    prev = None
                    col = d * 4 + ch
                    for (t0, n) in segs:
                        k_ = (uc + uidx) % 2
                        if uidx + 1 < len(ulist):
                            gate_mm(uidx + 1)
                        uidx += 1
                        T_ = ut[k_]
                        B_ = b_ut[k_]
                        fw.op("act", lambda e, k_=k_: e.activation(out=T_[0][:, 0:n], in_=pp[k_][0][:, 0:n], func=AF.Tanh,
                                                                scale=0.5, bias=sm[:, 2, col:col + 1]),
                              reads=[b_pp[k_][0], b_sm], writes=[B_[0]])
                        fw.op("act", lambda e, k_=k_: e.activation(out=T_[1][:, 0:n], in_=pp[k_][1][:, 0:n], func=AF.Tanh,
                                                                scale=0.5, bias=sm[:, 3, col:col + 1]),
                              reads=[b_pp[k_][1], b_sm], writes=[B_[1]])
                        fw.op("act", lambda e: e.activation(out=T_[2][:, 0:n], in_=T_[0][:, 0:n], func=AF.Exp,
                                                            scale=sm[:, 0, col:col + 1], bias=sm[:, 0, col:col + 1]),
                              reads=[B_[0], b_sm], writes=[B_[2]])
                        fw.op("act", lambda e: e.activation(out=T_[3][:, 0:n], in_=T_[0][:, 0:n], func=AF.Exp,
                                                            scale=sm[:, 1, col:col + 1], bias=sm[:, 1, col:col + 1]),
                              reads=[B_[0], b_sm], writes=[B_[3]])
                        fw.op("act", lambda e: e.activation(out=T_[4][:, 0:n], in_=T_[3][:, 0:n], func=AF.Sqrt, scale=-1.0, bias=half[:, 0:1]),
                              reads=[B_[3], b_half], writes=[B_[4]])
                        fw.op("dve", lambda e: e.scalar_tensor_tensor(out=T_[5][:, 0:n], in0=T_[1][:, 0:n], scalar=1.0,
                                                                      in1=u[:, t0:t0 + n], op0=ALU.add, op1=ALU.mult),
                              reads=[B_[1], b_u], writes=[B_[5]])
                        fw.op("dve", lambda e: e.scalar_tensor_tensor(out=T_[6][:, 0:n], in0=T_[5][:, 0:n], scalar=0.5,
                                                                      in1=T_[4][:, 0:n], op0=ALU.mult, op1=ALU.mult),
                              reads=[B_[5], B_[4]], writes=[B_[6]])
                        if uc == 1:
                            for q_ in range(7):
                                self.dump("T%d" % q_, T_[q_][:, 0:n], [B_[q_]])
                            self.dump("u", u[:, t0:t0 + n], [b_u])
                        if d == 0:
                            init = 0.0 if prev is None else hf[:, prev[0] + prev[1] - 1:prev[0] + prev[1]]
                            fw.op("dve", lambda e, init=init: e.tensor_tensor_scan(
                                out=hf[:, t0:t0 + n], data0=T_[2][:, 0:n], data1=T_[6][:, 0:n], initial=init,
                                op0=ALU.mult, op1=ALU.add), reads=[B_[2], B_[6], b_hf], writes=[b_hf])
                        else:
                            init = 0.0 if prev is None else hr[1 - k_][:, 0:1]
                            rd = [B_[2], B_[6]] + ([] if prev is None else [b_hr[1 - k_]])
                            fw.op("dve", lambda e, init=init, k_=k_: e.tensor_tensor_scan(
                                out=rev(hr[k_][:, 0:n]), data0=rev(T_[2][:, 0:n]), data1=rev(T_[6][:, 0:n]), initial=init,
                                op0=ALU.mult, op1=ALU.add), reads=rd, writes=[b_hr[k_]])
                            G, BG = gbt[k_], b_gbt[k_]
                            for (h_, tl_, nn_, o_) in self.tmap(t0, n):
                                fw.dma("sp", G[:, o_:o_ + nn_], self.m["GB"][rows, h_, tl_:tl_ + nn_], writes=[BG])
                            fw.op("pool", lambda e, G=G: e.tensor_tensor(out=gt[0][:, 0:n], in0=G[:, 0:n], in1=G[:, 0:n], op=ALU.mult),
                                  reads=[BG], writes=[b_gt[0]])
                            fw.op("pool", lambda e: e.tensor_scalar(out=gt[0][:, 0:n], in0=gt[0][:, 0:n], scalar1=0.044715, scalar2=1.0,
                                                                    op0=ALU.mult, op1=ALU.add), reads=[b_gt[0]], writes=[b_gt[0]])
                            fw.op("pool", lambda e, G=G: e.tensor_tensor(out=gt[0][:, 0:n], in0=gt[0][:, 0:n], in1=G[:, 0:n], op=ALU.mult),
                                  reads=[b_gt[0], BG], writes=[b_gt[0]])
                            fw.op("act", lambda e: e.activation(out=gt[1][:, 0:n], in_=gt[0][:, 0:n], func=AF.Tanh, scale=0.7978845608028654),
                                  reads=[b_gt[0]], writes=[b_gt[1]])
                            fw.op("pool", lambda e: e.tensor_scalar(out=gt[1][:, 0:n], in0=gt[1][:, 0:n], scalar1=0.5, scalar2=0.5,
                                                                    op0=ALU.mult, op1=ALU.add), reads=[b_gt[1]], writes=[b_gt[1]])
                            fw.op("pool", lambda e, G=G: e.tensor_tensor(out=gt[1][:, 0:n], in0=gt[1][:, 0:n], in1=G[:, 0:n], op=ALU.mult),
                                  reads=[b_gt[1], BG], writes=[b_gt[1]])
                            fw.op("pool", lambda e, k_=k_: e.tensor_tensor(out=gt[2][:, 0:n], in0=hf[:, t0:t0 + n], in1=hr[k_][:, 0:n], op=ALU.add),
                                  reads=[b_hf, b_hr[k_]], writes=[b_gt[2]])
                            Y, BY = yst[k_], b_yst[k_]
                            fw.op("pool", lambda e, Y=Y: e.tensor_tensor(out=Y[:, 0:n], in0=gt[2][:, 0:n], in1=gt[1][:, 0:n], op=ALU.mult),
                                  reads=[b_gt[2], b_gt[1]], writes=[BY])
                            for (h_, tl_, nn_, o_) in self.tmap(t0, n):
                                fw.dma("pool", self.m["YB"][rows, h_, tl_:tl_ + nn_], Y[:, o_:o_ + nn_], reads=[BY])
                        prev = (t0, n)
                        yield
                uc += len(ulist)

    def phase_na(self, l):
        nc, fw, S, T, R = self.nc, self.fw, self.S, self.T, self.R
        sc = self.s
        NT = T // 128
        NQ = R // 2
        upd_ctx = l < self.L - 1
        with ExitStack() as es:
            KT = self.sb(es, "nKT", [128, T], BF16); b_KT = Buf()
            Qm = [self.sb(es, "nQ%d" % k, [128, T], BF16) for k in range(2)]; b_Qm = [Buf(), Buf()]
            Vp = self.sb(es, "nV", [128, NT, 2, 128], BF16); b_Vp = Buf()
            onesp = self.sb(es, "nones", [128, 2, 128], BF16); b_ones = Buf()
            btf = self.sb(es, "nbtf", [128, 1280], F32); b_btf = Buf()
            bt = self.sb(es, "nbt", [128, 5, 1280], BF16); b_bt = Buf()
            P = [[self.sb(es, "nP%d_%d" % (a, b), [128, 7, 128], BF16) for b in range(2)] for a in range(2)]
            b_P = [[Buf(), Buf()] for _ in range(2)]
            rd = self.sb(es, "nrd", [128, 128], F32); b_rd = Buf()
            yst = [self.sb(es, "nys%d" % k, [128, 512], BF16) for k in range(2)]; b_yst = [Buf(), Buf()]
            Sp = [self.ps(es, "nS%d" % k, [128, 1024]) for k in range(2)]; b_Sp = [Buf(True), Buf(True)]
            po = [self.ps(es, "npo%d" % k, [128, 512]) for k in range(2)]; b_po = [Buf(True), Buf(True)]
            pd = [self.ps(es, "npd%d" % k, [128, 512]) for k in range(2)]; b_pd = [Buf(True), Buf(True)]
            ident = self.cmb[:, CM_ID:CM_ID + 128]
            fw.op("pool", lambda e: e.memset(Vp[:], 0.0), writes=[b_Vp])
            fw.op("pool", lambda e: e.memset(onesp[:], 0.0), writes=[b_ones])
            for h2 in range(2):
                fw.op("pool", lambda e, h2=h2: e.memset(onesp[:, h2, h2 * 64:(h2 + 1) * 64], 1.0), writes=[b_ones])
                fw.op("pool", lambda e, h2=h2: e.memset(Qm[h2][(1 - h2) * 64:(2 - h2) * 64, :], 0.0), writes=[b_Qm[h2]])

            def kvtile(q, j):
                if q is None:
                    return S // 128 + j
                kb = min(max(2 * q - 4, 0), R - 10)
                return kb // 2 + j if j < 5 else S // 128 + (j - 5)

            def variant(q):
                return {0: 1, 1: 2, NQ - 2: 3, NQ - 1: 4}.get(q, 0)

            SLT = self.SL // 128
            for hp in range(2):
                self.ld_nat("sp", lambda a_, n_: KT[:, a_:a_ + n_], self.m["KA"], slice(hp * 128, (hp + 1) * 128), [b_KT])
                for h2 in range(2):
                    r0 = hp * 128 + h2 * 64
                    self.ld_nat("sp", lambda a_, n_, h2=h2: Qm[h2][h2 * 64:(h2 + 1) * 64, a_:a_ + n_], self.m["QA"], slice(r0, r0 + 64), [b_Qm[h2]])
                    for h_ in range(2):
                        fw.dma("sp", Vp[:, h_ * SLT:(h_ + 1) * SLT, h2, h2 * 64:(h2 + 1) * 64],
                               self.m["VA"][h_, 0:self.SL, r0:r0 + 64].rearrange("(j p) f -> p j f", p=128), writes=[b_Vp])
                        fw.dma("sp", Vp[:, S // 128 + h_, h2, h2 * 64:(h2 + 1) * 64], self.m["VA"][h_, self.SL:self.TL, r0:r0 + 64], writes=[b_Vp])
                for v_ in range(5):
                    fw.dma("sp", btf[:], self.i["nabias"][l, v_, hp], writes=[b_btf])
                    fw.op("act", lambda e, v_=v_: e.activation(out=bt[:, v_, :], in_=btf[:], func=AF.Exp), reads=[b_btf], writes=[b_bt])
                items = [("lat", q) for q in range(NQ)] + ([("ctx", b) for b in range(2)] if upd_ctx else [])

                def qk(it, pi):
                    kind, q = it
                    q0 = q * 128 if kind == "lat" else S + q * 128
                    ntile = 7 if kind == "lat" else 2
                    for h2 in range(2):
                        for j in range(ntile):
                            kt = kvtile(q if kind == "lat" else None, j)
                            loc = kind == "lat" and j < 5
                            fw.op("pe", lambda e, h2=h2, j=j, kt=kt: e.matmul(
                                Sp[h2][:, j * 128:(j + 1) * 128], lhsT=KT[:, kt * 128:(kt + 1) * 128],
                                rhs=Qm[h2][:, q0:q0 + 128], start=True, stop=True),
                                reads=[b_KT, b_Qm[h2]], writes=[b_Sp[h2]], sig=(j == ntile - 1))
                        fw.op("act", lambda e, h2=h2: e.activation(
                            out=P[pi][h2][:, 0:ntile, :].rearrange("p a b -> p (a b)"), in_=Sp[h2][:, 0:ntile * 128], func=AF.Exp),
                            reads=[b_Sp[h2]], writes=[b_P[pi][h2]])
                        if kind == "lat":
                            v = variant(q)
                            fw.op("dve" if h2 == 0 else "pool", lambda e, h2=h2, v=v: e.tensor_tensor(
                                out=P[pi][h2][:, 0:5, :].rearrange("p a b -> p (a b)"), in0=P[pi][h2][:, 0:5, :].rearrange("p a b -> p (a b)"),
                                in1=bt[:, v, h2 * 640:(h2 + 1) * 640], op=ALU.mult), reads=[b_P[pi][h2], b_bt], writes=[b_P[pi][h2]])

                def pv(it, pi, idx):
                    kind, q = it
                    q0 = q * 128 if kind == "lat" else S + q * 128
                    ntile = 7 if kind == "lat" else 2
                    o = idx % 2
                    nmm = 2 * ntile
                    for (dst, bdst, isden) in ((po[o], b_po[o], False), (pd[o], b_pd[o], True)):
                        c = 0
                        for h2 in range(2):
                            for j in range(ntile):
                                kt = kvtile(q if kind == "lat" else None, j)
                                lh = onesp[:, h2, :] if isden else Vp[:, kt, h2, :]
                                fw.op("pe", lambda e, lh=lh, h2=h2, j=j, c=c, dst=dst: e.matmul(
                                    dst[:, 0:128], lhsT=lh, rhs=P[pi][h2][:, j, :], start=(c == 0), stop=(c == nmm - 1)),
                                    reads=[b_ones if isden else b_Vp, b_P[pi][h2]], writes=[bdst], sig=(c == nmm - 1))
                                c += 1
                    fw.op("dve", lambda e: e.reciprocal(out=rd[:], in_=pd[o][:, 0:128]), reads=[b_pd[o]], writes=[b_rd])
                    ys = (idx // 4) % 2
                    col = (idx % 4) * 128
                    fw.op("dve", lambda e: e.tensor_tensor(out=yst[ys][:, col:col + 128], in0=po[o][:, 0:128], in1=rd[:], op=ALU.mult),
                          reads=[b_po[o], b_rd], writes=[b_yst[ys]])
                    last = (idx == len(items) - 1)
                    if idx % 4 == 3 or last:
                        nb = idx % 4 + 1
                        first_it = items[idx - nb + 1]
                        t0 = first_it[1] * 128 if first_it[0] == "lat" else S + first_it[1] * 128
                        for (h_, tl_, nn_, o_) in self.tmap(t0, nb * 128):
                            fw.dma("pool", self.m["YA"][hp * 128:(hp + 1) * 128, h_, tl_:tl_ + nn_], yst[ys][:, o_:o_ + nn_], reads=[b_yst[ys]])

                qk(items[0], 0)
                for idx, it in enumerate(items):
                    if idx + 1 < len(items):
                        qk(items[idx + 1], (idx + 1) % 2)
                    pv(it, idx % 2, idx)

    def phase_swa(self, l):
        nc, fw, S, T = self.nc, self.fw, self.S, self.T
        sc = self.s
        NT = T // 128
        NB = S // 128
        upd_ctx = l < self.L - 1
        pvl = self.pv[:, l * NPV:(l + 1) * NPV]
        with ExitStack() as es:
            KSb = self.sb(es, "sK", [128, T], BF16); b_K = Buf()
            Qp = [self.sb(es, "sQ%d" % a, [128, 4, 512], BF16) for a in range(2)]; b_Qp = [Buf(), Buf()]
            Vp = self.sb(es, "sV", [128, NT, 128], BF16); b_Vp = Buf()
            onesp = self.sb(es, "sones", [128, 128], BF16); b_ones = Buf()
            esk = self.sb(es, "sesk", [128, 1], F32); b_esk = Buf()
            stab = self.sb(es, "sstab", [128, 2, 256], BF16); b_stab = Buf()
            P = [self.sb(es, "sP%d" % a, [128, 5, 256], BF16) for a in range(2)]; b_P = [Buf(), Buf()]
            rd = self.sb(es, "srd", [128, 256], F32); b_rd = Buf()
            yst = [[self.sb(es, "sys%d_%d" % (a, hh), [128, 2, 512], BF16) for hh in range(2)] for a in range(2)]
            b_yst = [[Buf(), Buf()] for _ in range(2)]
            Sg0 = self.ps(es, "sS0", [128, 1536]); b_Sg0 = Buf(True)
            Sg, b_Sg = [Sg0, Sg0], [b_Sg0, b_Sg0]
            pod = self.ps(es, "spod", [128, 512]); b_pod = Buf(True)
            po, pd, b_po, b_pd = pod[:, 0:256], pod[:, 256:512], b_pod, b_pod
            ident = self.cmb[:, CM_ID:CM_ID + 128]
            esel = self.cmb[:, CM_ESEL:CM_ESEL + 128]
            masks = {0: self.cmb[:, CM_MPREV:CM_MPREV + 256], 2: self.cmb[:, CM_MNEXT:CM_MNEXT + 256]}
            fw.op("pool", lambda e: e.memset(Vp[:], 0.0), writes=[b_Vp])
            fw.op("pool", lambda e: e.memset(onesp[:], 0.0), writes=[b_ones])
            fw.op("pool", lambda e: e.memset(onesp[:, 0:64], 1.0), writes=[b_ones])
            fw.op("pool", lambda e: e.memset(KSb[64:128, :], 0.0), writes=[b_K])
            for a in range(2):
                fw.op("pool", lambda e, a=a: e.memset(Qp[a][64:128, :, :], 0.0), writes=[b_Qp[a]])
            fw.op("act", lambda e: e.activation(out=esk[:], in_=pvl[:, PV_SINK:PV_SINK + 1], func=AF.Exp), reads=[self.b_pv], writes=[b_esk])
            for hh in range(2):
                c0 = CM_SEL0 if hh == 0 else CM_SEL1
                fw.op("dve", lambda e, hh=hh, c0=c0: e.tensor_scalar(out=stab[:, hh, :], in0=self.cm[:, c0:c0 + 256], scalar1=esk[:, 0:1],
                                                                    scalar2=None, op0=ALU.mult), reads=[self.b_cm, b_esk], writes=[b_stab])
            SLT = self.SL // 128
            self.ld_nat("sp", lambda a_, n_: KSb[0:64, a_:a_ + n_], self.m["KS"], slice(0, 64), [b_K])
            for h_ in range(2):
                fw.dma("sp", Vp[:, h_ * SLT:(h_ + 1) * SLT, 0:64], self.m["VS"][h_, 0:self.SL, :].rearrange("(j p) f -> p j f", p=128), writes=[b_Vp])
                fw.dma("sp", Vp[:, S // 128 + h_, 0:64], self.m["VS"][h_, self.SL:self.TL, :], writes=[b_Vp])
            sbs = [(c * 512, 4, False) for c in range(S // 512)] + ([(S, 2, True)] if upd_ctx else [])

            def load_q(si):
                t0, nb, isc = sbs[si]
                a = si % 2
                for r in range(4):
                    for (h_, tl_, nn_, o_) in self.tmap(t0, nb * 128):
                        fw.dma("sp", Qp[a][0:64, r, o_:o_ + nn_], self.m["QS"][r * 64:(r + 1) * 64, h_, tl_:tl_ + nn_], writes=[b_Qp[a]])

            units = []
            for si, (t0, nb, isc) in enumerate(sbs):
                for nl in range(nb):
                    for hh in range(2):
                        units.append((si, nl, hh))

            def tiles(si, nl):
                t0, nb, isc = sbs[si]
                res = []
                if not isc:
                    n = t0 // 128 + nl
                    if n >= 1:
                        res.append((0, n - 1))
                    res.append((1, n))
                    if n <= NB - 2:
                        res.append((2, n + 1))
                res.append((3, S // 128))
                res.append((4, S // 128 + 1))
                return res

            def qk(ui, pi):
                si, nl, hh = units[ui]
                a = si % 2
                tl = tiles(si, nl)
                g = pi
                for (j, kt) in tl:
                    mk = masks.get(j)
                    fw.op("pe", lambda e, j=j, kt=kt, mk=mk: e.matmul(
                        Sg[g][:, j * 256:(j + 1) * 256], lhsT=KSb[:, kt * 128:(kt + 1) * 128],
                        rhs=Qp[a][:, 2 * hh:2 * hh + 2, nl * 128:(nl + 1) * 128], start=True, stop=(mk is None)),
                        reads=[b_K, b_Qp[a]], writes=[b_Sg[g]], sig=(mk is None and j == 4))
                    if mk is not None:
                        fw.op("pe", lambda e, j=j, mk=mk: e.matmul(
                            Sg[g][:, j * 256:(j + 1) * 256], lhsT=ident, rhs=mk, start=False, stop=True),
                            reads=[self.b_cmb], writes=[b_Sg[g]], sig=False)
                js = [j for (j, _) in tl]
                runs = []
                for j in js:
                    if runs and runs[-1][1] == j:
                        runs[-1][1] = j + 1
                    else:
                        runs.append([j, j + 1])
                for (j0, j1) in runs:
                    fw.op("act", lambda e, j0=j0, j1=j1: e.activation(
                        out=P[pi][:, j0:j1, :].rearrange("p a b -> p (a b)"), in_=Sg[g][:, j0 * 256:j1 * 256], func=AF.Exp),
                        reads=[b_Sg[g]], writes=[b_P[pi]])

            def pv(ui, pi):
                si, nl, hh = units[ui]
                t0, nb, isc = sbs[si]
                a = si % 2
                tl = tiles(si, nl)
                nmm = len(tl)
                for isden in (False, True):
                    dst, bdst = (pd, b_pd) if isden else (po, b_po)
                    for c, (j, kt) in enumerate(tl):
                        lh = onesp[:] if isden else Vp[:, kt, :]
                        fw.op("pe", lambda e, lh=lh, j=j, c=c, dst=dst, isden=isden: e.matmul(
                            dst, lhsT=lh, rhs=P[pi][:, j, :], start=(c == 0), stop=(c == nmm - 1 and not isden)),
                            reads=[b_ones if isden else b_Vp, b_P[pi]], writes=[bdst], sig=(c == nmm - 1 and not isden))
                    if isden:
                        fw.op("pe", lambda e: e.matmul(pd, lhsT=esel, rhs=stab[:, hh, :], start=False, stop=True),
                              reads=[self.b_cmb, b_stab], writes=[b_pd])
                fw.op("dve", lambda e: e.reciprocal(out=rd[0:64, :], in_=pod[0:64, 256:512]), reads=[b_pd], writes=[b_rd])
                Y, BY = yst[a][hh], b_yst[a][hh]
                fw.op("dve", lambda e: e.tensor_tensor(out=Y[0:64, :, nl * 128:(nl + 1) * 128],
                                                       in0=pod[0:64, 0:256].rearrange("p (r t) -> p r t", r=2),
                                                       in1=rd[0:64, :].rearrange("p (r t) -> p r t", r=2), op=ALU.mult),
                      reads=[b_po, b_rd], writes=[BY])
                if nl == nb - 1:
                    for rr in range(2):
                        r_ = 2 * hh + rr
                        for (h_, tl_, nn_, o_) in self.tmap(t0, nb * 128):
                            fw.dma("pool", self.m["YC"][r_ * 64:(r_ + 1) * 64, h_, tl_:tl_ + nn_], Y[0:64, rr, o_:o_ + nn_], reads=[BY])

            load_q(0)
            qk(0, 0)
            for ui in range(len(units)):
                si, nl, hh = units[ui]
                if nl == 0 and hh == 0 and si + 1 < len(sbs):
                    load_q(si + 1)
                if ui + 1 < len(units):
                    qk(ui + 1, (ui + 1) % 2)
                pv(ui, ui % 2)
                yield

    def _gate_tiles(self, es, l, grp, tag):
        fw = self.fw
        G = self.sb(es, tag + "G", [128, 2, D], F32); b_G = Buf()
        for w in range(2):
            row = self.s["MODROW"][l % 2, l // 2, w * 48 + grp * 8:w * 48 + grp * 8 + 8, :].rearrange("j p -> (j p)")
            fw.dma("sp", G[:, w, :], row.partition_broadcast(128), writes=[b_G])
        return G, b_G

    def phase3(self, l):
        nc, fw, S, T = self.nc, self.fw, self.S, self.T
        sc = self.s
        src = self.i["xin"] if l == 0 else sc["XA"]
        chunks = self.chunks if l < self.L - 1 else self.chunks[:-1]
        with ExitStack() as es:
            Wb = self.sb(es, "m3Wb", [128, 12, D], BF16); b_Wb = Buf()
            Wo = self.sb(es, "m3Wo", [128, 8, D], BF16); b_Wo = Buf()
            b_Wbs = [Buf() for _ in range(4)]
            for q4 in range(4):
                fw.dma("pool", Wb[:, :, q4 * 256:(q4 + 1) * 256], self.i["w_branch"][l, :, q4 * 256:(q4 + 1) * 256].rearrange("(k p) m -> p k m", p=128),
                       writes=[b_Wbs[q4]])
            fw.dma("pool", Wo[:], self.i["w_out"][l].rearrange("(k p) m -> p k m", p=128), writes=[b_Wo])
            G, b_G = self._gate_tiles(es, l, 2, "m3")
            Y = [self.sb(es, "m3Y%d" % k, [128, 12, 512], BF16) for k in range(2)]; b_Y = [Buf(), Buf()]
            GT = [self.sb(es, "m3GT%d" % k, [128, 24, 512], BF16) for k in range(2)]; b_GT = [Buf(), Buf()]
            X = [self.sb(es, "m3X%d" % k, [128, 4, D], F32) for k in range(2)]; b_X = [Buf(), Buf()]
            mT = [self.sb(es, "m3mT%d" % k, [128, 8, 512], BF16) for k in range(2)]; b_mT = [Buf(), Buf()]
            tt = [[self.sb(es, "m3t%d_%d" % (a, k), [128, 512], F32) for k in range(3)] for a in range(2)]
            b_tt = [[Buf() for _ in range(3)] for _ in range(2)]
            pb = [self.ps(es, "m3pb%d" % k, [128, 512]) for k in range(6)]; b_pb = [Buf(True) for _ in range(6)]
            pq = [self.ps(es, "m3pq%d" % k, [128, 512]) for k in range(2)]; b_pq = [Buf(True), Buf(True)]

            def load(ci):
                t0, n, w = chunks[ci]
                a = ci % 2
                for bi, nm in enumerate(("YA", "YB", "YC")):
                    for fh_ in range(2):
                        fw.dma("sp", Y[a][:, 4 * bi + 2 * fh_:4 * bi + 2 * fh_ + 2, 0:n],
                               sc[nm][fh_, :, t0:t0 + n].rearrange("(c p) t -> p c t", p=128), writes=[b_Y[a]])
                fw.dma("sp", GT[a][:, :, 0:n], sc["GT"][:, t0:t0 + n].rearrange("(c p) t -> p c t", p=128), writes=[b_GT[a]])
                fw.dma("sp", X[a][:, 0:n // 128, :], src[t0:t0 + n, :].rearrange("(j p) f -> p j f", p=128), writes=[b_X[a]])

            load(0)
            cnt = 0
            for ci, (t0, n, w) in enumerate(chunks):
                a = ci % 2
                nt = n // 128
                if ci + 1 < len(chunks):
                    load(ci + 1)
                for oc in range(8):
                    s3 = (oc % 2) * 3
                    tset = oc % 2
                    for bi in range(3):
                        for k in range(4):
                            fw.op("pe", lambda e, bi=bi, k=k, oc=oc, s3=s3: e.matmul(
                                pb[s3 + bi][:, 0:n], lhsT=Wb[:, 4 * bi + k, oc * 128:(oc + 1) * 128], rhs=Y[a][:, 4 * bi + k, 0:n],
                                start=(k == 0), stop=(k == 3)), reads=[b_Wbs[oc // 2], b_Y[a]], writes=[b_pb[s3 + bi]], sig=(k == 3))
                    for bi in range(3):
                        fw.op("dve", lambda e, bi=bi, oc=oc, s3=s3, tset=tset: e.tensor_tensor(
                            out=tt[tset][bi][:, 0:n], in0=pb[s3 + bi][:, 0:n], in1=GT[a][:, bi * 8 + oc, 0:n], op=ALU.mult),
                            reads=[b_pb[s3 + bi], b_GT[a]], writes=[b_tt[tset][bi]])
                    fw.op("pool", lambda e, tset=tset: e.tensor_tensor(out=tt[tset][0][:, 0:n], in0=tt[tset][0][:, 0:n], in1=tt[tset][1][:, 0:n], op=ALU.add),
                          reads=[b_tt[tset][0], b_tt[tset][1]], writes=[b_tt[tset][0]])
                    fw.op("pool", lambda e, tset=tset, oc=oc: e.tensor_tensor(out=mT[a][:, oc, 0:n], in0=tt[tset][0][:, 0:n], in1=tt[tset][2][:, 0:n], op=ALU.add),
                          reads=[b_tt[tset][0], b_tt[tset][2]], writes=[b_mT[a]])
                for j in range(nt):
                    for hf_ in range(2):
                        q_ = cnt % 2
                        cnt += 1
                        for k in range(8):
                            fw.op("pe", lambda e, q_=q_, k=k, j=j, hf_=hf_: e.matmul(
                                pq[q_][:, :], lhsT=mT[a][:, k, j * 128:(j + 1) * 128], rhs=Wo[:, k, hf_ * 512:(hf_ + 1) * 512],
                                start=(k == 0), stop=(k == 7)), reads=[b_mT[a], b_Wo], writes=[b_pq[q_]], sig=(k == 7))
                        tq = tt[q_][0]
                        fw.op("dve", lambda e, q_=q_, hf_=hf_, tq=tq: e.tensor_tensor(out=tq[:], in0=pq[q_][:], in1=G[:, w, hf_ * 512:(hf_ + 1) * 512], op=ALU.mult),
                              reads=[b_pq[q_], b_G], writes=[b_tt[q_][0]])
                        fw.op("pool", lambda e, j=j, hf_=hf_, tq=tq: e.tensor_tensor(
                            out=X[a][:, j, hf_ * 512:(hf_ + 1) * 512], in0=X[a][:, j, hf_ * 512:(hf_ + 1) * 512], in1=tq[:], op=ALU.add),
                            reads=[b_X[a], b_tt[q_][0]], writes=[b_X[a]])
                fw.dma("pool", sc["XB"][t0:t0 + n, :].rearrange("(j p) f -> p j f", p=128), X[a][:, 0:nt, :], reads=[b_X[a]])

    def phase4a(self, l):
        nc, fw, S, T = self.nc, self.fw, self.S, self.T
        sc = self.s
        chunks = self.chunks if l < self.L - 1 else self.chunks[:-1]
        with ExitStack() as es:
            W = self.sb(es, "f1W", [128, 8, DFF], BF16); b_W = [Buf() for _ in range(8)]
            for og_ in range(8):
                fw.dma("pool", W[:, :, og_ * 512:(og_ + 1) * 512], self.i["w_ff1"][l, :, og_ * 512:(og_ + 1) * 512].rearrange("(k p) m -> p k m", p=128),
                       writes=[b_W[og_]])
            np_ = self._norm_pipe(es, l, 2, 3, sc["XB"], chunks, "f1")
            hT, b_hT, load_x, prep = np_["hT"], np_["b_hT"], np_["load_x"], np_["prep"]
            tmp = [self.sb(es, "f1t%d" % k, [128, 512], BF16) for k in range(2)]; b_tmp = [Buf(), Buf()]
            stg = [self.sb(es, "f1s%d" % k, [128, 4, 512], BF16) for k in range(3)]; b_stg = [Buf() for _ in range(3)]
            pac = [self.ps(es, "f1ac%d" % k, [128, 512]) for k in range(4)]; b_pac = [Buf(True) for _ in range(4)]
            load_x(0)
            prep(0)
            acc = 0
            sg = 0
            for ci, (t0, n, w) in enumerate(chunks):
                slot = ci % 2
                for og in range(8):
                    S_, BS_ = stg[sg % 3], b_stg[sg % 3]
                    sg += 1
                    if og == 4 and ci + 1 < len(chunks):
                        prep(ci + 1)
                    for cc in range(4):
                        oc = og * 4 + cc
                        a = acc % 4
                        acc += 1
                        for k in range(8):
                            fw.op("pe", lambda e, a=a, k=k, oc=oc: e.matmul(pac[a][:, 0:n], lhsT=W[:, k, oc * 128:(oc + 1) * 128],
                                                                            rhs=hT[slot][:, k, 0:n], start=(k == 0), stop=(k == 7)),
                                  reads=[b_W[og], b_hT[slot]], writes=[b_pac[a]], sig=(k == 7))
                        tq = cc % 2
                        if cc % 2 == 0:
                            fw.op("act", lambda e, a=a, tq=tq: e.activation(out=tmp[tq][:, 0:n], in_=pac[a][:, 0:n], func=AF.Relu),
                                  reads=[b_pac[a]], writes=[b_tmp[tq]])
                        else:
                            fw.op("dve", lambda e, a=a, tq=tq: e.tensor_scalar(out=tmp[tq][:, 0:n], in0=pac[a][:, 0:n], scalar1=0.0, scalar2=None,
                                                                              op0=ALU.max), reads=[b_pac[a]], writes=[b_tmp[tq]])
                        fw.op("pool", lambda e, tq=tq, cc=cc, S_=S_: e.tensor_tensor(out=S_[:, cc, 0:n], in0=tmp[tq][:, 0:n], in1=tmp[tq][:, 0:n], op=ALU.mult),
                              reads=[b_tmp[tq]], writes=[BS_])
                    fw.dma("pool", sc["AT"][og * 512:(og + 1) * 512, t0:t0 + n].rearrange("(c p) t -> p c t", p=128), S_[:, :, 0:n], reads=[BS_])

    def phase4b(self, l):
        nc, fw, S, T = self.nc, self.fw, self.S, self.T
        sc = self.s
        last = l == self.L - 1
        chunks = self.chunks if not last else self.chunks[:-1]
        with ExitStack() as es:
            W = self.sb(es, "f2W", [128, 32, D], BF16); b_W = [[Buf() for _ in range(4)] for _ in range(2)]
            for hf2 in range(2):
                for k4 in range(4):
                    fw.dma("pool", W[:, 8 * k4:8 * k4 + 8, hf2 * 512:(hf2 + 1) * 512],
                           self.i["w_ff2"][l, 1024 * k4:1024 * (k4 + 1), hf2 * 512:(hf2 + 1) * 512].rearrange("(k p) m -> p k m", p=128),
                           writes=[b_W[hf2][k4]])
            G, b_G = self._gate_tiles(es, l, 5, "f2")
            A = [self.sb(es, "f2A%d" % k, [128, 32, 512], BF16) for k in range(2)]; b_A = [Buf(), Buf()]
            X = [self.sb(es, "f2X%d" % k, [128, 4, D], F32) for k in range(2)]; b_X = [Buf(), Buf()]
            tq = [self.sb(es, "f2t%d" % k, [128, 512], F32) for k in range(2)]; b_tq = [Buf(), Buf()]
            pq = [self.ps(es, "f2pq%d" % k, [128, 512]) for k in range(4)]; b_pq = [Buf(True) for _ in range(4)]
            if last:
                FG = self.sb(es, "f2FG", [128, D], F32); b_FG = Buf()
                fw.dma("sp", FG[:], self.i["fgrow"].partition_broadcast(128), writes=[b_FG])
                junk = self.sb(es, "f2junk", [128, D], BF16); b_junk = Buf()
                ss = self.sb(es, "f2ss", [128, 4], F32); b_ss = Buf()
                eps_t = self.sb(es, "f2eps", [128, 1], F32); b_eps = Buf()
                fw.op("pool", lambda e: e.memset(eps_t[:], 1e-6), writes=[b_eps])

            def load(ci):
                t0, n, w = chunks[ci]
                a = ci % 2
                for k4 in range(4):
                    fw.dma("sp", A[a][:, 8 * k4:8 * k4 + 8, 0:n], sc["AT"][1024 * k4:1024 * (k4 + 1), t0:t0 + n].rearrange("(c p) t -> p c t", p=128),
                           writes=[b_A[a]])
                fw.dma("sp", X[a][:, 0:n // 128, :], sc["XB"][t0:t0 + n, :].rearrange("(j p) f -> p j f", p=128), writes=[b_X[a]])

            load(0)
            cnt = 0
            for ci, (t0, n, w) in enumerate(chunks):
                a = ci % 2
                nt = n // 128
                if ci + 1 < len(chunks):
                    load(ci + 1)
                for j in range(nt):
                    for hf_ in range(2):
                        q_ = cnt % 4
                        t_ = cnt % 2
                        cnt += 1
                        for k in range(32):
                            fw.op("pe", lambda e, q_=q_, k=k, j=j, hf_=hf_: e.matmul(
                                pq[q_][:, :], lhsT=A[a][:, k, j * 128:(j + 1) * 128], rhs=W[:, k, hf_ * 512:(hf_ + 1) * 512],
                                start=(k == 0), stop=(k == 31)), reads=[b_A[a], b_W[hf_][k // 8]], writes=[b_pq[q_]], sig=(k == 31))
                        fw.op("dve", lambda e, q_=q_, hf_=hf_, t_=t_: e.tensor_tensor(out=tq[t_][:], in0=pq[q_][:], in1=G[:, w, hf_ * 512:(hf_ + 1) * 512], op=ALU.mult),
                              reads=[b_pq[q_], b_G], writes=[b_tq[t_]])
                        fw.op("pool", lambda e, j=j, hf_=hf_, t_=t_: e.tensor_tensor(
                            out=X[a][:, j, hf_ * 512:(hf_ + 1) * 512], in0=X[a][:, j, hf_ * 512:(hf_ + 1) * 512], in1=tq[t_][:], op=ALU.add),
                            reads=[b_X[a], b_tq[t_]], writes=[b_X[a]])
                if not last:
                    fw.dma("pool", sc["XA"][t0:t0 + n, :].rearrange("(j p) f -> p j f", p=128), X[a][:, 0:nt, :], reads=[b_X[a]])
                else:
                    for j in range(nt):
                        fw.op("act", lambda e, j=j: e.activation(out=junk[:], in_=X[a][:, j, :], func=AF.Square, accum_out=ss[:, j:j + 1]),
                              reads=[b_X[a]], writes=[b_junk, b_ss])
                    fw.op("act", lambda e: e.activation(out=ss[:, 0:nt], in_=ss[:, 0:nt], func=AF.Sqrt, bias=eps_t[:], scale=1.0 / D),
                          reads=[b_ss, b_eps], writes=[b_ss])
                    fw.op("dve", lambda e: e.reciprocal(out=ss[:, 0:nt], in_=ss[:, 0:nt]), reads=[b_ss], writes=[b_ss])
                    for j in range(nt):
                        fw.op("dve", lambda e, j=j: e.scalar_tensor_tensor(out=X[a][:, j, :], in0=X[a][:, j, :], scalar=ss[:, j:j + 1], in1=FG[:],
                                                                          op0=ALU.mult, op1=ALU.mult), reads=[b_X[a], b_ss, b_FG], writes=[b_X[a]])
                    fw.dma("pool", self.out[t0:t0 + n, :].rearrange("(j p) f -> p j f", p=128), X[a][:, 0:nt, :], reads=[b_X[a]])


def _fm(v):
    v = np.asarray(v, np.float32)
    return np.ascontiguousarray(v.reshape(-1, 128).T)


def _na_tables(rpb, R):
    qs = [2, 0, 1, R // 2 - 2, R // 2 - 1]
    out = np.empty((5, 4, 128, 2, 5, 128), np.float32)
    part = np.arange(128)
    i2, jc = part // 64, part % 64
    qc = np.arange(128)
    r2, w = qc // 64, qc % 64
    for vi, q in enumerate(qs):
        kb = int(np.clip(2 * q - 4, 0, R - 10))
        r = 2 * q + r2
        rs = np.clip(r - 4, 0, R - 8)
        cs = np.clip(w - 8, 0, GW - 16)
        for j in range(5):
            kr = kb + 2 * j + i2
            valid = ((kr[:, None] >= rs[None]) & (kr[:, None] < rs[None] + 8)
                     & (jc[:, None] >= cs[None]) & (jc[:, None] < cs[None] + 16))
            dr = np.clip(kr[:, None] - r[None] + 7, 0, 14)
            dc = np.clip(jc[:, None] - w[None] + 15, 0, 30)
            for h in range(8):
                tab = np.where(valid, rpb[h][dr, dc], np.float32(MASKV))
                out[vi, h // 2, :, h % 2, j, :] = tab
    return out.reshape(5, 4, 128, 1280)


def host_prep(inp, S, L):
    B = inp["x"].shape[0]
    R = S // GW
    SL = S // 2
    f32 = np.float32
    t = np.arange(S)
    row = (t // GW).astype(f32)
    col = (t % GW).astype(f32)
    inv = (np.float32(10000.0) ** (-np.arange(16, dtype=f32) / np.float32(16))).astype(f32)
    ang = np.concatenate([row[:, None] * inv, col[:, None] * inv], axis=-1).astype(f32)
    cosv, sinv = np.cos(ang.astype(np.float64)), np.sin(ang.astype(np.float64))
    p = np.arange(128)
    ropec = np.ascontiguousarray(cosv[:, p % 32].T).astype(f32)
    sgn = np.where((p % 64) < 32, -1.0, 1.0)
    ropes = np.ascontiguousarray((sinv[:, p % 32] * sgn[None]).T).astype(f32)
    cm = np.zeros((128, NCM), f32)
    cm[:, CM_ID:CM_ID + 128] = np.eye(128, dtype=f32)
    partner = np.where((p % 64) < 32, p + 32, p - 32)
    cm[partner, CM_PERM + p] = 1.0
    kk, qq = np.meshgrid(np.arange(128), np.arange(128), indexing="ij")
    mprev = np.where(kk >= qq, 0.0, MASKV).astype(f32)
    mnext = np.where(kk <= qq, 0.0, MASKV).astype(f32)
    cm[:, CM_MPREV:CM_MPREV + 256] = np.tile(mprev, (1, 2))
    cm[:, CM_MNEXT:CM_MNEXT + 256] = np.tile(mnext, (1, 2))
    for k in range(4):
        cm[k, CM_ESEL:CM_ESEL + 64] = 1.0
        base = CM_SEL0 if k < 2 else CM_SEL1
        cm[k, base + (k % 2) * 128:base + (k % 2) * 128 + 128] = 1.0
    nab_all = [_na_tables(np.asarray(inp["na_rpb"][l], f32), R) for l in range(L)]
    nonce = int(np.random.default_rng().integers(1, 2 ** 30))
    ep = (nonce + np.arange(16)).astype(np.int32)[None]
    Lh = (L + 1) // 2
    shared = dict(w_in=np.ascontiguousarray(inp["w_in"][:L], f32),
                  w_branch=np.ascontiguousarray(inp["w_branch"][:L], f32), w_out=np.ascontiguousarray(inp["w_out"][:L], f32),
                  w_ff1=np.ascontiguousarray(inp["w_ff1"][:L], f32), w_ff2=np.ascontiguousarray(inp["w_ff2"][:L], f32),
                  fgv=_fm(inp["final_g"]), fgrow=np.ascontiguousarray(inp["final_g"], f32), cmat=cm, ep=ep)
    par_maps = []
    for par in range(2):
        pvec = np.zeros((128, L, NPV), f32)
        lruw = np.zeros((L, 2, 2, 4, 128, 128), f32)
        nab = np.zeros((L, 5, 2, 128, 1280), f32)
        for l in range(L):
            pvec[:, l, PV_BMOD:PV_BMOD + 48] = _fm(inp["b_mod"][l])
            pvec[:, l, PV_N1G:PV_N1G + 8] = _fm(inp["norm1_g"][l])
            pvec[:, l, PV_N2G:PV_N2G + 8] = _fm(inp["norm2_g"][l])
            chs = slice(2 * par, 2 * par + 2)
            for tap in range(4):
                pvec[:, l, PV_CW + tap:PV_CW + 8:4] = _fm(inp["conv_w"][l, tap])[:, chs]
            pvec[:, l, PV_CB:PV_CB + 2] = _fm(inp["conv_b"][l])[:, chs]
            for d in range(2):
                pvec[:, l, PV_BA + 4 * d:PV_BA + 4 * d + 2] = _fm(inp["lru_ba"][l, d])[:, chs]
                pvec[:, l, PV_BX + 4 * d:PV_BX + 4 * d + 2] = _fm(inp["lru_bx"][l, d])[:, chs]
                pvec[:, l, PV_LAM + 4 * d:PV_LAM + 4 * d + 4] = 8.0
                pvec[:, l, PV_LAM + 4 * d:PV_LAM + 4 * d + 2] = _fm(inp["lru_lambda"][l, d])[:, chs]
                for ax, nm in enumerate(("lru_wa", "lru_wx")):
                    for cl in range(2):
                        for hb in range(2):
                            lruw[l, d, ax, cl, hb * 64:(hb + 1) * 64, hb * 64:(hb + 1) * 64] = inp[nm][l, d, 2 * (2 * par + cl) + hb]
            pvec[0:4, l, PV_SINK] = inp["swa_sink"][l][4 * par:4 * par + 4]
            nab[l] = nab_all[l][:, 2 * par:2 * par + 2]
        mine = [2 * li + par if 2 * li + par < L else 0 for li in range(Lh)]
        pvec0 = np.zeros((128, Lh, 64), f32)
        for li, l_ in enumerate(mine):
            pvec0[:, li, 0:48] = _fm(inp["b_mod"][l_])
            pvec0[:, li, 48:56] = _fm(inp["norm1_g"][l_])
            pvec0[:, li, 56:64] = _fm(inp["norm2_g"][l_])
        par_maps.append(dict(pvec=pvec.reshape(128, L * NPV), lruw=lruw, nabias=nab,
                             w_mod=np.ascontiguousarray(np.asarray(inp["w_mod"], f32)[mine]), pvec0=pvec0.reshape(128, Lh * 64),
                             ropec=np.ascontiguousarray(ropec[:, par * SL:(par + 1) * SL]),
                             ropes=np.ascontiguousarray(ropes[:, par * SL:(par + 1) * SL])))
    maps = []
    for b in range(B):
        for par in range(2):
            m = dict(shared)
            m.update(par_maps[par])
            m["xin"] = np.ascontiguousarray(np.concatenate([inp["x"][b][par * SL:(par + 1) * SL],
                                                           inp["ctx"][b][par * 128:(par + 1) * 128]], axis=0), f32)
            cv = np.empty((128, 16), f32)
            cv[:, 0::2] = _fm(inp["c"][b])
            cv[:, 1::2] = _fm(inp["c_ctx"])
            m["cvec"] = cv
            maps.append(m)
    return maps


_CACHE = {}


def run(inp, S, L, dbg=()):
    key = (S, L, tuple(dbg))
    maps = host_prep(inp, S, L)
    prog = Prog(S, L, dbg)
    nc = prog.build()
    res = run_bass_kernel_spmd(nc, maps, core_ids=list(range(len(maps))))
    return res.results, prog


def kernel(**inputs):
    inp = {k: np.asarray(v) for k, v in inputs.items()}
    S = inp["x"].shape[1]
    L = inp["w_in"].shape[0]
    results, _ = run(inp, S, L)
    B = inp["x"].shape[0]
    return np.stack([np.concatenate([results[2 * b]["out"], results[2 * b + 1]["out"]], axis=0) for b in range(B)], axis=0).astype(np.float32)
```
